# Optimizing a Trainium2 kernel written in Bass

```python
import math
import jax, jax.numpy as jnp
from jax import lax
import numpy as np

D_MODEL = 1024
BATCH = 2
SEQ = 8192
DEPTH = 2

N_META = 16
N_MIXERS = 2
N_LRU_LAYERS = (DEPTH + 1) // 2
N_ATTN_LAYERS = DEPTH // 2

LRU_WIDTH = D_MODEL
LRU_HEADS = 8
LRU_BLOCK = LRU_WIDTH // LRU_HEADS
LRU_CONV = 4
LRU_C = 8.0

ATTN_HEAD_DIM = 64
ATTN_HEADS = D_MODEL // (2 * ATTN_HEAD_DIM)
ROPE_THETA = 10000.0
Q_BLOCK = 128

FFN_DIM = 3 * D_MODEL
FFN_CONV = 3

NORM_EPS = 1e-6

kernel_name = "hybrid_rglru_diffattn_convffn"


def rmsnorm(x, g):
    xf = x.astype(jnp.float32)
    y = xf * lax.rsqrt(jnp.mean(xf * xf, axis=-1, keepdims=True) + NORM_EPS)
    return (y * g.astype(jnp.float32)).astype(x.dtype)


def causal_depthwise_conv(x, w, b):
    K, C = w.shape
    out = lax.conv_general_dilated(
        x, w[:, None, :].astype(x.dtype), window_strides=(1,), padding=[(K - 1, 0)],
        dimension_numbers=("NWC", "WIO", "NWC"), feature_group_count=C)
    return out + b.astype(x.dtype)


def rg_lru(x, w_a, b_a, w_i, b_i, lam):
    B, T, W = x.shape
    xb = x.reshape(B, T, LRU_HEADS, LRU_BLOCK)
    r = jax.nn.sigmoid(jnp.einsum('bthi,hij->bthj', xb, w_a).reshape(B, T, W) + b_a)
    gi = jax.nn.sigmoid(jnp.einsum('bthi,hij->bthj', xb, w_i).reshape(B, T, W) + b_i)
    log_a = LRU_C * r.astype(jnp.float32) * jax.nn.log_sigmoid(lam.astype(jnp.float32))
    a = jnp.exp(log_a)
    mult = jnp.sqrt(-jnp.expm1(2.0 * log_a))
    u = mult * (gi.astype(jnp.float32) * x.astype(jnp.float32))

    def combine(left, right):
        a_l, h_l = left
        a_r, h_r = right
        return a_l * a_r, a_r * h_l + h_r

    _, h = lax.associative_scan(combine, (a, u), axis=1)
    return h.astype(x.dtype)


def recurrent_block(x, w_in, b_in, conv_w, conv_b, w_a, b_a, w_i, b_i, lam, w_out, b_out):
    u = x @ w_in + b_in
    gate, rec = jnp.split(u, 2, axis=-1)
    rec = causal_depthwise_conv(rec, conv_w, conv_b)
    h = rg_lru(rec, w_a, b_a, w_i, b_i, lam)
    y = jax.nn.gelu(gate, approximate=True) * h
    return y @ w_out + b_out


def rope_tables(T, dtype):
    inv = 1.0 / (ROPE_THETA ** (jnp.arange(0, ATTN_HEAD_DIM, 2, dtype=jnp.float32) / ATTN_HEAD_DIM))
    ang = jnp.arange(T, dtype=jnp.float32)[:, None] * inv[None, :]
    ang = jnp.concatenate([ang, ang], axis=-1)
    return jnp.cos(ang).astype(dtype), jnp.sin(ang).astype(dtype)


def apply_rope(x, cos, sin):
    half = ATTN_HEAD_DIM // 2
    x1, x2 = x[..., :half], x[..., half:]
    return x * cos + jnp.concatenate([-x2, x1], axis=-1) * sin


def diff_attention(x, w_qkv, lq1, lk1, lq2, lk2, subln_g, w_o, lambda_init):
    B, T, _ = x.shape
    H, d = ATTN_HEADS, ATTN_HEAD_DIM
    q, k, v = jnp.split(x @ w_qkv, 3, axis=-1)
    q = q.reshape(B, T, H, 2, d).transpose(0, 2, 3, 1, 4)
    k = k.reshape(B, T, H, 2, d).transpose(0, 2, 3, 1, 4)
    v = v.reshape(B, T, H, 2 * d).transpose(0, 2, 1, 3)
    cos, sin = rope_tables(T, x.dtype)
    q = apply_rope(q, cos, sin) * (d ** -0.5)
    k = apply_rope(k, cos, sin)
    lam = (jnp.exp(jnp.sum(lq1.astype(jnp.float32) * lk1.astype(jnp.float32)))
           - jnp.exp(jnp.sum(lq2.astype(jnp.float32) * lk2.astype(jnp.float32)))
           + lambda_init)
    k_pos = jnp.arange(T, dtype=jnp.int32)

    def attend(q_blk, q_pos):
        s = jnp.einsum('bhmqd,bhmkd->bhmqk', q_blk, k).astype(jnp.float32)
        mask = k_pos[None, :] <= q_pos[:, None]
        s = jnp.where(mask, s, -jnp.inf)
        p = jax.nn.softmax(s, axis=-1)
        wgt = p[:, :, 0] - lam * p[:, :, 1]
        return jnp.einsum('bhqk,bhkv->bhqv', wgt.astype(v.dtype), v)

    o_meta = attend(q[:, :, :, :N_META], k_pos[:N_META])
    n_blk = (T - N_META) // Q_BLOCK
    q_real = q[:, :, :, N_META:].reshape(B, H, 2, n_blk, Q_BLOCK, d)
    q_real = jnp.moveaxis(q_real, 3, 0)
    pos_real = k_pos[N_META:].reshape(n_blk, Q_BLOCK)
    o_real = lax.map(lambda args: attend(args[0], args[1]), (q_real, pos_real))
    o_real = jnp.moveaxis(o_real, 0, 2).reshape(B, H, n_blk * Q_BLOCK, 2 * d)
    o = jnp.concatenate([o_meta, o_real], axis=2)
    o = rmsnorm(o, subln_g) * (1.0 - lambda_init)
    o = o.transpose(0, 2, 1, 3).reshape(B, T, H * 2 * d)
    return o @ w_o


def conv_ffn(x, w_up, conv_w, conv_b, w_down):
    u = causal_depthwise_conv(x @ w_up, conv_w, conv_b)
    g, val = jnp.split(u, 2, axis=-1)
    return (jax.nn.gelu(g, approximate=True) * val) @ w_down


def setup_inputs(seed: int = 0) -> dict:
    key = jax.random.key(seed)
    ks = jax.random.split(key, 32)
    f32 = jnp.float32
    D, W, F, H, d = D_MODEL, LRU_WIDTH, FFN_DIM, ATTN_HEADS, ATTN_HEAD_DIM
    nl, na = N_LRU_LAYERS, N_ATTN_LAYERS

    def nrm(k, shape, scale):
        return jax.random.normal(k, shape, f32) * scale

    u = jax.random.uniform(ks[10], (nl, W), f32, 0.9, 0.999)
    base = u ** (1.0 / LRU_C)
    lru_L = jnp.log(base) - jnp.log1p(-base)
    return {
        "x": nrm(ks[0], (BATCH, SEQ, D), 1.0),
        "meta_tokens": nrm(ks[1], (N_META, D), 1.0),
        "mix_norm_g": 1.0 + nrm(ks[2], (DEPTH, D), 0.01),
        "lru_w_in": nrm(ks[3], (nl, D, 2 * W), D ** -0.5),
        "lru_b_in": nrm(ks[4], (nl, 2 * W), 0.01),
        "lru_conv_w": nrm(ks[5], (nl, LRU_CONV, W), LRU_CONV ** -0.5),
        "lru_conv_b": nrm(ks[6], (nl, W), 0.01),
        "lru_w_a": nrm(ks[7], (nl, LRU_HEADS, LRU_BLOCK, LRU_BLOCK), LRU_BLOCK ** -0.5),
        "lru_b_a": nrm(ks[8], (nl, W), 0.01),
        "lru_w_i": nrm(ks[9], (nl, LRU_HEADS, LRU_BLOCK, LRU_BLOCK), LRU_BLOCK ** -0.5),
        "lru_b_i": nrm(ks[11], (nl, W), 0.01),
        "lru_L": lru_L,
        "lru_w_out": nrm(ks[12], (nl, W, D), W ** -0.5),
        "lru_b_out": nrm(ks[13], (nl, D), 0.01),
        "attn_w_qkv": nrm(ks[14], (na, D, 3 * H * 2 * d), D ** -0.5),
        "attn_lambda_q1": nrm(ks[15], (na, d), 0.1),
        "attn_lambda_k1": nrm(ks[16], (na, d), 0.1),
        "attn_lambda_q2": nrm(ks[17], (na, d), 0.1),
        "attn_lambda_k2": nrm(ks[18], (na, d), 0.1),
        "attn_subln_g": 1.0 + nrm(ks[19], (na, 2 * d), 0.01),
        "attn_w_o": nrm(ks[20], (na, H * 2 * d, D), (H * 2 * d) ** -0.5),
        "ffn_norm_g": 1.0 + nrm(ks[21], (DEPTH, D), 0.01),
        "ffn_w_up": nrm(ks[22], (DEPTH, D, 2 * F), D ** -0.5),
        "ffn_conv_w": nrm(ks[23], (DEPTH, FFN_CONV, 2 * F), FFN_CONV ** -0.5),
        "ffn_conv_b": nrm(ks[24], (DEPTH, 2 * F), 0.01),
        "ffn_w_down": nrm(ks[25], (DEPTH, F, D), F ** -0.5),
        "final_norm_g": 1.0 + nrm(ks[26], (D,), 0.01),
    }


def reference(x, meta_tokens, mix_norm_g, lru_w_in, lru_b_in, lru_conv_w, lru_conv_b, lru_w_a, lru_b_a,
              lru_w_i, lru_b_i, lru_L, lru_w_out, lru_b_out, attn_w_qkv, attn_lambda_q1, attn_lambda_k1,
              attn_lambda_q2, attn_lambda_k2, attn_subln_g, attn_w_o, ffn_norm_g, ffn_w_up, ffn_conv_w,
              ffn_conv_b, ffn_w_down, final_norm_g):
    B = x.shape[0]
    meta = jnp.broadcast_to(meta_tokens[None].astype(x.dtype), (B, N_META, D_MODEL))
    h = jnp.concatenate([meta, x], axis=1)
    for i in range(DEPTH):
        j = i // N_MIXERS
        hn = rmsnorm(h, mix_norm_g[i])
        if i % N_MIXERS == 0:
            h = h + recurrent_block(hn, lru_w_in[j], lru_b_in[j], lru_conv_w[j], lru_conv_b[j],
                                    lru_w_a[j], lru_b_a[j], lru_w_i[j], lru_b_i[j], lru_L[j],
                                    lru_w_out[j], lru_b_out[j])
        else:
            lambda_init = 0.8 - 0.6 * math.exp(-0.3 * i)
            h = h + diff_attention(hn, attn_w_qkv[j], attn_lambda_q1[j], attn_lambda_k1[j],
                                   attn_lambda_q2[j], attn_lambda_k2[j], attn_subln_g[j], attn_w_o[j],
                                   lambda_init)
        h = h + conv_ffn(rmsnorm(h, ffn_norm_g[i]), ffn_w_up[i], ffn_conv_w[i], ffn_conv_b[i], ffn_w_down[i])
    h = rmsnorm(h, final_norm_g)
    return h[:, N_META:, :]
```

```python
import math
import numpy as np
import ml_dtypes
import concourse.bass as bass
import concourse.mybir as mybir
from concourse.bass_utils import run_bass_kernel_spmd
from contextlib import ExitStack

F32 = mybir.dt.float32
BF16 = mybir.dt.bfloat16
AF = mybir.ActivationFunctionType
ALU = mybir.AluOpType

D = 1024
WIN = 2064
HALO = 16
OWN = 2048
TSEQ = 8208
TP = 8704
NCORE = 8
LAMBDA_INIT = 0.8 - 0.6 * math.exp(-0.3 * 1)
TILES = [(0, 512), (512, 512), (1024, 512), (1536, 512), (2048, 16)]
SEM_ROT = 24000
WSLOT = 6144
CC_QOS = "P2"


class Buf:
    __slots__ = ("name", "w", "r")

    def __init__(self, name=""):
        self.name = name
        self.w = None
        self.r = {}


class Eng:
    def __init__(self, S, name, eng, same_sync=True):
        self.S = S
        self.name = name
        self.eng = eng
        self.same_sync = same_sync
        self.known = {}
        self.sem = None
        self.cnt = 0
        self._newsem()

    def _newsem(self):
        self.sem = self.S.nc.alloc_semaphore(f"s{self.name}{self.S.nsem}")
        self.S.nsem += 1
        self.cnt = 0

    def rotate(self):
        if self.cnt > 0:
            self.eng.wait_ge(self.sem, self.cnt)
            self.known[id(self.sem)] = self.cnt
        self.S.old.append((self.sem, self.cnt))
        self._newsem()

    def _collect(self, reads, writes):
        need = {}

        def add(st):
            if st is None:
                return
            sem, val, en = st
            if en == self.name and sem is self.sem:
                if not self.same_sync or val < self.cnt:
                    return
            if self.known.get(id(sem), 0) >= val:
                return
            cur = need.get(id(sem))
            if cur is None or cur[1] < val:
                need[id(sem)] = st

        for b in reads:
            add(b.w)
        for b in writes:
            add(b.w)
            for st in b.r.values():
                add(st)
        return need

    def _emit_waits(self, need):
        for sem, val, en in need.values():
            self.eng.wait_ge(sem, val)
            self.known[id(sem)] = val

    def op(self, fn, reads=(), writes=()):
        if self.cnt >= SEM_ROT:
            self.rotate()
        need = self._collect(reads, writes)
        self._emit_waits(need)
        inst = fn(self.eng)
        self.cnt += 1
        inst.then_inc(self.sem, 1)
        st = (self.sem, self.cnt, self.name)
        for b in reads:
            b.r[id(self.sem)] = st
        for b in writes:
            b.w = st
            b.r = {}
        return inst

    def wait_stamp(self, st):
        need = {}
        sem, val, en = st
        if self.known.get(id(sem), 0) < val:
            self.eng.wait_ge(sem, val)
            self.known[id(sem)] = val


class DmaQ:
    def __init__(self, S, name, eng, nsems=8):
        self.S = S
        self.name = name
        self.eng = eng
        self.known = {}
        self.sems = [S.nc.alloc_semaphore(f"d{name}{i}") for i in range(nsems)]
        self.vals = [0] * nsems
        self.i = 0
        self.same_sync = False
        self.sem = None
        self.cnt = 0

    def dma(self, out, in_, reads=(), writes=(), **kw):
        k = self.i
        self.i = (self.i + 1) % len(self.sems)
        sem = self.sems[k]
        need = Eng._collect(self, reads, writes)
        if self.vals[k] > 0 and self.known.get(id(sem), 0) < self.vals[k]:
            need[id(sem)] = (sem, self.vals[k], self.name)
        Eng._emit_waits(self, need)
        inst = self.eng.dma_start(out=out, in_=in_, **kw)
        self.vals[k] += 16
        inst.then_inc(sem, 16)
        st = (sem, self.vals[k], self.name)
        for b in reads:
            b.r[id(sem)] = st
        for b in writes:
            b.w = st
            b.r = {}
        return st


class Sched:
    def __init__(self, nc):
        self.nc = nc
        self.nsem = 0
        self.old = []
        self.pe = Eng(self, "pe", nc.tensor, same_sync=False)
        self.dve = Eng(self, "dve", nc.vector)
        self.act = Eng(self, "act", nc.scalar)
        self.pool = Eng(self, "pool", nc.gpsimd)
        self.sp = DmaQ(self, "sp", nc.sync, 12)
        self.gq = DmaQ(self, "gq", nc.gpsimd, 6)
        self.engs = [self.pe, self.dve, self.act, self.pool]

    def barrier(self):
        stamps = [(e.sem, e.cnt, e.name) for e in self.engs if e.cnt > 0]
        for q in (self.sp, self.gq):
            for k, sem in enumerate(q.sems):
                if q.vals[k] > 0:
                    stamps.append((sem, q.vals[k], q.name))
        if getattr(self, "ccnt", 0) > 0:
            stamps.append((self.csem, self.ccnt, "cc"))
        for e in self.engs + [self.sp, self.gq]:
            for sem, val, en in stamps:
                if e.known.get(id(sem), 0) < val:
                    e.eng.wait_ge(sem, val)
                    e.known[id(sem)] = val

    def finish(self):
        self.barrier()

    def all_gather_async(self, src, dst, groups, src_bufs, dst_buf):
        if not hasattr(self, "csem"):
            self.csem = self.nc.alloc_semaphore("ccsem")
            self.ccnt = 0
        need = Eng._collect(self.pool, src_bufs, [dst_buf])
        Eng._emit_waits(self.pool, need)
        self.nc.gpsimd.collective_compute("AllGather", ALU.bypass, replica_groups=groups, ins=[src.opt()], outs=[dst.opt()], dma_qos=CC_QOS).then_inc(self.csem)
        self.ccnt += 1
        st = (self.csem, self.ccnt, "cc")
        for b in src_bufs:
            b.r[id(self.csem)] = st
        dst_buf.w = st
        dst_buf.r = {}

    def all_gather(self, pairs, groups):
        self.barrier()
        if not hasattr(self, "csem"):
            self.csem = self.nc.alloc_semaphore("ccsem")
            self.ccnt = 0
        for src, dst in pairs:
            self.nc.gpsimd.collective_compute("AllGather", ALU.bypass, replica_groups=groups, ins=[src.opt()], outs=[dst.opt()]).then_inc(self.csem)
            self.ccnt += 1
        for e in self.engs + [self.sp, self.gq]:
            e.eng.wait_ge(self.csem, self.ccnt)


class Scope:
    _n = 0

    def __init__(self, nc):
        self.nc = nc
        self.stack = ExitStack()

    def __enter__(self):
        self.stack.__enter__()
        return self

    def __exit__(self, *a):
        return self.stack.__exit__(*a)

    def sb(self, name, shape, dtype):
        Scope._n += 1
        return self.stack.enter_context(self.nc.sbuf_tensor(f"{name}_{Scope._n}", list(shape), dtype))

    def psum(self, name, shape, dtype):
        Scope._n += 1
        return self.stack.enter_context(self.nc.psum_tensor(f"{name}_{Scope._n}", list(shape), dtype))


class Ring:
    def __init__(self, sc, name, shape, dtype, n, psum=False):
        alloc = sc.psum if psum else sc.sb
        self.t = [alloc(f"{name}{i}", list(shape), dtype) for i in range(n)]
        self.b = [Buf(f"{name}{i}") for i in range(n)]
        self.i = 0

    def next(self):
        k = self.i
        self.i = (self.i + 1) % len(self.t)
        return self.t[k], self.b[k]


_VEC_LAYOUT = [
    ("mix_g0", 8), ("mix_g1", 8), ("ffn_g0", 8), ("ffn_g1", 8), ("fin_g", 8),
    ("b_in", 16), ("cw", 32), ("cb", 8), ("b_a", 8), ("b_i", 8), ("L", 8), ("b_out", 8),
    ("fcw0", 144), ("fcb0", 48), ("fcw1", 144), ("fcb1", 48),
    ("lam", 256), ("subg", 128), ("eps", 1), ("cmask", 1), ("onehot", 4), ("one", 1), ("subgp", 1),
]
VO = {}
_o = 0
for _n, _c in _VEC_LAYOUT:
    VO[_n] = _o
    _o += _c
NV = _o


def _pk(v):
    return np.ascontiguousarray(v.reshape(-1, 128).T)


def pack_vecs(inp, c):
    V = np.zeros((128, NV), np.float32)

    def put(name, arr):
        V[:, VO[name]:VO[name] + arr.shape[1]] = arr

    put("mix_g0", _pk(inp["mix_norm_g"][0]))
    put("mix_g1", _pk(inp["mix_norm_g"][1]))
    put("ffn_g0", _pk(inp["ffn_norm_g"][0]))
    put("ffn_g1", _pk(inp["ffn_norm_g"][1]))
    put("fin_g", _pk(inp["final_norm_g"]))
    put("b_in", _pk(inp["lru_b_in"][0]))
    put("cw", np.concatenate([_pk(inp["lru_conv_w"][0][k]) for k in range(4)], 1))
    put("cb", _pk(inp["lru_conv_b"][0]))
    put("b_a", _pk(inp["lru_b_a"][0]))
    put("b_i", _pk(inp["lru_b_i"][0]))
    put("L", _pk(inp["lru_L"][0]))
    put("b_out", _pk(inp["lru_b_out"][0]))
    for l in range(2):
        put(f"fcw{l}", np.concatenate([_pk(inp["ffn_conv_w"][l][k]) for k in range(3)], 1))
        put(f"fcb{l}", _pk(inp["ffn_conv_b"][l]))
    lam = np.concatenate([inp["attn_lambda_q1"][0], inp["attn_lambda_k1"][0], inp["attn_lambda_q2"][0], inp["attn_lambda_k2"][0]])
    put("lam", np.broadcast_to(lam[None, :], (128, 256)))
    put("subg", np.broadcast_to(inp["attn_subln_g"][0][None, :], (128, 128)))
    V[:, VO["eps"]] = 1e-6
    V[:, VO["cmask"]] = 0.0 if c == 0 else 1.0
    V[:, VO["onehot"] + c] = 1.0
    V[:, VO["one"]] = 1.0
    V[:, VO["subgp"]] = inp["attn_subln_g"][0]
    return V


class Ctx:
    def __init__(self, nc, sc, vecs_dram, ident_dram, n_f32=8, n_bf=8):
        self.nc = nc
        self.S = Sched(nc)
        S = self.S
        self.ps = None
        self.wr = None
        self.fr = Ring(sc, "fr", [128, 516], F32, n_f32)
        self.br = Ring(sc, "br", [128, 512], BF16, n_bf)
        self.vecs = sc.sb("vecs_sb", [128, NV], F32)
        self.vB = Buf("vecs")
        S.sp.dma(self.vecs[:], vecs_dram, writes=[self.vB])
        self.ident = sc.sb("ident_sb", [128, 128], BF16)
        self.idB = Buf("ident")
        S.sp.dma(self.ident[:], ident_dram, writes=[self.idB])
        self.ones = sc.sb("ones", [128, 128], BF16)
        self.onesB = Buf("ones")
        S.pool.op(lambda e: e.memset(self.ones[:], 1.0), writes=[self.onesB])
        self.cid = nc.sync.partition_id() % 4

    def phase(self, sc, n_w=4, n_ps=8):
        self.ps = Ring(sc, "ps", [128, 512], F32, n_ps, psum=True) if n_ps else None
        self.wr = Ring(sc, "wr", [128, WSLOT], BF16, n_w) if n_w else None

    def v(self, name, i=0):
        o = VO[name] + i
        return self.vecs[:, o:o + 1]


def _nb(lst):
    b = Buf()
    lst.append(b)
    return b


def fm(dram):
    return dram.rearrange("(kc p) t -> p kc t", p=128)


def rmsnorm_tile(C, src, srcB, gname, n, dst, dstB, nfeat=1024.0):
    S = C.S
    ps, psB = C.ps.next()
    for kc in range(8):
        sq, sqB = C.br.next()
        S.act.op(lambda e: e.activation(out=sq[:, 0:n], in_=src(kc), func=AF.Square), reads=[srcB[kc]], writes=[sqB])
        S.pe.op(lambda e: e.matmul(ps[:, 0:n], lhsT=C.ones[:], rhs=sq[:, 0:n], start=(kc == 0), stop=(kc == 7)),
                reads=[sqB, C.onesB], writes=[psB])
    rs, rsB = C.fr.next()
    S.act.op(lambda e: e.activation(out=rs[:, 0:n], in_=ps[:, 0:n], func=AF.Sqrt, bias=C.v("eps"), scale=1.0 / nfeat),
             reads=[psB, C.vB], writes=[rsB])
    S.dve.op(lambda e: e.reciprocal(out=rs[:, 0:n], in_=rs[:, 0:n]), reads=[rsB], writes=[rsB])
    for kc in range(8):
        S.dve.op(lambda e: e.scalar_tensor_tensor(out=dst(kc), in0=src(kc), scalar=C.v(gname, kc), in1=rs[:, 0:n],
                                                  op0=ALU.mult, op1=ALU.mult),
                 reads=[srcB[kc], rsB, C.vB], writes=[dstB[kc]])


def load_w(C, dram_rows_ap, ncols, nk=8):
    assert nk * ncols <= WSLOT
    slot, sB = C.wr.next()
    view = slot[:, 0:nk * ncols].rearrange("p (kc n) -> p kc n", kc=nk)
    C.S.gq.dma(view, dram_rows_ap.rearrange("(kc p) n -> p kc n", p=128), writes=[sB])
    return slot, sB


class W1024:
    def __init__(self, C, dram):
        self.h = [load_w(C, dram[:, i * 512:(i + 1) * 512], 512) for i in range(2)]

    def lhsT(self, kc, oc):
        slot, sB = self.h[oc // 4]
        c0 = kc * 512 + (oc % 4) * 128
        return slot[:, c0:c0 + 128], sB


GROUPS = [[0, 1, 2, 3], [4, 5, 6, 7]]
TPA = 8320
TO = 10240
NKT = 65
QBLKS = [(512 * i, 512) for i in range(16)] + [(8192, 128)]


def lru_front(C, sc, xw, w_in, w_a, w_i, gate_o, hloc_o, pcum_o, ends_src):
    nc, S = C.nc, C.S
    wg = W1024(C, w_in[:, 0:D])
    wrc = W1024(C, w_in[:, D:2 * D])
    wai, waiB = C.wr.next()
    S.gq.dma(wai[:, 0:1024].rearrange("p (h j) -> p h j", h=8), w_a.rearrange("(h p) j -> p h j", p=128), writes=[waiB])
    S.gq.dma(wai[:, 1024:2048].rearrange("p (h j) -> p h j", h=8), w_i.rearrange("(h p) j -> p h j", p=128), writes=[waiB])
    c8 = sc.sb("c8", [128, 8], F32)
    c16 = sc.sb("c16", [128, 8], F32)
    c8B = Buf("c8")
    Lap = C.vecs[:, VO["L"]:VO["L"] + 8]
    S.act.op(lambda e: e.activation(out=c8[:], in_=Lap, func=AF.Sigmoid), reads=[C.vB], writes=[c8B])
    S.act.op(lambda e: e.activation(out=c8[:], in_=c8[:], func=AF.Ln), reads=[c8B], writes=[c8B])
    S.dve.op(lambda e: e.tensor_scalar(out=c16[:], in0=c8[:], scalar1=16.0, scalar2=None, op0=ALU.mult), reads=[c8B], writes=[c8B])
    S.dve.op(lambda e: e.tensor_scalar(out=c8[:], in0=c8[:], scalar1=8.0, scalar2=None, op0=ALU.mult), reads=[c8B], writes=[c8B])
    h8 = sc.sb("h8", [128, 8], F32)
    hb = sc.sb("hb", [128, 16], F32)
    qtr = sc.sb("qtr", [128, 1], F32)
    S.dve.op(lambda e: e.tensor_scalar(out=h8[:], in0=c8[:], scalar1=0.5, scalar2=None, op0=ALU.mult), reads=[c8B], writes=[c8B])
    S.dve.op(lambda e: e.tensor_scalar(out=hb[:, 0:8], in0=C.vecs[:, VO["b_a"]:VO["b_a"] + 8], scalar1=0.5, scalar2=None, op0=ALU.mult), reads=[C.vB, c8B], writes=[c8B])
    S.dve.op(lambda e: e.tensor_scalar(out=hb[:, 8:16], in0=C.vecs[:, VO["b_i"]:VO["b_i"] + 8], scalar1=0.5, scalar2=None, op0=ALU.mult), reads=[C.vB, c8B], writes=[c8B])
    S.dve.op(lambda e: e.memset(qtr[:], 0.25), writes=[c8B])

    xt = sc.sb("xt", [128, 8, 512], F32)
    xtB = [Buf() for _ in range(8)]
    hn = [sc.sb(f"hn{i}", [128, 8, 512], BF16) for i in range(2)]
    hnB = [[Buf() for _ in range(8)] for _ in range(2)]
    rec = [sc.sb(f"rec{i}", [128, 8, 516], F32) for i in range(2)]
    recB = [[Buf() for _ in range(8)] for _ in range(2)]
    gt = sc.sb("gt", [128, 8, 512], BF16)
    gtB = [Buf() for _ in range(8)]
    hl = sc.sb("hl", [128, 8, 512], F32)
    hlB = [Buf() for _ in range(8)]
    pc = sc.sb("pc", [128, 8, 512], F32)
    pcB = [Buf() for _ in range(8)]
    zeros = sc.sb("zeros", [128, 512], F32)
    zB = Buf("zeros")
    S.pool.op(lambda e: e.memset(zeros[:], 0.0), writes=[zB])
    st = sc.sb("st", [128, 16], F32)
    stB = [Buf() for _ in range(8)]
    xwf = fm(xw)
    cvr = Ring(sc, "cvr", [128, 516], F32, 4)

    def norm_in(par, c0, n):
        for kc in range(8):
            S.sp.dma(xt[:, kc, 0:n], xwf[:, kc, c0:c0 + n], writes=[xtB[kc]])
        rmsnorm_tile(C, lambda kc: xt[:, kc, 0:n], xtB, "mix_g0", n,
                     lambda kc: hn[par][:, kc, 0:n], hnB[par])

    def mm8(W, par, oc, n):
        ps, psB = C.ps.next()
        for kc in range(8):
            l_, lB = W.lhsT(kc, oc)
            S.pe.op(lambda e: e.matmul(ps[:, 0:n], lhsT=l_, rhs=hn[par][:, kc, 0:n], start=(kc == 0), stop=(kc == 7)),
                    reads=[lB, hnB[par][kc]], writes=[psB])
        return ps, psB

    norm_in(1, 0, 3)
    for oc in range(8):
        ps, psB = mm8(wrc, 1, oc, 3)
        S.dve.op(lambda e: e.tensor_scalar(out=rec[0][:, oc, 0:3], in0=ps[:, 0:3], scalar1=C.v("b_in", 8 + oc), scalar2=C.v("cmask"),
                                           op0=ALU.add, op1=ALU.mult), reads=[psB, C.vB], writes=[recB[0][oc]])

    for ti, (t0, n) in enumerate(TILES):
        par = ti % 2
        norm_in(par, 3 + t0, n)
        for oc in range(8):
            ps, psB = mm8(wg, par, oc, n)
            S.act.op(lambda e: e.activation(out=gt[:, oc, 0:n], in_=ps[:, 0:n], func=AF.Gelu_apprx_tanh, bias=C.v("b_in", oc)),
                     reads=[psB, C.vB], writes=[gtB[oc]])
            S.gq.dma(fm(gate_o)[:, oc, t0:t0 + n], gt[:, oc, 0:n], reads=[gtB[oc]])
        stg = {}

        def stage1(oc):
            ps, psB = mm8(wrc, par, oc, n)
            S.dve.op(lambda e: e.tensor_scalar(out=rec[par][:, oc, 3:3 + n], in0=ps[:, 0:n], scalar1=C.v("b_in", 8 + oc), scalar2=None, op0=ALU.add),
                     reads=[psB, C.vB], writes=[recB[par][oc]])
            if ti + 1 < len(TILES):
                S.pool.op(lambda e: e.tensor_copy(out=rec[1 - par][:, oc, 0:3], in_=rec[par][:, oc, n:n + 3]),
                          reads=[recB[par][oc]], writes=[recB[1 - par][oc]])
            cv, cvB = cvr.next()
            S.dve.op(lambda e: e.tensor_scalar(out=cv[:, 0:n], in0=rec[par][:, oc, 3:3 + n], scalar1=C.v("cw", 3 * 8 + oc), scalar2=C.v("cb", oc),
                                               op0=ALU.mult, op1=ALU.add), reads=[recB[par][oc], C.vB], writes=[cvB])
            for k in range(3):
                S.dve.op(lambda e: e.scalar_tensor_tensor(out=cv[:, 0:n], in0=rec[par][:, oc, k:k + n], scalar=C.v("cw", k * 8 + oc), in1=cv[:, 0:n],
                                                          op0=ALU.mult, op1=ALU.add), reads=[recB[par][oc], cvB, C.vB], writes=[cvB])
            cvb, cvbB = C.br.next()
            S.dve.op(lambda e: e.tensor_copy(out=cvb[:, 0:n], in_=cv[:, 0:n]), reads=[cvB], writes=[cvbB])
            psa, psaB = C.ps.next()
            S.pe.op(lambda e: e.matmul(psa[:, 0:n], lhsT=wai[:, oc * 128:(oc + 1) * 128], rhs=cvb[:, 0:n], start=True, stop=True),
                    reads=[waiB, cvbB], writes=[psaB])
            psi, psiB = C.ps.next()
            S.pe.op(lambda e: e.matmul(psi[:, 0:n], lhsT=wai[:, 1024 + oc * 128:1024 + (oc + 1) * 128], rhs=cvb[:, 0:n], start=True, stop=True),
                    reads=[waiB, cvbB], writes=[psiB])
            stg[oc] = (cv, cvB, psa, psaB, psi, psiB)

        def stage2(oc):
            cv, cvB, psa, psaB, psi, psiB = stg.pop(oc)
            r_, rB = C.fr.next()
            S.act.op(lambda e: e.activation(out=r_[:, 0:n], in_=psa[:, 0:n], func=AF.Tanh, bias=hb[:, oc:oc + 1], scale=0.5), reads=[psaB, c8B], writes=[rB])
            gi, giB = C.fr.next()
            S.act.op(lambda e: e.activation(out=gi[:, 0:n], in_=psi[:, 0:n], func=AF.Tanh, bias=hb[:, 8 + oc:9 + oc], scale=0.5), reads=[psiB, c8B], writes=[giB])
            a_, aB = C.fr.next()
            S.act.op(lambda e: e.activation(out=a_[:, 0:n], in_=r_[:, 0:n], func=AF.Exp, scale=h8[:, oc:oc + 1], bias=h8[:, oc:oc + 1]), reads=[rB, c8B], writes=[aB])
            S.act.op(lambda e: e.activation(out=r_[:, 0:n], in_=a_[:, 0:n], func=AF.Square), reads=[aB, rB], writes=[rB])
            S.act.op(lambda e: e.activation(out=r_[:, 0:n], in_=r_[:, 0:n], func=AF.Sqrt, bias=qtr[:, 0:1], scale=-0.25), reads=[rB, c8B], writes=[rB])
            S.dve.op(lambda e: e.scalar_tensor_tensor(out=gi[:, 0:n], in0=gi[:, 0:n], scalar=1.0, in1=cv[:, 0:n], op0=ALU.add, op1=ALU.mult),
                     reads=[giB, cvB], writes=[giB])
            S.dve.op(lambda e: e.tensor_tensor(out=gi[:, 0:n], in0=gi[:, 0:n], in1=r_[:, 0:n], op=ALU.mult), reads=[giB, rB], writes=[giB])
            if ti == 0:
                ih, ip = 0.0, 1.0
                rd = []
            else:
                ip = st[:, oc:oc + 1]
                ih = st[:, 8 + oc:9 + oc]
                rd = [stB[oc]]
            S.dve.op(lambda e: e.tensor_tensor_scan(out=hl[:, oc, 0:n], data0=a_[:, 0:n], data1=gi[:, 0:n], initial=ih, op0=ALU.mult, op1=ALU.add),
                     reads=[aB, giB] + rd, writes=[hlB[oc]])
            S.dve.op(lambda e: e.tensor_tensor_scan(out=pc[:, oc, 0:n], data0=a_[:, 0:n], data1=zeros[:, 0:n], initial=ip, op0=ALU.mult, op1=ALU.add),
                     reads=[aB, zB] + rd, writes=[pcB[oc]])
            if ti + 1 < len(TILES):
                S.pool.op(lambda e: e.tensor_copy(out=st[:, 8 + oc:9 + oc], in_=hl[:, oc, n - 1:n]), reads=[hlB[oc]], writes=[stB[oc]])
                S.pool.op(lambda e: e.tensor_copy(out=st[:, oc:oc + 1], in_=pc[:, oc, n - 1:n]), reads=[pcB[oc]], writes=[stB[oc]])
            S.gq.dma(fm(hloc_o)[:, oc, t0:t0 + n], hl[:, oc, 0:n], reads=[hlB[oc]])
            S.gq.dma(fm(pcum_o)[:, oc, t0:t0 + n], pc[:, oc, 0:n], reads=[pcB[oc]])

        stage1(0)
        stage1(1)
        for oc in range(8):
            if oc + 2 < 8:
                stage1(oc + 2)
            stage2(oc)
    S.sp.dma(ends_src, st[:], reads=stB)


NG = 8


def fm_w(w):
    return w.rearrange("(kc p) n -> p kc n", p=128)


def ffn_block(C, sc, hT, hTB, hn, hnB, w_up, w_dn, layer):
    nc, S = C.nc, C.S
    gname = f"ffn_g{layer}"
    for ti, (t0, n) in enumerate(TILES):
        rmsnorm_tile(C, lambda kc: hT[:, kc, t0:t0 + n], [hTB[kc][ti] for kc in range(8)], gname, n,
                     lambda kc: hn[:, kc, t0:t0 + n], [hnB[kc][ti] for kc in range(8)])
    tails = [sc.sb(f"tl{layer}_{i}", [128, 6, 2], F32) for i in range(2)]
    tlB = [[Buf() for _ in range(6)] for _ in range(2)]
    act = [sc.sb(f"act{layer}_{i}", [128, 3, 512], BF16) for i in range(2)]
    actB = [[Buf() for _ in range(3)] for _ in range(2)]
    fcw, fcb = f"fcw{layer}", f"fcb{layer}"
    step = 0
    pending = None

    def emit_down(wd, wdB, ap_, ti, t0, n):
        for oc in range(8):
            ps, psB = C.ps.next()
            for j in range(3):
                S.pe.op(lambda e: e.matmul(ps[:, 0:n], lhsT=wd[:, j * D + oc * 128: j * D + (oc + 1) * 128], rhs=act[ap_][:, j, 0:n],
                                           start=(j == 0), stop=(j == 2)), reads=[wdB, actB[ap_][j]], writes=[psB])
            S.dve.op(lambda e: e.tensor_tensor(out=hT[:, oc, t0:t0 + n], in0=ps[:, 0:n], in1=hT[:, oc, t0:t0 + n], op=ALU.add),
                     reads=[psB, hTB[oc][ti]], writes=[hTB[oc][ti]])

    def load_group(g):
        slot, sB = C.wr.next()
        view = slot[:, 0:8 * 768].rearrange("p (kc n) -> p kc n", kc=8)
        S.gq.dma(view[:, :, 0:384], fm_w(w_up)[:, :, g * 384:(g + 1) * 384], writes=[sB])
        S.gq.dma(view[:, :, 384:768], fm_w(w_up)[:, :, 3072 + g * 384:3072 + (g + 1) * 384], writes=[sB])
        wd, wdB = load_w(C, w_dn[g * 384:(g + 1) * 384, :], D, nk=3)
        return slot, sB, wd, wdB

    nxt = load_group(0)
    for g in range(NG):
        slot, sB, wd, wdB = nxt
        for ti, (t0, n) in enumerate(TILES):
            if ti == 1 and g + 1 < NG:
                nxt = load_group(g + 1)
            ap_ = step % 2
            step += 1
            tp = ti % 2
            for j in range(3):
                outs = []
                for half in range(2):
                    fc = half * 24 + g * 3 + j
                    ci = half * 3 + j
                    ps, psB = C.ps.next()
                    for kc in range(8):
                        c0 = kc * 768 + half * 384 + j * 128
                        S.pe.op(lambda e: e.matmul(ps[:, 0:n], lhsT=slot[:, c0:c0 + 128], rhs=hn[:, kc, t0:t0 + n], start=(kc == 0), stop=(kc == 7)),
                                reads=[sB, hnB[kc][ti]], writes=[psB])
                    raw, rawB = C.fr.next()
                    S.act.op(lambda e: e.activation(out=raw[:, 2:2 + n], in_=ps[:, 0:n], func=AF.Identity), reads=[psB], writes=[rawB])
                    if ti == 0:
                        S.pool.op(lambda e: e.memset(raw[:, 0:2], 0.0), writes=[rawB])
                    else:
                        S.pool.op(lambda e: e.tensor_copy(out=raw[:, 0:2], in_=tails[1 - tp][:, ci, :]), reads=[tlB[1 - tp][ci]], writes=[rawB])
                    if ti + 1 < len(TILES):
                        S.pool.op(lambda e: e.tensor_copy(out=tails[tp][:, ci, :], in_=raw[:, n:n + 2]), reads=[rawB], writes=[tlB[tp][ci]])
                    cv, cvB = C.fr.next()
                    S.act.op(lambda e: e.activation(out=cv[:, 0:n], in_=ps[:, 0:n], func=AF.Identity, bias=C.v(fcb, fc), scale=C.v(fcw, 2 * 48 + fc)),
                             reads=[psB, C.vB], writes=[cvB])
                    for k in range(2):
                        S.dve.op(lambda e: e.scalar_tensor_tensor(out=cv[:, 0:n], in0=raw[:, k:k + n], scalar=C.v(fcw, k * 48 + fc), in1=cv[:, 0:n],
                                                                  op0=ALU.mult, op1=ALU.add), reads=[rawB, cvB, C.vB], writes=[cvB])
                    outs.append((cv, cvB))
                (gv, gvB), (vv, vvB) = outs
                S.act.op(lambda e: e.activation(out=gv[:, 0:n], in_=gv[:, 0:n], func=AF.Gelu_apprx_tanh), reads=[gvB], writes=[gvB])
                S.pool.op(lambda e: e.tensor_tensor(out=act[ap_][:, j, 0:n], in0=gv[:, 0:n], in1=vv[:, 0:n], op=ALU.mult),
                          reads=[gvB, vvB], writes=[actB[ap_][j]])
            if pending is not None:
                emit_down(*pending)
            pending = (wd, wdB, ap_, ti, t0, n)
    emit_down(*pending)


def lru_back(C, sc, xw, gate_i, hloc_i, pcum_i, ends_all, w_out, hT, hTB):
    nc, S = C.nc, C.S
    wo = W1024(C, w_out)
    en = sc.sb("ends_sb", [128, 4, 2, 8], F32)
    enB = Buf("ends")
    S.sp.dma(en[:].rearrange("p j a k -> p j (a k)"), ends_all.rearrange("(j p) f -> p j f", p=128), writes=[enB])
    cr = sc.sb("carry", [128, 8], F32)
    cj = sc.sb("cj", [128, 8], F32)
    crB = Buf("carry")
    S.dve.op(lambda e: e.memset(cr[:], 0.0), writes=[crB])
    S.dve.op(lambda e: e.memset(cj[:], 0.0), writes=[crB])
    for j in range(3):
        S.dve.op(lambda e: e.tensor_tensor(out=cj[:], in0=cj[:], in1=en[:, j, 0, :], op=ALU.mult), reads=[crB, enB], writes=[crB])
        S.dve.op(lambda e: e.tensor_tensor(out=cj[:], in0=cj[:], in1=en[:, j, 1, :], op=ALU.add), reads=[crB, enB], writes=[crB])
        S.dve.op(lambda e: e.scalar_tensor_tensor(out=cr[:], in0=cj[:], scalar=C.v("onehot", j + 1), in1=cr[:], op0=ALU.mult, op1=ALU.add),
                 reads=[crB, C.vB], writes=[crB])
    yb = sc.sb("yb", [128, 8, 512], BF16)
    ybB = [Buf() for _ in range(8)]
    xwf = fm(xw)
    for ti, (t0, n) in enumerate(TILES):
        for kc in range(8):
            hl, hlB = C.fr.next()
            S.sp.dma(hl[:, 0:n], fm(hloc_i)[:, kc, t0:t0 + n], writes=[hlB])
            pcm, pcmB = C.fr.next()
            S.sp.dma(pcm[:, 0:n], fm(pcum_i)[:, kc, t0:t0 + n], writes=[pcmB])
            gt, gtB = C.br.next()
            S.sp.dma(gt[:, 0:n], fm(gate_i)[:, kc, t0:t0 + n], writes=[gtB])
            S.dve.op(lambda e: e.scalar_tensor_tensor(out=hl[:, 0:n], in0=pcm[:, 0:n], scalar=cr[:, kc:kc + 1], in1=hl[:, 0:n], op0=ALU.mult, op1=ALU.add),
                     reads=[pcmB, hlB, crB], writes=[hlB])
            S.pool.op(lambda e: e.tensor_tensor(out=yb[:, kc, 0:n], in0=hl[:, 0:n], in1=gt[:, 0:n], op=ALU.mult),
                      reads=[hlB, gtB], writes=[ybB[kc]])
        for oc in range(8):
            xs, xsB = C.fr.next()
            S.sp.dma(xs[:, 0:n], xwf[:, oc, 3 + t0:3 + t0 + n], writes=[xsB])
            ps, psB = C.ps.next()
            for kc in range(8):
                l_, lB = wo.lhsT(kc, oc)
                S.pe.op(lambda e: e.matmul(ps[:, 0:n], lhsT=l_, rhs=yb[:, kc, 0:n], start=(kc == 0), stop=(kc == 7)),
                        reads=[lB, ybB[kc]], writes=[psB])
            S.dve.op(lambda e: e.scalar_tensor_tensor(out=hT[:, oc, t0:t0 + n], in0=ps[:, 0:n], scalar=C.v("b_out", oc), in1=xs[:, 0:n],
                                                      op0=ALU.add, op1=ALU.add), reads=[psB, xsB, C.vB], writes=[hTB[oc][ti]])


def qkv_block(C, sc, hT, hTB, hn, hnB, wA_d, wR_d, wV_d, cosF, sinF, hn_src, hn_src4, hn_all, hn_all4, qk_sel, v_sel, XB):
    nc, S = C.nc, C.S
    wA, wAB = load_w(C, wA_d, 512)
    wR, wRB = load_w(C, wR_d, 512)
    wV, wVB = load_w(C, wV_d, 256)
    for ti, (t0, n) in enumerate(TILES):
        bl = [hnB[kc][ti] for kc in range(8)]
        rmsnorm_tile(C, lambda kc: hT[:, kc, t0:t0 + n], [hTB[kc][ti] for kc in range(8)], "mix_g1", n,
                     lambda kc: hn[:, kc, t0:t0 + n], bl)
        src = hn_src[ti] if n == 512 else hn_src4
        dst = hn_all[ti] if n == 512 else hn_all4
        stB = []
        S.sp.dma(src.rearrange("(kc p) t -> p kc t", p=128), hn[:, :, t0:t0 + n], reads=bl, writes=[_nb(stB)])
        S.all_gather_async(src, dst, GROUPS, stB, XB["hn"][ti])
    NXH = 4
    xh = [hn[:, :, i * 512:(i + 1) * 512] for i in range(NXH)]
    xhB = [Buf() for _ in range(NXH)]
    xh_guard = [[hnB[kc][i] for kc in range(8)] for i in range(NXH)]
    csr = [sc.sb(f"csr{i}", [128, 512], F32) for i in range(2)]
    snr = [sc.sb(f"snr{i}", [128, 512], F32) for i in range(2)]
    csrB = [Buf() for _ in range(2)]
    snrB = [Buf() for _ in range(2)]
    NVB = 6
    vb = [sc.sb(f"vb{i}", [128, 256], BF16) for i in range(NVB)]
    vbB = [Buf() for _ in range(NVB)]
    vsel_r = v_sel.rearrange("h r v -> r h v")
    vi = 0
    plan = []
    for ti, (t0, n) in enumerate(TILES):
        for j in range(4):
            lo = HALO if (j > 0 and ti == 0) else 0
            plan.append((ti, t0, n, j, lo))

    def issue_loads(k):
        ti, t0, n, j, lo = plan[k]
        all_t = hn_all[ti] if n == 512 else hn_all4
        m_ = n - lo
        s0 = 2048 * j + t0 + lo
        S.sp.dma(xh[k % NXH][:, :, 0:m_], all_t[j * 1024:(j + 1) * 1024, lo:n].rearrange("(kc p) t -> p kc t", p=128), reads=[XB["hn"][ti]],
                 writes=[xhB[k % NXH]] + xh_guard[k % NXH])
        S.sp.dma(csr[k % 2][:, 0:m_], cosF[:, s0:s0 + m_], writes=[csrB[k % 2]])
        S.sp.dma(snr[k % 2][:, 0:m_], sinF[:, s0:s0 + m_], writes=[snrB[k % 2]])

    issue_loads(0)
    for step, (ti, t0, n, j, lo) in enumerate(plan):
        if True:
            m_ = n - lo
            c0 = t0 + lo
            x_, xB = xh[step % NXH], xhB[step % NXH]
            cs, csB, sn, snB = csr[step % 2], csrB[step % 2], snr[step % 2], snrB[step % 2]
            if step + 1 < len(plan):
                issue_loads(step + 1)
            for c4 in range(4):
                psA, psAB = C.ps.next()
                psR, psRB = C.ps.next()
                for (w_, wB_, p_, pB_) in ((wA, wAB, psA, psAB), (wR, wRB, psR, psRB)):
                    for kc in range(8):
                        S.pe.op(lambda e: e.matmul(p_[:, 0:m_], lhsT=w_[:, kc * 512 + c4 * 128: kc * 512 + (c4 + 1) * 128], rhs=x_[:, kc, 0:m_],
                                                   start=(kc == 0), stop=(kc == 7)), reads=[wB_, xB], writes=[pB_])
                t1, t1B = C.fr.next()
                S.dve.op(lambda e: e.tensor_tensor(out=t1[:, 0:m_], in0=psA[:, 0:m_], in1=cs[:, 0:m_], op=ALU.mult), reads=[psAB, csB], writes=[t1B])
                t2, t2B = C.fr.next()
                S.dve.op(lambda e: e.tensor_tensor(out=t2[:, 0:m_], in0=psR[:, 0:m_], in1=sn[:, 0:m_], op=ALU.mult), reads=[psRB, snB], writes=[t2B])
                ob, obB = C.br.next()
                S.dve.op(lambda e: e.tensor_tensor(out=ob[:, 0:m_], in0=t1[:, 0:m_], in1=t2[:, 0:m_], op=ALU.add), reads=[t1B, t2B], writes=[obB])
                r0 = c4 * 512 + j * 128
                S.sp.dma(qk_sel[r0:r0 + 128, c0:c0 + m_], ob[:, 0:m_], reads=[obB], writes=[_nb(XB["selqk"][c4 // 2])])
            for u0 in range(0, m_, 128):
                um = min(128, m_ - u0)
                ps, psB = C.ps.next()
                for kc in range(8):
                    S.pe.op(lambda e: e.matmul(ps[0:um, 0:256], lhsT=x_[:, kc, u0:u0 + um], rhs=wV[:, kc * 256:(kc + 1) * 256],
                                               start=(kc == 0), stop=(kc == 7)), reads=[wVB, xB], writes=[psB])
                v_, vB_ = vb[vi % NVB], vbB[vi % NVB]
                vi += 1
                S.act.op(lambda e: e.activation(out=v_[0:um, :], in_=ps[0:um, 0:256], func=AF.Identity), reads=[psB], writes=[vB_])
                rr = j * WIN + c0 + u0
                S.sp.dma(vsel_r[rr:rr + um, :, :], v_[0:um, :].rearrange("r (h v) -> r h v", h=2), reads=[vB_], writes=[_nb(XB["selv"])])


def attention(C, sc, qk_all, v_all, qk_sel, v_sel, masks, oT_src, oT_all, XB):
    nc, S = C.nc, C.S
    hp = C.cid
    q = sc.sb("q", [128, 2 * TPA], BF16)
    kz = [sc.sb(f"kz{m}", [128, TPA], BF16) for m in range(2)]
    vs = sc.sb("vs", [128, NKT, 2, 128], BF16)
    mk = sc.sb("mk", [128, 128], BF16)
    mkB = Buf("mk")
    qBs = [[Buf() for _ in range(5)] for _ in range(2)]
    kzB = [[Buf() for _ in range(5)] for _ in range(2)]
    vsB = [Buf() for _ in range(NKT)]
    S.pool.op(lambda e: e.memset(vs[:, 64, :, :].rearrange("p b c -> p (b c)"), 0.0), writes=[vsB[64]])
    for hh in range(2):
        S.dve.op(lambda e: e.memset(q[:, hh * TPA + TSEQ:(hh + 1) * TPA], 0.0), writes=[qBs[hh][4]])
    for m in range(2):
        S.pool.op(lambda e: e.memset(kz[m][:], 0.0), writes=kzB[m])
    selB_q, selB_k, selB_v = XB["selqk"][0], XB["selqk"][1], XB["selv"]
    def owner(s_):
        return 4 if s_ >= TSEQ else (0 if s_ < WIN else 1 + (s_ - WIN) // 2048)

    def owners(s_lo, s_hi):
        return sorted({owner(s_lo), owner(s_hi - 1)} | ({owner(x) for x in (WIN, WIN + 2048, WIN + 4096, TSEQ) if s_lo < x < s_hi}))

    def load_q(hh, j):
        lo = 0 if j == 0 else HALO
        r0 = hh * 512 + j * 128
        S.sp.dma(q[:, hh * TPA + 2048 * j + lo: hh * TPA + 2048 * j + WIN], qk_sel[r0:r0 + 128, lo:WIN], reads=selB_q, writes=[qBs[hh][j]])

    def load_kj(hh, j):
        lo = 0 if j == 0 else HALO
        r0 = 1024 + hh * 512 + j * 128
        for m in range(2):
            S.sp.dma(kz[m][m * 64:(m + 1) * 64, 2048 * j + lo:2048 * j + WIN], qk_sel[r0 + m * 64:r0 + (m + 1) * 64, lo:WIN], reads=selB_k, writes=[kzB[m][j]])

    def load_k(hh):
        for j in range(4):
            load_kj(hh, j)

    vsel_r = v_sel.rearrange("h r v -> r h v")

    def load_v(t):
        for p0, p1 in ((0, 16), (16, 128)) if t in (16, 32, 48, 64) else ((0, 128),):
            s_ = t * 128 + p0
            if s_ >= TSEQ:
                continue
            j = 0 if s_ < WIN else 1 + (s_ - WIN) // 2048
            r0 = s_ + 16 * j
            S.sp.dma(vs[p0:p1, t, :, :], vsel_r[r0:r0 + (p1 - p0), :, :], reads=selB_v, writes=[vsB[t]])

    for j in range(4):
        load_q(0, j)
        load_kj(0, j)
        for t in range(16 * j + (1 if j else 0), 16 * (j + 1) + 1):
            load_v(t)
    for j in range(4):
        load_q(1, j)
    S.sp.dma(mk[:], masks[:, 0:128], writes=[mkB])
    lt = sc.sb("lt", [128, 128], F32)
    ls = sc.sb("ls", [128, 4], F32)
    lB = Buf("lam")
    lo_ = VO["lam"]
    S.dve.op(lambda e: e.tensor_tensor(out=lt[:, 0:64], in0=C.vecs[:, lo_:lo_ + 64], in1=C.vecs[:, lo_ + 64:lo_ + 128], op=ALU.mult), reads=[C.vB], writes=[lB])
    S.dve.op(lambda e: e.tensor_tensor(out=lt[:, 64:128], in0=C.vecs[:, lo_ + 128:lo_ + 192], in1=C.vecs[:, lo_ + 192:lo_ + 256], op=ALU.mult), reads=[C.vB, lB], writes=[lB])
    S.dve.op(lambda e: e.tensor_reduce(out=ls[:, 0:1], in_=lt[:, 0:64], axis=mybir.AxisListType.X, op=ALU.add), reads=[lB], writes=[lB])
    S.dve.op(lambda e: e.tensor_reduce(out=ls[:, 1:2], in_=lt[:, 64:128], axis=mybir.AxisListType.X, op=ALU.add), reads=[lB], writes=[lB])
    S.act.op(lambda e: e.activation(out=ls[:, 0:2], in_=ls[:, 0:2], func=AF.Exp), reads=[lB], writes=[lB])
    S.dve.op(lambda e: e.tensor_tensor(out=ls[:, 2:3], in0=ls[:, 1:2], in1=ls[:, 0:1], op=ALU.subtract), reads=[lB], writes=[lB])
    S.dve.op(lambda e: e.tensor_scalar(out=ls[:, 3:4], in0=ls[:, 2:3], scalar1=float(-LAMBDA_INIT), scalar2=None, op0=ALU.add), reads=[lB], writes=[lB])
    nlam = ls[:, 3:4]
    gp = sc.sb("gp", [128, 1], F32)
    gpB = Buf("gp")
    S.dve.op(lambda e: e.tensor_scalar(out=gp[:], in0=C.v("subgp"), scalar1=float(1.0 - LAMBDA_INIT), scalar2=None, op0=ALU.mult),
             reads=[C.vB], writes=[gpB])

    accT = [[sc.psum(f"accT{i}_{m}", [128, 512], F32) for m in range(2)] for i in range(2)]
    accB = [[Buf() for m in range(2)] for i in range(2)]
    sps = Ring(sc, "sps", [128, 512], F32, 3, psum=True)
    rsacc = sc.psum("rsacc", [128, 512], F32)
    rsaccB = Buf("rsacc")
    rs1 = [sc.sb(f"rs1_{i}", [128, 512], F32) for i in range(2)]
    rs1B = [Buf() for _ in range(2)]
    otb = [sc.sb(f"aot{i}", [128, 512], BF16) for i in range(2)]
    otB = [Buf() for _ in range(2)]
    osrcB = [[[] for _ in range(5)] for _ in range(2)]
    LA = 2
    FDLY = 10
    rs1d = [rs1[0], rs1[1]]
    rs1dB = [rs1B[0], rs1B[1]]
    blocks = []
    for hh in range(2):
        for bi, (q0, nq) in enumerate(QBLKS):
            nqs = nq // 128
            qt0 = q0 // 128
            kt_last = min(NKT - 1, qt0 + nqs - 1)
            blocks.append(dict(idx=len(blocks), hh=hh, bi=bi, q0=q0, nq=nq, qt0=qt0, kt_last=kt_last, par=len(blocks) % 2, rs1_started=False))
    steps = [(blk, kt, m) for blk in blocks for kt in range(blk["kt_last"] + 1) for m in range(2)]
    pts = {}
    loaded_k = {0}

    def emit_score(i):
        blk, kt, m = steps[i]
        hh, q0, nq, qt0, par = blk["hh"], blk["q0"], blk["nq"], blk["qt0"], blk["par"]
        if hh not in loaded_k:
            load_k(hh)
            loaded_k.add(hh)
        dq = kt - qt0
        c0 = max(dq, 0) * 128
        sp_, spB = sps.next()
        S.pe.op(lambda e: e.matmul(sp_[:, c0:nq], lhsT=kz[m][:, kt * 128:(kt + 1) * 128],
                                   rhs=q[:, hh * TPA + q0 + c0: hh * TPA + q0 + nq], start=True, stop=True),
                reads=[kzB[m][c_] for c_ in owners(kt * 128, kt * 128 + 128)] + [qBs[hh][c_] for c_ in owners(q0 + c0, q0 + nq)], writes=[spB])
        pt, ptB = C.br.next()
        S.act.op(lambda e: e.activation(out=pt[:, c0:nq], in_=sp_[:, c0:nq], func=AF.Exp, scale=0.125), reads=[spB], writes=[ptB])
        if dq >= 0:
            S.dve.op(lambda e: e.tensor_tensor(out=pt[:, c0:c0 + 128], in0=pt[:, c0:c0 + 128], in1=mk[:], op=ALU.mult),
                     reads=[ptB, mkB], writes=[ptB])
        if m == 1:
            r1, r1B = rs1d[par], rs1dB[par]
            if not blk["rs1_started"]:
                S.dve.op(lambda e: e.tensor_copy(out=r1[:, 0:nq], in_=pt[:, 0:nq]), reads=[ptB], writes=[r1B])
                blk["rs1_started"] = True
            else:
                S.dve.op(lambda e: e.tensor_tensor(out=r1[:, c0:nq], in0=r1[:, c0:nq], in1=pt[:, c0:nq], op=ALU.add),
                         reads=[ptB, r1B], writes=[r1B])
        pts[i] = (pt, ptB, c0)

    def emit_pv(i):
        blk, kt, m = steps[i]
        hh, nq, par, kt_last = blk["hh"], blk["nq"], blk["par"], blk["kt_last"]
        pt, ptB, c0 = pts.pop(i)
        S.pe.op(lambda e: e.matmul(accT[par][m][:, c0:nq], lhsT=vs[:, kt, hh, :], rhs=pt[:, c0:nq],
                                   start=(kt == 0), stop=(kt == kt_last)),
                reads=[ptB, vsB[kt]], writes=[accB[par][m]])
        if m == 0:
            S.pe.op(lambda e: e.matmul(rsacc[:, c0:nq], lhsT=C.ones[:], rhs=pt[:, c0:nq], start=(kt == 0), stop=(kt == kt_last)),
                    reads=[ptB, C.onesB], writes=[rsaccB])

    def fin_early(blk):
        nq = blk["nq"]
        rl, rlB = C.fr.next()
        S.act.op(lambda e: e.activation(out=rl[:, 0:nq], in_=rsacc[:, 0:nq], func=AF.Identity), reads=[rsaccB], writes=[rlB])
        blk["rl0"] = (rl, rlB)

    def fin_late(blk):
        hh, bi, q0, nq, par = blk["hh"], blk["bi"], blk["q0"], blk["nq"], blk["par"]
        r1, r1B = rs1d[par], rs1dB[par]
        tn = []
        for m in range(2):
            if m == 0:
                rl, rlB = blk["rl0"]
                S.dve.op(lambda e: e.reciprocal(out=rl[:, 0:nq], in_=rl[:, 0:nq]), reads=[rlB], writes=[rlB])
            else:
                rl, rlB = C.fr.next()
                hi, hiB = C.br.next()
                S.dve.op(lambda e: e.tensor_copy(out=hi[:, 0:nq], in_=r1[:, 0:nq]), reads=[r1B], writes=[hiB])
                lo2, lo2B = C.br.next()
                S.dve.op(lambda e: e.tensor_tensor(out=lo2[:, 0:nq], in0=r1[:, 0:nq], in1=hi[:, 0:nq], op=ALU.subtract),
                         reads=[r1B, hiB], writes=[lo2B])
                fin, finB = sps.next()
                S.pe.op(lambda e: e.matmul(fin[:, 0:nq], lhsT=C.ones[:], rhs=hi[:, 0:nq], start=True, stop=False), reads=[hiB, C.onesB], writes=[finB])
                S.pe.op(lambda e: e.matmul(fin[:, 0:nq], lhsT=C.ones[:], rhs=lo2[:, 0:nq], start=False, stop=True), reads=[lo2B, C.onesB], writes=[finB])
                S.dve.op(lambda e: e.reciprocal(out=rl[:, 0:nq], in_=fin[:, 0:nq]), reads=[finB], writes=[rlB])
            t_, tB = C.fr.next()
            S.dve.op(lambda e: e.tensor_tensor(out=t_[:, 0:nq], in0=accT[par][m][:, 0:nq], in1=rl[:, 0:nq], op=ALU.mult),
                     reads=[accB[par][m], rlB], writes=[tB])
            tn.append((t_, tB))
        (t0_, t0B), (t1_, t1B) = tn
        S.dve.op(lambda e: e.scalar_tensor_tensor(out=t0_[:, 0:nq], in0=t1_[:, 0:nq], scalar=nlam, in1=t0_[:, 0:nq], op0=ALU.mult, op1=ALU.add),
                 reads=[t0B, t1B, lB], writes=[t0B])
        sq, sqB = C.br.next()
        S.act.op(lambda e: e.activation(out=sq[:, 0:nq], in_=t0_[:, 0:nq], func=AF.Square), reads=[t0B], writes=[sqB])
        fin, finB = sps.next()
        S.pe.op(lambda e: e.matmul(fin[:, 0:nq], lhsT=C.ones[:], rhs=sq[:, 0:nq], start=True, stop=True), reads=[sqB, C.onesB], writes=[finB])
        S.act.op(lambda e: e.activation(out=t1_[:, 0:nq], in_=fin[:, 0:nq], func=AF.Sqrt, bias=C.v("eps"), scale=1.0 / 128),
                 reads=[finB, C.vB], writes=[t1B])
        S.dve.op(lambda e: e.reciprocal(out=t1_[:, 0:nq], in_=t1_[:, 0:nq]), reads=[t1B], writes=[t1B])
        o_t, o_tB = otb[par], otB[par]
        S.dve.op(lambda e: e.scalar_tensor_tensor(out=o_t[:, 0:nq], in0=t0_[:, 0:nq], scalar=gp[:, 0:1], in1=t1_[:, 0:nq], op0=ALU.mult, op1=ALU.mult),
                 reads=[t0B, t1B, gpB], writes=[o_tB])
        c = q0 // 2048
        S.sp.dma(oT_src[hh, c, :, q0 % 2048:q0 % 2048 + nq], o_t[:, 0:nq], reads=[o_tB], writes=[_nb(osrcB[hh][c])])
        if (q0 + nq) % 2048 == 0 or bi == len(QBLKS) - 1:
            S.all_gather_async(oT_src[hh, c], oT_all[hh, c], GROUPS, osrcB[hh][c], XB["o"][hh][c])

    NS = len(steps)
    late = {}
    for i in range(NS + LA + FDLY + 1):
        if i < NS:
            blk, kt, m = steps[i]
            if kt == 0 and m == 0 and (blk["idx"] - 2) in late:
                fin_late(late.pop(blk["idx"] - 2)[1])
            emit_score(i)
        j = i - LA
        if 0 <= j < NS:
            emit_pv(j)
            blk, kt, m = steps[j]
            if kt == blk["kt_last"] and m == 1:
                fin_early(blk)
                late[blk["idx"]] = (i + FDLY, blk)
        for k_ in [k_ for k_, (due, _) in late.items() if due <= i]:
            fin_late(late.pop(k_)[1])
    assert not late and not pts


def attn_out(C, sc, oT_all, w_o, hT, hTB, hn, hnB, XB):
    nc, S = C.nc, C.S
    wo = W1024(C, w_o)
    cid = C.cid
    hn4 = hn[:].rearrange("p (hp hh) t -> p hp hh t", hh=2)
    for hh in range(2):
        o5 = oT_all[hh].rearrange("c (hp p) t -> p hp c t", p=128)
        o5s = oT_all[hh, 1:5].rearrange("c (hp p) t -> p hp c t", p=128)
        S.sp.dma(hn4[:, :, hh, 0:2048].unsqueeze(2), o5[:, :, bass.ds(cid, 1), :], reads=XB["o"][hh],
                 writes=[hnB[hp_ * 2 + hh][ti] for hp_ in range(4) for ti in range(4)])
        S.sp.dma(hn4[:, :, hh, 2048:WIN].unsqueeze(2), o5s[:, :, bass.ds(cid, 1), 0:WIN - 2048], reads=XB["o"][hh],
                 writes=[hnB[hp_ * 2 + hh][4] for hp_ in range(4)])
    for ti, (t0, n) in enumerate(TILES):
        for oc in range(8):
            ps, psB = C.ps.next()
            for kc in range(8):
                l_, lB = wo.lhsT(kc, oc)
                S.pe.op(lambda e: e.matmul(ps[:, 0:n], lhsT=l_, rhs=hn[:, kc, t0:t0 + n],
                                           start=(kc == 0), stop=(kc == 7)), reads=[lB, hnB[kc][ti]], writes=[psB])
            S.dve.op(lambda e: e.tensor_tensor(out=hT[:, oc, t0:t0 + n], in0=ps[:, 0:n], in1=hT[:, oc, t0:t0 + n], op=ALU.add),
                     reads=[psB, hTB[oc][ti]], writes=[hTB[oc][ti]])


def final_norm(C, hT, hTB, out_o):
    S = C.S
    for ti, (t0, n) in enumerate(TILES):
        bl = [hTB[kc][ti] for kc in range(8)]
        rmsnorm_tile(C, lambda kc: hT[:, kc, t0:t0 + n], bl, "fin_g", n, lambda kc: hT[:, kc, t0:t0 + n], bl)
        for kc in range(8):
            S.sp.dma(fm(out_o)[:, kc, t0:t0 + n], hT[:, kc, t0:t0 + n], reads=[bl[kc]])


def build_fused():
    nc = bass.Bass("TRN2", target_bir_lowering=False)
    I = lambda name, shape, dt=F32: nc.dram_tensor(name, list(shape), dt, kind="ExternalInput").ap()
    T_ = lambda name, shape, dt=F32: nc.dram_tensor(name, list(shape), dt).ap()
    xw = I("xw", [D, 3 + WIN])
    vecs = I("vecs", [128, NV])
    ident = I("ident", [128, 128], BF16)
    masks = I("masks", [128, 4 * 512], BF16)
    w_in = I("w_in", [D, 2 * D])
    w_a = I("w_a", [D, 128])
    w_i = I("w_i", [D, 128])
    w_out = I("w_out", [D, D])
    w_up0 = I("w_up0", [D, 6 * D])
    w_dn0 = I("w_dn0", [3 * D, D])
    wA_d = I("wA_hp", [D, 512])
    wR_d = I("wR_hp", [D, 512])
    wV_d = I("wV_hp", [D, 256])
    cosF = I("cosF", [128, TSEQ])
    sinF = I("sinF", [128, TSEQ])
    w_o = I("w_o", [D, D])
    w_up1 = I("w_up1", [D, 6 * D])
    w_dn1 = I("w_dn1", [3 * D, D])
    out_o = nc.dram_tensor("out_o", [D, WIN], F32, kind="ExternalOutput").ap()
    gate_s = T_("gate_s", [D, WIN], BF16)
    hloc_s = T_("hloc_s", [D, WIN])
    pcum_s = T_("pcum_s", [D, WIN])
    ends_src = T_("ends_src", [128, 16])
    ends_all = T_("ends_all", [4 * 128, 16])
    hn_src = T_("hn_src", [4, D, 512], BF16)
    hn_all = T_("hn_all", [4, 4 * D, 512], BF16)
    hn_src4 = T_("hn_src4", [D, 16], BF16)
    hn_all4 = T_("hn_all4", [4 * D, 16], BF16)
    qk_sel = T_("qk_sel", [2 * D, WIN], BF16)
    v_sel = T_("v_sel", [2, 4 * WIN, 128], BF16)
    oT_src = T_("oT_src", [2, 5, 128, 2048], BF16)
    oT_all = T_("oT_all", [2, 5, 4 * 128, 2048], BF16)
    XB = {"hn": [Buf() for _ in range(5)], "o": [[Buf() for _ in range(5)] for _ in range(2)], "selqk": [[], []], "selv": []}
    with nc.allow_low_precision("bf16 matmul operands, fp32 accumulation"), Scope(nc) as top:
        C = Ctx(nc, top, vecs, ident)
        S = C.S
        with Scope(nc) as sc:
            C.phase(sc, n_w=5)
            lru_front(C, sc, xw, w_in, w_a, w_i, gate_s, hloc_s, pcum_s, ends_src)
            S.barrier()
        S.all_gather([(ends_src, ends_all)], GROUPS)
        hT = top.sb("hT", [128, 8, WIN], F32)
        hTB = [[Buf() for _ in TILES] for _ in range(8)]
        with Scope(nc) as sc:
            C.phase(sc, n_w=4)
            hn = sc.sb("hnw", [128, 8, WIN], BF16)
            hnB = [[Buf() for _ in TILES] for _ in range(8)]
            with Scope(nc) as s2:
                lru_back(C, s2, xw, gate_s, hloc_s, pcum_s, ends_all, w_out, hT, hTB)
                S.barrier()
            with Scope(nc) as s2:
                ffn_block(C, s2, hT, hTB, hn, hnB, w_up0, w_dn0, 0)
                S.barrier()
            with Scope(nc) as s2:
                qkv_block(C, s2, hT, hTB, hn, hnB, wA_d, wR_d, wV_d, cosF, sinF, hn_src, hn_src4, hn_all, hn_all4, qk_sel, v_sel, XB)
                S.barrier()
        with Scope(nc) as sc:
            C.phase(sc, n_w=0, n_ps=0)
            attention(C, sc, None, None, qk_sel, v_sel, masks, oT_src, oT_all, XB)
            S.barrier()
        with Scope(nc) as sc:
            C.phase(sc, n_w=4)
            hn = sc.sb("hnw", [128, 8, WIN], BF16)
            hnB = [[Buf() for _ in TILES] for _ in range(8)]
            attn_out(C, sc, oT_all, w_o, hT, hTB, hn, hnB, XB)
            ffn_block(C, sc, hT, hTB, hn, hnB, w_up1, w_dn1, 1)
            final_norm(C, hT, hTB, out_o)
            S.finish()
    return nc


_CACHE = {}


def _get(name, fn):
    if name not in _CACHE:
        _CACHE[name] = fn()
    return _CACHE[name]


def _rope_tables():
    inv = (1.0 / (10000.0 ** (np.arange(0, 64, 2, dtype=np.float32) / np.float32(64)))).astype(np.float32)
    ang = (np.arange(TSEQ, dtype=np.float32)[:, None] * inv[None, :]).astype(np.float32)
    ang = np.concatenate([ang, ang], -1)
    cos = np.cos(ang).astype(np.float32)
    sin = np.sin(ang).astype(np.float32)
    sgn = np.concatenate([-np.ones(32, np.float32), np.ones(32, np.float32)])
    sin = sin * sgn[None, :]
    cosT = np.concatenate([cos.T, cos.T], 0)
    sinT = np.concatenate([sin.T, sin.T], 0)
    return np.ascontiguousarray(cosT), np.ascontiguousarray(sinT)


def _masks():
    p = np.arange(128)[:, None]
    f = np.arange(512)[None, :]
    m = np.stack([((r * 128 + p) <= f) for r in range(4)], 1).astype(np.float32)
    return np.ascontiguousarray(m.reshape(128, 2048)).astype(ml_dtypes.bfloat16)


def kernel(**inp):
    inp = {k: np.asarray(v) for k, v in inp.items()}
    B = 2
    x = inp["x"]
    seq = [np.concatenate([inp["meta_tokens"], x[b]], 0) for b in range(B)]
    cores = [(b, c) for b in range(B) for c in range(4)]
    ids = list(range(NCORE))
    cosT, sinT = _rope_tables()
    wqkv = inp["attn_w_qkv"][0]
    perm = np.arange(2048).reshape(-1, 2, 32)[:, ::-1, :].reshape(-1)
    w_qk = np.ascontiguousarray(wqkv[:, 0:2048])
    shared = {
        "ident": np.eye(128, dtype=np.float32).astype(ml_dtypes.bfloat16),
        "masks": _masks(),
        "w_in": np.ascontiguousarray(inp["lru_w_in"][0]),
        "w_a": np.ascontiguousarray(inp["lru_w_a"][0].reshape(D, 128)),
        "w_i": np.ascontiguousarray(inp["lru_w_i"][0].reshape(D, 128)),
        "w_out": np.ascontiguousarray(inp["lru_w_out"][0]),
        "w_up0": np.ascontiguousarray(inp["ffn_w_up"][0]),
        "w_dn0": np.ascontiguousarray(inp["ffn_w_down"][0]),
        "cosF": cosT,
        "sinF": sinT,
        "w_o": np.ascontiguousarray(inp["attn_w_o"][0]),
        "w_up1": np.ascontiguousarray(inp["ffn_w_up"][1]),
        "w_dn1": np.ascontiguousarray(inp["ffn_w_down"][1]),
    }
    in_maps = []
    for (b, c) in cores:
        s0 = 2048 * c
        w = np.zeros((3 + WIN, D), np.float32)
        w[3:] = seq[b][s0:s0 + WIN]
        if c > 0:
            w[0:3] = seq[b][s0 - 3:s0]
        m = dict(shared)
        m["xw"] = np.ascontiguousarray(w.T)
        m["vecs"] = pack_vecs(inp, c)
        w_qkr = w_qk[:, perm]
        cols = np.concatenate([np.arange(wh * D + (2 * c + hh_) * 128, wh * D + (2 * c + hh_ + 1) * 128) for wh in range(2) for hh_ in range(2)])
        m["wA_hp"] = np.ascontiguousarray(w_qk[:, cols])
        m["wR_hp"] = np.ascontiguousarray(w_qkr[:, cols])
        m["wV_hp"] = np.ascontiguousarray(wqkv[:, 2048 + c * 256:2048 + (c + 1) * 256])
        in_maps.append(m)
    nc = _get("fused", build_fused)
    res = run_bass_kernel_spmd(nc, in_maps, core_ids=ids).results
    out = np.zeros((B, 8192, D), np.float32)
    for i, (b, c) in enumerate(cores):
        out[b, 2048 * c:2048 * (c + 1)] = res[i]["out_o"][:, HALO:].T
    return out
```

```python
import math
import numpy as np
import ml_dtypes
import concourse.bass as bass
import concourse.mybir as mybir
from concourse.bass_utils import run_bass_kernel_spmd
from contextlib import ExitStack

F32 = mybir.dt.float32
BF16 = mybir.dt.bfloat16
AF = mybir.ActivationFunctionType
ALU = mybir.AluOpType

D = 1024
WIN = 2064
HALO = 16
OWN = 2048
TSEQ = 8208
TP = 8704
NCORE = 8
LAMBDA_INIT = 0.8 - 0.6 * math.exp(-0.3 * 1)
TILES = [(0, 512), (512, 512), (1024, 512), (1536, 512), (2048, 16)]
SEM_ROT = 24000
WSLOT = 6144
CC_QOS = "P2"


class Buf:
    __slots__ = ("name", "w", "r")

    def __init__(self, name=""):
        self.name = name
        self.w = None
        self.r = {}


class Eng:
    def __init__(self, S, name, eng, same_sync=True):
        self.S = S
        self.name = name
        self.eng = eng
        self.same_sync = same_sync
        self.known = {}
        self.sem = None
        self.cnt = 0
        self._newsem()

    def _newsem(self):
        self.sem = self.S.nc.alloc_semaphore(f"s{self.name}{self.S.nsem}")
        self.S.nsem += 1
        self.cnt = 0

    def rotate(self):
        if self.cnt > 0:
            self.eng.wait_ge(self.sem, self.cnt)
            self.known[id(self.sem)] = self.cnt
        self.S.old.append((self.sem, self.cnt))
        self._newsem()

    def _collect(self, reads, writes):
        need = {}

        def add(st):
            if st is None:
                return
            sem, val, en = st
            if en == self.name and sem is self.sem:
                if not self.same_sync or val < self.cnt:
                    return
            if self.known.get(id(sem), 0) >= val:
                return
            cur = need.get(id(sem))
            if cur is None or cur[1] < val:
                need[id(sem)] = st

        for b in reads:
            add(b.w)
        for b in writes:
            add(b.w)
            for st in b.r.values():
                add(st)
        return need

    def _emit_waits(self, need):
        for sem, val, en in need.values():
            self.eng.wait_ge(sem, val)
            self.known[id(sem)] = val

    def op(self, fn, reads=(), writes=()):
        if self.cnt >= SEM_ROT:
            self.rotate()
        need = self._collect(reads, writes)
        self._emit_waits(need)
        inst = fn(self.eng)
        self.cnt += 1
        inst.then_inc(self.sem, 1)
        st = (self.sem, self.cnt, self.name)
        for b in reads:
            b.r[id(self.sem)] = st
        for b in writes:
            b.w = st
            b.r = {}
        return inst

    def wait_stamp(self, st):
        need = {}
        sem, val, en = st
        if self.known.get(id(sem), 0) < val:
            self.eng.wait_ge(sem, val)
            self.known[id(sem)] = val


class DmaQ:
    def __init__(self, S, name, eng, nsems=8):
        self.S = S
        self.name = name
        self.eng = eng
        self.known = {}
        self.sems = [S.nc.alloc_semaphore(f"d{name}{i}") for i in range(nsems)]
        self.vals = [0] * nsems
        self.i = 0
        self.same_sync = False
        self.sem = None
        self.cnt = 0

    def dma(self, out, in_, reads=(), writes=(), **kw):
        k = self.i
        self.i = (self.i + 1) % len(self.sems)
        sem = self.sems[k]
        need = Eng._collect(self, reads, writes)
        if self.vals[k] > 0 and self.known.get(id(sem), 0) < self.vals[k]:
            need[id(sem)] = (sem, self.vals[k], self.name)
        Eng._emit_waits(self, need)
        inst = self.eng.dma_start(out=out, in_=in_, **kw)
        self.vals[k] += 16
        inst.then_inc(sem, 16)
        st = (sem, self.vals[k], self.name)
        for b in reads:
            b.r[id(sem)] = st
        for b in writes:
            b.w = st
            b.r = {}
        return st


class Sched:
    def __init__(self, nc):
        self.nc = nc
        self.nsem = 0
        self.old = []
        self.pe = Eng(self, "pe", nc.tensor, same_sync=False)
        self.dve = Eng(self, "dve", nc.vector)
        self.act = Eng(self, "act", nc.scalar)
        self.pool = Eng(self, "pool", nc.gpsimd)
        self.sp = DmaQ(self, "sp", nc.sync, 12)
        self.gq = DmaQ(self, "gq", nc.gpsimd, 6)
        self.engs = [self.pe, self.dve, self.act, self.pool]

    def barrier(self):
        stamps = [(e.sem, e.cnt, e.name) for e in self.engs if e.cnt > 0]
        for q in (self.sp, self.gq):
            for k, sem in enumerate(q.sems):
                if q.vals[k] > 0:
                    stamps.append((sem, q.vals[k], q.name))
        if getattr(self, "ccnt", 0) > 0:
            stamps.append((self.csem, self.ccnt, "cc"))
        for e in self.engs + [self.sp, self.gq]:
            for sem, val, en in stamps:
                if e.known.get(id(sem), 0) < val:
                    e.eng.wait_ge(sem, val)
                    e.known[id(sem)] = val

    def finish(self):
        self.barrier()

    def all_gather_async(self, src, dst, groups, src_bufs, dst_buf):
        if not hasattr(self, "csem"):
            self.csem = self.nc.alloc_semaphore("ccsem")
            self.ccnt = 0
        need = Eng._collect(self.pool, src_bufs, [dst_buf])
        Eng._emit_waits(self.pool, need)
        self.nc.gpsimd.collective_compute("AllGather", ALU.bypass, replica_groups=groups, ins=[src.opt()], outs=[dst.opt()], dma_qos=CC_QOS).then_inc(self.csem)
        self.ccnt += 1
        st = (self.csem, self.ccnt, "cc")
        for b in src_bufs:
            b.r[id(self.csem)] = st
        dst_buf.w = st
        dst_buf.r = {}

    def all_gather(self, pairs, groups):
        self.barrier()
        if not hasattr(self, "csem"):
            self.csem = self.nc.alloc_semaphore("ccsem")
            self.ccnt = 0
        for src, dst in pairs:
            self.nc.gpsimd.collective_compute("AllGather", ALU.bypass, replica_groups=groups, ins=[src.opt()], outs=[dst.opt()]).then_inc(self.csem)
            self.ccnt += 1
        for e in self.engs + [self.sp, self.gq]:
            e.eng.wait_ge(self.csem, self.ccnt)


class Scope:
    _n = 0

    def __init__(self, nc):
        self.nc = nc
        self.stack = ExitStack()

    def __enter__(self):
        self.stack.__enter__()
        return self

    def __exit__(self, *a):
        return self.stack.__exit__(*a)

    def sb(self, name, shape, dtype):
        Scope._n += 1
        return self.stack.enter_context(self.nc.sbuf_tensor(f"{name}_{Scope._n}", list(shape), dtype))

    def psum(self, name, shape, dtype):
        Scope._n += 1
        return self.stack.enter_context(self.nc.psum_tensor(f"{name}_{Scope._n}", list(shape), dtype))


class Ring:
    def __init__(self, sc, name, shape, dtype, n, psum=False):
        alloc = sc.psum if psum else sc.sb
        self.t = [alloc(f"{name}{i}", list(shape), dtype) for i in range(n)]
        self.b = [Buf(f"{name}{i}") for i in range(n)]
        self.i = 0

    def next(self):
        k = self.i
        self.i = (self.i + 1) % len(self.t)
        return self.t[k], self.b[k]


_VEC_LAYOUT = [
    ("mix_g0", 8), ("mix_g1", 8), ("ffn_g0", 8), ("ffn_g1", 8), ("fin_g", 8),
    ("b_in", 16), ("cw", 32), ("cb", 8), ("b_a", 8), ("b_i", 8), ("L", 8), ("b_out", 8),
    ("fcw0", 144), ("fcb0", 48), ("fcw1", 144), ("fcb1", 48),
    ("lam", 256), ("subg", 128), ("eps", 1), ("cmask", 1), ("onehot", 4), ("one", 1), ("subgp", 1),
]
VO = {}
_o = 0
for _n, _c in _VEC_LAYOUT:
    VO[_n] = _o
    _o += _c
NV = _o


def _pk(v):
    return np.ascontiguousarray(v.reshape(-1, 128).T)


def pack_vecs(inp, c):
    V = np.zeros((128, NV), np.float32)

    def put(name, arr):
        V[:, VO[name]:VO[name] + arr.shape[1]] = arr

    put("mix_g0", _pk(inp["mix_norm_g"][0]))
    put("mix_g1", _pk(inp["mix_norm_g"][1]))
    put("ffn_g0", _pk(inp["ffn_norm_g"][0]))
    put("ffn_g1", _pk(inp["ffn_norm_g"][1]))
    put("fin_g", _pk(inp["final_norm_g"]))
    put("b_in", _pk(inp["lru_b_in"][0]))
    put("cw", np.concatenate([_pk(inp["lru_conv_w"][0][k]) for k in range(4)], 1))
    put("cb", _pk(inp["lru_conv_b"][0]))
    put("b_a", _pk(inp["lru_b_a"][0]))
    put("b_i", _pk(inp["lru_b_i"][0]))
    put("L", _pk(inp["lru_L"][0]))
    put("b_out", _pk(inp["lru_b_out"][0]))
    for l in range(2):
        put(f"fcw{l}", np.concatenate([_pk(inp["ffn_conv_w"][l][k]) for k in range(3)], 1))
        put(f"fcb{l}", _pk(inp["ffn_conv_b"][l]))
    lam = np.concatenate([inp["attn_lambda_q1"][0], inp["attn_lambda_k1"][0], inp["attn_lambda_q2"][0], inp["attn_lambda_k2"][0]])
    put("lam", np.broadcast_to(lam[None, :], (128, 256)))
    put("subg", np.broadcast_to(inp["attn_subln_g"][0][None, :], (128, 128)))
    V[:, VO["eps"]] = 1e-6
    V[:, VO["cmask"]] = 0.0 if c == 0 else 1.0
    V[:, VO["onehot"] + c] = 1.0
    V[:, VO["one"]] = 1.0
    V[:, VO["subgp"]] = inp["attn_subln_g"][0]
    return V


class Ctx:
    def __init__(self, nc, sc, vecs_dram, ident_dram, n_f32=8, n_bf=8):
        self.nc = nc
        self.S = Sched(nc)
        S = self.S
        self.ps = None
        self.wr = None
        self.fr = Ring(sc, "fr", [128, 516], F32, n_f32)
        self.br = Ring(sc, "br", [128, 512], BF16, n_bf)
        self.vecs = sc.sb("vecs_sb", [128, NV], F32)
        self.vB = Buf("vecs")
        S.sp.dma(self.vecs[:], vecs_dram, writes=[self.vB])
        self.ident = sc.sb("ident_sb", [128, 128], BF16)
        self.idB = Buf("ident")
        S.sp.dma(self.ident[:], ident_dram, writes=[self.idB])
        self.ones = sc.sb("ones", [128, 128], BF16)
        self.onesB = Buf("ones")
        S.pool.op(lambda e: e.memset(self.ones[:], 1.0), writes=[self.onesB])
        self.cid = nc.sync.partition_id() % 4

    def phase(self, sc, n_w=4, n_ps=8):
        self.ps = Ring(sc, "ps", [128, 512], F32, n_ps, psum=True) if n_ps else None
        self.wr = Ring(sc, "wr", [128, WSLOT], BF16, n_w) if n_w else None

    def v(self, name, i=0):
        o = VO[name] + i
        return self.vecs[:, o:o + 1]


def _nb(lst):
    b = Buf()
    lst.append(b)
    return b


def fm(dram):
    return dram.rearrange("(kc p) t -> p kc t", p=128)


def rmsnorm_tile(C, src, srcB, gname, n, dst, dstB, nfeat=1024.0):
    S = C.S
    ps, psB = C.ps.next()
    for kc in range(8):
        sq, sqB = C.br.next()
        S.act.op(lambda e: e.activation(out=sq[:, 0:n], in_=src(kc), func=AF.Square), reads=[srcB[kc]], writes=[sqB])
        S.pe.op(lambda e: e.matmul(ps[:, 0:n], lhsT=C.ones[:], rhs=sq[:, 0:n], start=(kc == 0), stop=(kc == 7)),
                reads=[sqB, C.onesB], writes=[psB])
    rs, rsB = C.fr.next()
    S.act.op(lambda e: e.activation(out=rs[:, 0:n], in_=ps[:, 0:n], func=AF.Sqrt, bias=C.v("eps"), scale=1.0 / nfeat),
             reads=[psB, C.vB], writes=[rsB])
    S.dve.op(lambda e: e.reciprocal(out=rs[:, 0:n], in_=rs[:, 0:n]), reads=[rsB], writes=[rsB])
    for kc in range(8):
        S.dve.op(lambda e: e.scalar_tensor_tensor(out=dst(kc), in0=src(kc), scalar=C.v(gname, kc), in1=rs[:, 0:n],
                                                  op0=ALU.mult, op1=ALU.mult),
                 reads=[srcB[kc], rsB, C.vB], writes=[dstB[kc]])


def load_w(C, dram_rows_ap, ncols, nk=8):
    assert nk * ncols <= WSLOT
    slot, sB = C.wr.next()
    view = slot[:, 0:nk * ncols].rearrange("p (kc n) -> p kc n", kc=nk)
    C.S.gq.dma(view, dram_rows_ap.rearrange("(kc p) n -> p kc n", p=128), writes=[sB])
    return slot, sB


class W1024:
    def __init__(self, C, dram):
        self.h = [load_w(C, dram[:, i * 512:(i + 1) * 512], 512) for i in range(2)]

    def lhsT(self, kc, oc):
        slot, sB = self.h[oc // 4]
        c0 = kc * 512 + (oc % 4) * 128
        return slot[:, c0:c0 + 128], sB


GROUPS = [[0, 1, 2, 3], [4, 5, 6, 7]]
TPA = 8320
TO = 10240
NKT = 65
QBLKS = [(512 * i, 512) for i in range(16)] + [(8192, 128)]


def lru_front(C, sc, xw, w_in, w_a, w_i, gate_o, hloc_o, pcum_o, ends_src):
    nc, S = C.nc, C.S
    wg = W1024(C, w_in[:, 0:D])
    wrc = W1024(C, w_in[:, D:2 * D])
    wai, waiB = C.wr.next()
    S.gq.dma(wai[:, 0:1024].rearrange("p (h j) -> p h j", h=8), w_a.rearrange("(h p) j -> p h j", p=128), writes=[waiB])
    S.gq.dma(wai[:, 1024:2048].rearrange("p (h j) -> p h j", h=8), w_i.rearrange("(h p) j -> p h j", p=128), writes=[waiB])
    c8 = sc.sb("c8", [128, 8], F32)
    c16 = sc.sb("c16", [128, 8], F32)
    c8B = Buf("c8")
    Lap = C.vecs[:, VO["L"]:VO["L"] + 8]
    S.act.op(lambda e: e.activation(out=c8[:], in_=Lap, func=AF.Sigmoid), reads=[C.vB], writes=[c8B])
    S.act.op(lambda e: e.activation(out=c8[:], in_=c8[:], func=AF.Ln), reads=[c8B], writes=[c8B])
    S.dve.op(lambda e: e.tensor_scalar(out=c16[:], in0=c8[:], scalar1=16.0, scalar2=None, op0=ALU.mult), reads=[c8B], writes=[c8B])
    S.dve.op(lambda e: e.tensor_scalar(out=c8[:], in0=c8[:], scalar1=8.0, scalar2=None, op0=ALU.mult), reads=[c8B], writes=[c8B])
    h8 = sc.sb("h8", [128, 8], F32)
    hb = sc.sb("hb", [128, 16], F32)
    qtr = sc.sb("qtr", [128, 1], F32)
    S.dve.op(lambda e: e.tensor_scalar(out=h8[:], in0=c8[:], scalar1=0.5, scalar2=None, op0=ALU.mult), reads=[c8B], writes=[c8B])
    S.dve.op(lambda e: e.tensor_scalar(out=hb[:, 0:8], in0=C.vecs[:, VO["b_a"]:VO["b_a"] + 8], scalar1=0.5, scalar2=None, op0=ALU.mult), reads=[C.vB, c8B], writes=[c8B])
    S.dve.op(lambda e: e.tensor_scalar(out=hb[:, 8:16], in0=C.vecs[:, VO["b_i"]:VO["b_i"] + 8], scalar1=0.5, scalar2=None, op0=ALU.mult), reads=[C.vB, c8B], writes=[c8B])
    S.dve.op(lambda e: e.memset(qtr[:], 0.25), writes=[c8B])

    xt = sc.sb("xt", [128, 8, 512], F32)
    xtB = [Buf() for _ in range(8)]
    hn = [sc.sb(f"hn{i}", [128, 8, 512], BF16) for i in range(2)]
    hnB = [[Buf() for _ in range(8)] for _ in range(2)]
    rec = [sc.sb(f"rec{i}", [128, 8, 516], F32) for i in range(2)]
    recB = [[Buf() for _ in range(8)] for _ in range(2)]
    gt = sc.sb("gt", [128, 8, 512], BF16)
    gtB = [Buf() for _ in range(8)]
    hl = sc.sb("hl", [128, 8, 512], F32)
    hlB = [Buf() for _ in range(8)]
    pc = sc.sb("pc", [128, 8, 512], F32)
    pcB = [Buf() for _ in range(8)]
    zeros = sc.sb("zeros", [128, 512], F32)
    zB = Buf("zeros")
    S.pool.op(lambda e: e.memset(zeros[:], 0.0), writes=[zB])
    st = sc.sb("st", [128, 16], F32)
    stB = [Buf() for _ in range(8)]
    xwf = fm(xw)
    cvr = Ring(sc, "cvr", [128, 516], F32, 4)

    def norm_in(par, c0, n):
        for kc in range(8):
            S.sp.dma(xt[:, kc, 0:n], xwf[:, kc, c0:c0 + n], writes=[xtB[kc]])
        rmsnorm_tile(C, lambda kc: xt[:, kc, 0:n], xtB, "mix_g0", n,
                     lambda kc: hn[par][:, kc, 0:n], hnB[par])

    def mm8(W, par, oc, n):
        ps, psB = C.ps.next()
        for kc in range(8):
            l_, lB = W.lhsT(kc, oc)
            S.pe.op(lambda e: e.matmul(ps[:, 0:n], lhsT=l_, rhs=hn[par][:, kc, 0:n], start=(kc == 0), stop=(kc == 7)),
                    reads=[lB, hnB[par][kc]], writes=[psB])
        return ps, psB

    norm_in(1, 0, 3)
    for oc in range(8):
        ps, psB = mm8(wrc, 1, oc, 3)
        S.dve.op(lambda e: e.tensor_scalar(out=rec[0][:, oc, 0:3], in0=ps[:, 0:3], scalar1=C.v("b_in", 8 + oc), scalar2=C.v("cmask"),
                                           op0=ALU.add, op1=ALU.mult), reads=[psB, C.vB], writes=[recB[0][oc]])

    for ti, (t0, n) in enumerate(TILES):
        par = ti % 2
        norm_in(par, 3 + t0, n)
        for oc in range(8):
            ps, psB = mm8(wg, par, oc, n)
            S.act.op(lambda e: e.activation(out=gt[:, oc, 0:n], in_=ps[:, 0:n], func=AF.Gelu_apprx_tanh, bias=C.v("b_in", oc)),
                     reads=[psB, C.vB], writes=[gtB[oc]])
            S.gq.dma(fm(gate_o)[:, oc, t0:t0 + n], gt[:, oc, 0:n], reads=[gtB[oc]])
        stg = {}

        def stage1(oc):
            ps, psB = mm8(wrc, par, oc, n)
            S.dve.op(lambda e: e.tensor_scalar(out=rec[par][:, oc, 3:3 + n], in0=ps[:, 0:n], scalar1=C.v("b_in", 8 + oc), scalar2=None, op0=ALU.add),
                     reads=[psB, C.vB], writes=[recB[par][oc]])
            if ti + 1 < len(TILES):
                S.pool.op(lambda e: e.tensor_copy(out=rec[1 - par][:, oc, 0:3], in_=rec[par][:, oc, n:n + 3]),
                          reads=[recB[par][oc]], writes=[recB[1 - par][oc]])
            cv, cvB = cvr.next()
            S.dve.op(lambda e: e.tensor_scalar(out=cv[:, 0:n], in0=rec[par][:, oc, 3:3 + n], scalar1=C.v("cw", 3 * 8 + oc), scalar2=C.v("cb", oc),
                                               op0=ALU.mult, op1=ALU.add), reads=[recB[par][oc], C.vB], writes=[cvB])
            for k in range(3):
                S.dve.op(lambda e: e.scalar_tensor_tensor(out=cv[:, 0:n], in0=rec[par][:, oc, k:k + n], scalar=C.v("cw", k * 8 + oc), in1=cv[:, 0:n],
                                                          op0=ALU.mult, op1=ALU.add), reads=[recB[par][oc], cvB, C.vB], writes=[cvB])
            cvb, cvbB = C.br.next()
            S.dve.op(lambda e: e.tensor_copy(out=cvb[:, 0:n], in_=cv[:, 0:n]), reads=[cvB], writes=[cvbB])
            psa, psaB = C.ps.next()
            S.pe.op(lambda e: e.matmul(psa[:, 0:n], lhsT=wai[:, oc * 128:(oc + 1) * 128], rhs=cvb[:, 0:n], start=True, stop=True),
                    reads=[waiB, cvbB], writes=[psaB])
            psi, psiB = C.ps.next()
            S.pe.op(lambda e: e.matmul(psi[:, 0:n], lhsT=wai[:, 1024 + oc * 128:1024 + (oc + 1) * 128], rhs=cvb[:, 0:n], start=True, stop=True),
                    reads=[waiB, cvbB], writes=[psiB])
            stg[oc] = (cv, cvB, psa, psaB, psi, psiB)

        def stage2(oc):
            cv, cvB, psa, psaB, psi, psiB = stg.pop(oc)
            r_, rB = C.fr.next()
            S.act.op(lambda e: e.activation(out=r_[:, 0:n], in_=psa[:, 0:n], func=AF.Tanh, bias=hb[:, oc:oc + 1], scale=0.5), reads=[psaB, c8B], writes=[rB])
            gi, giB = C.fr.next()
            S.act.op(lambda e: e.activation(out=gi[:, 0:n], in_=psi[:, 0:n], func=AF.Tanh, bias=hb[:, 8 + oc:9 + oc], scale=0.5), reads=[psiB, c8B], writes=[giB])
            a_, aB = C.fr.next()
            S.act.op(lambda e: e.activation(out=a_[:, 0:n], in_=r_[:, 0:n], func=AF.Exp, scale=h8[:, oc:oc + 1], bias=h8[:, oc:oc + 1]), reads=[rB, c8B], writes=[aB])
            S.act.op(lambda e: e.activation(out=r_[:, 0:n], in_=a_[:, 0:n], func=AF.Square), reads=[aB, rB], writes=[rB])
            S.act.op(lambda e: e.activation(out=r_[:, 0:n], in_=r_[:, 0:n], func=AF.Sqrt, bias=qtr[:, 0:1], scale=-0.25), reads=[rB, c8B], writes=[rB])
            S.dve.op(lambda e: e.scalar_tensor_tensor(out=gi[:, 0:n], in0=gi[:, 0:n], scalar=1.0, in1=cv[:, 0:n], op0=ALU.add, op1=ALU.mult),
                     reads=[giB, cvB], writes=[giB])
            S.dve.op(lambda e: e.tensor_tensor(out=gi[:, 0:n], in0=gi[:, 0:n], in1=r_[:, 0:n], op=ALU.mult), reads=[giB, rB], writes=[giB])
            if ti == 0:
                ih, ip = 0.0, 1.0
                rd = []
            else:
                ip = st[:, oc:oc + 1]
                ih = st[:, 8 + oc:9 + oc]
                rd = [stB[oc]]
            S.dve.op(lambda e: e.tensor_tensor_scan(out=hl[:, oc, 0:n], data0=a_[:, 0:n], data1=gi[:, 0:n], initial=ih, op0=ALU.mult, op1=ALU.add),
                     reads=[aB, giB] + rd, writes=[hlB[oc]])
            S.dve.op(lambda e: e.tensor_tensor_scan(out=pc[:, oc, 0:n], data0=a_[:, 0:n], data1=zeros[:, 0:n], initial=ip, op0=ALU.mult, op1=ALU.add),
                     reads=[aB, zB] + rd, writes=[pcB[oc]])
            if ti + 1 < len(TILES):
                S.pool.op(lambda e: e.tensor_copy(out=st[:, 8 + oc:9 + oc], in_=hl[:, oc, n - 1:n]), reads=[hlB[oc]], writes=[stB[oc]])
                S.pool.op(lambda e: e.tensor_copy(out=st[:, oc:oc + 1], in_=pc[:, oc, n - 1:n]), reads=[pcB[oc]], writes=[stB[oc]])
            S.gq.dma(fm(hloc_o)[:, oc, t0:t0 + n], hl[:, oc, 0:n], reads=[hlB[oc]])
            S.gq.dma(fm(pcum_o)[:, oc, t0:t0 + n], pc[:, oc, 0:n], reads=[pcB[oc]])

        stage1(0)
        stage1(1)
        for oc in range(8):
            if oc + 2 < 8:
                stage1(oc + 2)
            stage2(oc)
    S.sp.dma(ends_src, st[:], reads=stB)


NG = 8


def fm_w(w):
    return w.rearrange("(kc p) n -> p kc n", p=128)


def ffn_block(C, sc, hT, hTB, hn, hnB, w_up, w_dn, layer):
    nc, S = C.nc, C.S
    gname = f"ffn_g{layer}"
    for ti, (t0, n) in enumerate(TILES):
        rmsnorm_tile(C, lambda kc: hT[:, kc, t0:t0 + n], [hTB[kc][ti] for kc in range(8)], gname, n,
                     lambda kc: hn[:, kc, t0:t0 + n], [hnB[kc][ti] for kc in range(8)])
    tails = [sc.sb(f"tl{layer}_{i}", [128, 6, 2], F32) for i in range(2)]
    tlB = [[Buf() for _ in range(6)] for _ in range(2)]
    act = [sc.sb(f"act{layer}_{i}", [128, 3, 512], BF16) for i in range(2)]
    actB = [[Buf() for _ in range(3)] for _ in range(2)]
    fcw, fcb = f"fcw{layer}", f"fcb{layer}"
    step = 0
    pending = None

    def emit_down(wd, wdB, ap_, ti, t0, n):
        for oc in range(8):
            ps, psB = C.ps.next()
            for j in range(3):
                S.pe.op(lambda e: e.matmul(ps[:, 0:n], lhsT=wd[:, j * D + oc * 128: j * D + (oc + 1) * 128], rhs=act[ap_][:, j, 0:n],
                                           start=(j == 0), stop=(j == 2)), reads=[wdB, actB[ap_][j]], writes=[psB])
            S.dve.op(lambda e: e.tensor_tensor(out=hT[:, oc, t0:t0 + n], in0=ps[:, 0:n], in1=hT[:, oc, t0:t0 + n], op=ALU.add),
                     reads=[psB, hTB[oc][ti]], writes=[hTB[oc][ti]])

    def load_group(g):
        slot, sB = C.wr.next()
        view = slot[:, 0:8 * 768].rearrange("p (kc n) -> p kc n", kc=8)
        S.gq.dma(view[:, :, 0:384], fm_w(w_up)[:, :, g * 384:(g + 1) * 384], writes=[sB])
        S.gq.dma(view[:, :, 384:768], fm_w(w_up)[:, :, 3072 + g * 384:3072 + (g + 1) * 384], writes=[sB])
        wd, wdB = load_w(C, w_dn[g * 384:(g + 1) * 384, :], D, nk=3)
        return slot, sB, wd, wdB

    nxt = load_group(0)
    for g in range(NG):
        slot, sB, wd, wdB = nxt
        for ti, (t0, n) in enumerate(TILES):
            if ti == 1 and g + 1 < NG:
                nxt = load_group(g + 1)
            ap_ = step % 2
            step += 1
            tp = ti % 2
            for j in range(3):
                outs = []
                for half in range(2):
                    fc = half * 24 + g * 3 + j
                    ci = half * 3 + j
                    ps, psB = C.ps.next()
                    for kc in range(8):
                        c0 = kc * 768 + half * 384 + j * 128
                        S.pe.op(lambda e: e.matmul(ps[:, 0:n], lhsT=slot[:, c0:c0 + 128], rhs=hn[:, kc, t0:t0 + n], start=(kc == 0), stop=(kc == 7)),
                                reads=[sB, hnB[kc][ti]], writes=[psB])
                    raw, rawB = C.fr.next()
                    S.act.op(lambda e: e.activation(out=raw[:, 2:2 + n], in_=ps[:, 0:n], func=AF.Identity), reads=[psB], writes=[rawB])
                    if ti == 0:
                        S.pool.op(lambda e: e.memset(raw[:, 0:2], 0.0), writes=[rawB])
                    else:
                        S.pool.op(lambda e: e.tensor_copy(out=raw[:, 0:2], in_=tails[1 - tp][:, ci, :]), reads=[tlB[1 - tp][ci]], writes=[rawB])
                    if ti + 1 < len(TILES):
                        S.pool.op(lambda e: e.tensor_copy(out=tails[tp][:, ci, :], in_=raw[:, n:n + 2]), reads=[rawB], writes=[tlB[tp][ci]])
                    cv, cvB = C.fr.next()
                    S.act.op(lambda e: e.activation(out=cv[:, 0:n], in_=ps[:, 0:n], func=AF.Identity, bias=C.v(fcb, fc), scale=C.v(fcw, 2 * 48 + fc)),
                             reads=[psB, C.vB], writes=[cvB])
                    for k in range(2):
                        S.dve.op(lambda e: e.scalar_tensor_tensor(out=cv[:, 0:n], in0=raw[:, k:k + n], scalar=C.v(fcw, k * 48 + fc), in1=cv[:, 0:n],
                                                                  op0=ALU.mult, op1=ALU.add), reads=[rawB, cvB, C.vB], writes=[cvB])
                    outs.append((cv, cvB))
                (gv, gvB), (vv, vvB) = outs
                S.act.op(lambda e: e.activation(out=gv[:, 0:n], in_=gv[:, 0:n], func=AF.Gelu_apprx_tanh), reads=[gvB], writes=[gvB])
                S.dve.op(lambda e: e.tensor_tensor(out=act[ap_][:, j, 0:n], in0=gv[:, 0:n], in1=vv[:, 0:n], op=ALU.mult),
                         reads=[gvB, vvB], writes=[actB[ap_][j]])
            if pending is not None:
                emit_down(*pending)
            pending = (wd, wdB, ap_, ti, t0, n)
    emit_down(*pending)


def lru_back(C, sc, xw, gate_i, hloc_i, pcum_i, ends_all, w_out, hT, hTB):
    nc, S = C.nc, C.S
    wo = W1024(C, w_out)
    en = sc.sb("ends_sb", [128, 4, 2, 8], F32)
    enB = Buf("ends")
    S.sp.dma(en[:].rearrange("p j a k -> p j (a k)"), ends_all.rearrange("(j p) f -> p j f", p=128), writes=[enB])
    cr = sc.sb("carry", [128, 8], F32)
    cj = sc.sb("cj", [128, 8], F32)
    crB = Buf("carry")
    S.dve.op(lambda e: e.memset(cr[:], 0.0), writes=[crB])
    S.dve.op(lambda e: e.memset(cj[:], 0.0), writes=[crB])
    for j in range(3):
        S.dve.op(lambda e: e.tensor_tensor(out=cj[:], in0=cj[:], in1=en[:, j, 0, :], op=ALU.mult), reads=[crB, enB], writes=[crB])
        S.dve.op(lambda e: e.tensor_tensor(out=cj[:], in0=cj[:], in1=en[:, j, 1, :], op=ALU.add), reads=[crB, enB], writes=[crB])
        S.dve.op(lambda e: e.scalar_tensor_tensor(out=cr[:], in0=cj[:], scalar=C.v("onehot", j + 1), in1=cr[:], op0=ALU.mult, op1=ALU.add),
                 reads=[crB, C.vB], writes=[crB])
    yb = sc.sb("yb", [128, 8, 512], BF16)
    ybB = [Buf() for _ in range(8)]
    xwf = fm(xw)
    for ti, (t0, n) in enumerate(TILES):
        for kc in range(8):
            hl, hlB = C.fr.next()
            S.sp.dma(hl[:, 0:n], fm(hloc_i)[:, kc, t0:t0 + n], writes=[hlB])
            pcm, pcmB = C.fr.next()
            S.sp.dma(pcm[:, 0:n], fm(pcum_i)[:, kc, t0:t0 + n], writes=[pcmB])
            gt, gtB = C.br.next()
            S.sp.dma(gt[:, 0:n], fm(gate_i)[:, kc, t0:t0 + n], writes=[gtB])
            S.dve.op(lambda e: e.scalar_tensor_tensor(out=hl[:, 0:n], in0=pcm[:, 0:n], scalar=cr[:, kc:kc + 1], in1=hl[:, 0:n], op0=ALU.mult, op1=ALU.add),
                     reads=[pcmB, hlB, crB], writes=[hlB])
            S.pool.op(lambda e: e.tensor_tensor(out=yb[:, kc, 0:n], in0=hl[:, 0:n], in1=gt[:, 0:n], op=ALU.mult),
                      reads=[hlB, gtB], writes=[ybB[kc]])
        for oc in range(8):
            xs, xsB = C.fr.next()
            S.sp.dma(xs[:, 0:n], xwf[:, oc, 3 + t0:3 + t0 + n], writes=[xsB])
            ps, psB = C.ps.next()
            for kc in range(8):
                l_, lB = wo.lhsT(kc, oc)
                S.pe.op(lambda e: e.matmul(ps[:, 0:n], lhsT=l_, rhs=yb[:, kc, 0:n], start=(kc == 0), stop=(kc == 7)),
                        reads=[lB, ybB[kc]], writes=[psB])
            S.dve.op(lambda e: e.scalar_tensor_tensor(out=hT[:, oc, t0:t0 + n], in0=ps[:, 0:n], scalar=C.v("b_out", oc), in1=xs[:, 0:n],
                                                      op0=ALU.add, op1=ALU.add), reads=[psB, xsB, C.vB], writes=[hTB[oc][ti]])


def qkv_block(C, sc, hT, hTB, hn, hnB, wA_d, wR_d, wV_d, cosF, sinF, hn_src, hn_src4, hn_all, hn_all4, qk_sel, v_sel, XB):
    nc, S = C.nc, C.S
    wA, wAB = load_w(C, wA_d, 512)
    wR, wRB = load_w(C, wR_d, 512)
    wV, wVB = load_w(C, wV_d, 256)
    for ti, (t0, n) in enumerate(TILES):
        bl = [hnB[kc][ti] for kc in range(8)]
        rmsnorm_tile(C, lambda kc: hT[:, kc, t0:t0 + n], [hTB[kc][ti] for kc in range(8)], "mix_g1", n,
                     lambda kc: hn[:, kc, t0:t0 + n], bl)
        src = hn_src[ti] if n == 512 else hn_src4
        dst = hn_all[ti] if n == 512 else hn_all4
        stB = []
        S.sp.dma(src.rearrange("(kc p) t -> p kc t", p=128), hn[:, :, t0:t0 + n], reads=bl, writes=[_nb(stB)])
        S.all_gather_async(src, dst, GROUPS, stB, XB["hn"][ti])
    NXH = 4
    xh = [hn[:, :, i * 512:(i + 1) * 512] for i in range(NXH)]
    xhB = [Buf() for _ in range(NXH)]
    xh_guard = [[hnB[kc][i] for kc in range(8)] for i in range(NXH)]
    csr = [sc.sb(f"csr{i}", [128, 512], F32) for i in range(2)]
    snr = [sc.sb(f"snr{i}", [128, 512], F32) for i in range(2)]
    csrB = [Buf() for _ in range(2)]
    snrB = [Buf() for _ in range(2)]
    NVB = 6
    vb = [sc.sb(f"vb{i}", [128, 256], BF16) for i in range(NVB)]
    vbB = [Buf() for _ in range(NVB)]
    vsel_r = v_sel.rearrange("h r v -> r h v")
    vi = 0
    plan = []
    for ti, (t0, n) in enumerate(TILES):
        for j in range(4):
            lo = HALO if (j > 0 and ti == 0) else 0
            plan.append((ti, t0, n, j, lo))

    def issue_loads(k):
        ti, t0, n, j, lo = plan[k]
        all_t = hn_all[ti] if n == 512 else hn_all4
        m_ = n - lo
        s0 = 2048 * j + t0 + lo
        S.sp.dma(xh[k % NXH][:, :, 0:m_], all_t[j * 1024:(j + 1) * 1024, lo:n].rearrange("(kc p) t -> p kc t", p=128), reads=[XB["hn"][ti]],
                 writes=[xhB[k % NXH]] + xh_guard[k % NXH])
        S.sp.dma(csr[k % 2][:, 0:m_], cosF[:, s0:s0 + m_], writes=[csrB[k % 2]])
        S.sp.dma(snr[k % 2][:, 0:m_], sinF[:, s0:s0 + m_], writes=[snrB[k % 2]])

    issue_loads(0)
    for step, (ti, t0, n, j, lo) in enumerate(plan):
        if True:
            m_ = n - lo
            c0 = t0 + lo
            x_, xB = xh[step % NXH], xhB[step % NXH]
            cs, csB, sn, snB = csr[step % 2], csrB[step % 2], snr[step % 2], snrB[step % 2]
            if step + 1 < len(plan):
                issue_loads(step + 1)
            for c4 in range(4):
                psA, psAB = C.ps.next()
                psR, psRB = C.ps.next()
                for (w_, wB_, p_, pB_) in ((wA, wAB, psA, psAB), (wR, wRB, psR, psRB)):
                    for kc in range(8):
                        S.pe.op(lambda e: e.matmul(p_[:, 0:m_], lhsT=w_[:, kc * 512 + c4 * 128: kc * 512 + (c4 + 1) * 128], rhs=x_[:, kc, 0:m_],
                                                   start=(kc == 0), stop=(kc == 7)), reads=[wB_, xB], writes=[pB_])
                t1, t1B = C.fr.next()
                S.dve.op(lambda e: e.tensor_tensor(out=t1[:, 0:m_], in0=psA[:, 0:m_], in1=cs[:, 0:m_], op=ALU.mult), reads=[psAB, csB], writes=[t1B])
                t2, t2B = C.fr.next()
                S.dve.op(lambda e: e.tensor_tensor(out=t2[:, 0:m_], in0=psR[:, 0:m_], in1=sn[:, 0:m_], op=ALU.mult), reads=[psRB, snB], writes=[t2B])
                ob, obB = C.br.next()
                S.dve.op(lambda e: e.tensor_tensor(out=ob[:, 0:m_], in0=t1[:, 0:m_], in1=t2[:, 0:m_], op=ALU.add), reads=[t1B, t2B], writes=[obB])
                r0 = c4 * 512 + j * 128
                S.sp.dma(qk_sel[r0:r0 + 128, c0:c0 + m_], ob[:, 0:m_], reads=[obB], writes=[_nb(XB["selqk"][c4 // 2])])
            for u0 in range(0, m_, 128):
                um = min(128, m_ - u0)
                ps, psB = C.ps.next()
                for kc in range(8):
                    S.pe.op(lambda e: e.matmul(ps[0:um, 0:256], lhsT=x_[:, kc, u0:u0 + um], rhs=wV[:, kc * 256:(kc + 1) * 256],
                                               start=(kc == 0), stop=(kc == 7)), reads=[wVB, xB], writes=[psB])
                v_, vB_ = vb[vi % NVB], vbB[vi % NVB]
                vi += 1
                S.act.op(lambda e: e.activation(out=v_[0:um, :], in_=ps[0:um, 0:256], func=AF.Identity), reads=[psB], writes=[vB_])
                rr = j * WIN + c0 + u0
                S.sp.dma(vsel_r[rr:rr + um, :, :], v_[0:um, :].rearrange("r (h v) -> r h v", h=2), reads=[vB_], writes=[_nb(XB["selv"])])


def attention(C, sc, qk_all, v_all, qk_sel, v_sel, masks, oT_src, oT_all, XB):
    nc, S = C.nc, C.S
    hp = C.cid
    q = sc.sb("q", [128, 2 * TPA], BF16)
    kz = [sc.sb(f"kz{m}", [128, TPA], BF16) for m in range(2)]
    vs = sc.sb("vs", [128, NKT, 2, 128], BF16)
    mk = sc.sb("mk", [128, 128], BF16)
    mkB = Buf("mk")
    qBs = [[Buf() for _ in range(5)] for _ in range(2)]
    kzB = [[Buf() for _ in range(5)] for _ in range(2)]
    vsB = [Buf() for _ in range(NKT)]
    S.pool.op(lambda e: e.memset(vs[:, 64, :, :].rearrange("p b c -> p (b c)"), 0.0), writes=[vsB[64]])
    for hh in range(2):
        S.dve.op(lambda e: e.memset(q[:, hh * TPA + TSEQ:(hh + 1) * TPA], 0.0), writes=[qBs[hh][4]])
    for m in range(2):
        S.pool.op(lambda e: e.memset(kz[m][:], 0.0), writes=kzB[m])
    selB_q, selB_k, selB_v = XB["selqk"][0], XB["selqk"][1], XB["selv"]
    def owner(s_):
        return 4 if s_ >= TSEQ else (0 if s_ < WIN else 1 + (s_ - WIN) // 2048)

    def owners(s_lo, s_hi):
        return sorted({owner(s_lo), owner(s_hi - 1)} | ({owner(x) for x in (WIN, WIN + 2048, WIN + 4096, TSEQ) if s_lo < x < s_hi}))

    def load_q(hh, j):
        lo = 0 if j == 0 else HALO
        r0 = hh * 512 + j * 128
        S.sp.dma(q[:, hh * TPA + 2048 * j + lo: hh * TPA + 2048 * j + WIN], qk_sel[r0:r0 + 128, lo:WIN], reads=selB_q, writes=[qBs[hh][j]])

    def load_kj(hh, j):
        lo = 0 if j == 0 else HALO
        r0 = 1024 + hh * 512 + j * 128
        for m in range(2):
            S.sp.dma(kz[m][m * 64:(m + 1) * 64, 2048 * j + lo:2048 * j + WIN], qk_sel[r0 + m * 64:r0 + (m + 1) * 64, lo:WIN], reads=selB_k, writes=[kzB[m][j]])

    def load_k(hh):
        for j in range(4):
            load_kj(hh, j)

    vsel_r = v_sel.rearrange("h r v -> r h v")

    def load_v(t):
        for p0, p1 in ((0, 16), (16, 128)) if t in (16, 32, 48, 64) else ((0, 128),):
            s_ = t * 128 + p0
            if s_ >= TSEQ:
                continue
            j = 0 if s_ < WIN else 1 + (s_ - WIN) // 2048
            r0 = s_ + 16 * j
            S.sp.dma(vs[p0:p1, t, :, :], vsel_r[r0:r0 + (p1 - p0), :, :], reads=selB_v, writes=[vsB[t]])

    for j in range(4):
        load_q(0, j)
        load_kj(0, j)
        for t in range(16 * j + (1 if j else 0), 16 * (j + 1) + 1):
            load_v(t)
    for j in range(4):
        load_q(1, j)
    S.sp.dma(mk[:], masks[:, 0:128], writes=[mkB])
    lt = sc.sb("lt", [128, 128], F32)
    ls = sc.sb("ls", [128, 4], F32)
    lB = Buf("lam")
    lo_ = VO["lam"]
    S.dve.op(lambda e: e.tensor_tensor(out=lt[:, 0:64], in0=C.vecs[:, lo_:lo_ + 64], in1=C.vecs[:, lo_ + 64:lo_ + 128], op=ALU.mult), reads=[C.vB], writes=[lB])
    S.dve.op(lambda e: e.tensor_tensor(out=lt[:, 64:128], in0=C.vecs[:, lo_ + 128:lo_ + 192], in1=C.vecs[:, lo_ + 192:lo_ + 256], op=ALU.mult), reads=[C.vB, lB], writes=[lB])
    S.dve.op(lambda e: e.tensor_reduce(out=ls[:, 0:1], in_=lt[:, 0:64], axis=mybir.AxisListType.X, op=ALU.add), reads=[lB], writes=[lB])
    S.dve.op(lambda e: e.tensor_reduce(out=ls[:, 1:2], in_=lt[:, 64:128], axis=mybir.AxisListType.X, op=ALU.add), reads=[lB], writes=[lB])
    S.act.op(lambda e: e.activation(out=ls[:, 0:2], in_=ls[:, 0:2], func=AF.Exp), reads=[lB], writes=[lB])
    S.dve.op(lambda e: e.tensor_tensor(out=ls[:, 2:3], in0=ls[:, 1:2], in1=ls[:, 0:1], op=ALU.subtract), reads=[lB], writes=[lB])
    S.dve.op(lambda e: e.tensor_scalar(out=ls[:, 3:4], in0=ls[:, 2:3], scalar1=float(-LAMBDA_INIT), scalar2=None, op0=ALU.add), reads=[lB], writes=[lB])
    nlam = ls[:, 3:4]
    gp = sc.sb("gp", [128, 1], F32)
    gpB = Buf("gp")
    S.dve.op(lambda e: e.tensor_scalar(out=gp[:], in0=C.v("subgp"), scalar1=float(1.0 - LAMBDA_INIT), scalar2=None, op0=ALU.mult),
             reads=[C.vB], writes=[gpB])

    accT = [[sc.psum(f"accT{i}_{m}", [128, 512], F32) for m in range(2)] for i in range(2)]
    accB = [[Buf() for m in range(2)] for i in range(2)]
    sps = Ring(sc, "sps", [128, 512], F32, 3, psum=True)
    rsacc = sc.psum("rsacc", [128, 512], F32)
    rsaccB = Buf("rsacc")
    rs1 = [sc.sb(f"rs1_{i}", [128, 512], F32) for i in range(2)]
    rs1B = [Buf() for _ in range(2)]
    otb = [sc.sb(f"aot{i}", [128, 512], BF16) for i in range(2)]
    otB = [Buf() for _ in range(2)]
    osrcB = [[[] for _ in range(5)] for _ in range(2)]
    LA = 2
    FDLY = 10
    rs1d = [rs1[0], rs1[1]]
    rs1dB = [rs1B[0], rs1B[1]]
    blocks = []
    for hh in range(2):
        for bi, (q0, nq) in enumerate(QBLKS):
            nqs = nq // 128
            qt0 = q0 // 128
            kt_last = min(NKT - 1, qt0 + nqs - 1)
            blocks.append(dict(idx=len(blocks), hh=hh, bi=bi, q0=q0, nq=nq, qt0=qt0, kt_last=kt_last, par=len(blocks) % 2, rs1_started=False))
    steps = [(blk, kt, m) for blk in blocks for kt in range(blk["kt_last"] + 1) for m in range(2)]
    pts = {}
    loaded_k = {0}

    def emit_score(i):
        blk, kt, m = steps[i]
        hh, q0, nq, qt0, par = blk["hh"], blk["q0"], blk["nq"], blk["qt0"], blk["par"]
        if hh not in loaded_k:
            load_k(hh)
            loaded_k.add(hh)
        dq = kt - qt0
        c0 = max(dq, 0) * 128
        sp_, spB = sps.next()
        S.pe.op(lambda e: e.matmul(sp_[:, c0:nq], lhsT=kz[m][:, kt * 128:(kt + 1) * 128],
                                   rhs=q[:, hh * TPA + q0 + c0: hh * TPA + q0 + nq], start=True, stop=True),
                reads=[kzB[m][c_] for c_ in owners(kt * 128, kt * 128 + 128)] + [qBs[hh][c_] for c_ in owners(q0 + c0, q0 + nq)], writes=[spB])
        pt, ptB = C.br.next()
        S.act.op(lambda e: e.activation(out=pt[:, c0:nq], in_=sp_[:, c0:nq], func=AF.Exp, scale=0.125), reads=[spB], writes=[ptB])
        if dq >= 0:
            S.dve.op(lambda e: e.tensor_tensor(out=pt[:, c0:c0 + 128], in0=pt[:, c0:c0 + 128], in1=mk[:], op=ALU.mult),
                     reads=[ptB, mkB], writes=[ptB])
        if m == 1:
            r1, r1B = rs1d[par], rs1dB[par]
            if not blk["rs1_started"]:
                S.dve.op(lambda e: e.tensor_copy(out=r1[:, 0:nq], in_=pt[:, 0:nq]), reads=[ptB], writes=[r1B])
                blk["rs1_started"] = True
            else:
                S.dve.op(lambda e: e.tensor_tensor(out=r1[:, c0:nq], in0=r1[:, c0:nq], in1=pt[:, c0:nq], op=ALU.add),
                         reads=[ptB, r1B], writes=[r1B])
        pts[i] = (pt, ptB, c0)

    def emit_pv(i):
        blk, kt, m = steps[i]
        hh, nq, par, kt_last = blk["hh"], blk["nq"], blk["par"], blk["kt_last"]
        pt, ptB, c0 = pts.pop(i)
        S.pe.op(lambda e: e.matmul(accT[par][m][:, c0:nq], lhsT=vs[:, kt, hh, :], rhs=pt[:, c0:nq],
                                   start=(kt == 0), stop=(kt == kt_last)),
                reads=[ptB, vsB[kt]], writes=[accB[par][m]])
        if m == 0:
            S.pe.op(lambda e: e.matmul(rsacc[:, c0:nq], lhsT=C.ones[:], rhs=pt[:, c0:nq], start=(kt == 0), stop=(kt == kt_last)),
                    reads=[ptB, C.onesB], writes=[rsaccB])

    def fin_early(blk):
        nq = blk["nq"]
        rl, rlB = C.fr.next()
        S.act.op(lambda e: e.activation(out=rl[:, 0:nq], in_=rsacc[:, 0:nq], func=AF.Identity), reads=[rsaccB], writes=[rlB])
        blk["rl0"] = (rl, rlB)

    def fin_late(blk):
        hh, bi, q0, nq, par = blk["hh"], blk["bi"], blk["q0"], blk["nq"], blk["par"]
        r1, r1B = rs1d[par], rs1dB[par]
        tn = []
        for m in range(2):
            if m == 0:
                rl, rlB = blk["rl0"]
                S.dve.op(lambda e: e.reciprocal(out=rl[:, 0:nq], in_=rl[:, 0:nq]), reads=[rlB], writes=[rlB])
            else:
                rl, rlB = C.fr.next()
                hi, hiB = C.br.next()
                S.dve.op(lambda e: e.tensor_copy(out=hi[:, 0:nq], in_=r1[:, 0:nq]), reads=[r1B], writes=[hiB])
                lo2, lo2B = C.br.next()
                S.dve.op(lambda e: e.tensor_tensor(out=lo2[:, 0:nq], in0=r1[:, 0:nq], in1=hi[:, 0:nq], op=ALU.subtract),
                         reads=[r1B, hiB], writes=[lo2B])
                fin, finB = sps.next()
                S.pe.op(lambda e: e.matmul(fin[:, 0:nq], lhsT=C.ones[:], rhs=hi[:, 0:nq], start=True, stop=False), reads=[hiB, C.onesB], writes=[finB])
                S.pe.op(lambda e: e.matmul(fin[:, 0:nq], lhsT=C.ones[:], rhs=lo2[:, 0:nq], start=False, stop=True), reads=[lo2B, C.onesB], writes=[finB])
                S.dve.op(lambda e: e.reciprocal(out=rl[:, 0:nq], in_=fin[:, 0:nq]), reads=[finB], writes=[rlB])
            t_, tB = C.fr.next()
            S.dve.op(lambda e: e.tensor_tensor(out=t_[:, 0:nq], in0=accT[par][m][:, 0:nq], in1=rl[:, 0:nq], op=ALU.mult),
                     reads=[accB[par][m], rlB], writes=[tB])
            tn.append((t_, tB))
        (t0_, t0B), (t1_, t1B) = tn
        S.dve.op(lambda e: e.scalar_tensor_tensor(out=t0_[:, 0:nq], in0=t1_[:, 0:nq], scalar=nlam, in1=t0_[:, 0:nq], op0=ALU.mult, op1=ALU.add),
                 reads=[t0B, t1B, lB], writes=[t0B])
        sq, sqB = C.br.next()
        S.act.op(lambda e: e.activation(out=sq[:, 0:nq], in_=t0_[:, 0:nq], func=AF.Square), reads=[t0B], writes=[sqB])
        fin, finB = sps.next()
        S.pe.op(lambda e: e.matmul(fin[:, 0:nq], lhsT=C.ones[:], rhs=sq[:, 0:nq], start=True, stop=True), reads=[sqB, C.onesB], writes=[finB])
        S.act.op(lambda e: e.activation(out=t1_[:, 0:nq], in_=fin[:, 0:nq], func=AF.Sqrt, bias=C.v("eps"), scale=1.0 / 128),
                 reads=[finB, C.vB], writes=[t1B])
        S.dve.op(lambda e: e.reciprocal(out=t1_[:, 0:nq], in_=t1_[:, 0:nq]), reads=[t1B], writes=[t1B])
        o_t, o_tB = otb[par], otB[par]
        S.dve.op(lambda e: e.scalar_tensor_tensor(out=o_t[:, 0:nq], in0=t0_[:, 0:nq], scalar=gp[:, 0:1], in1=t1_[:, 0:nq], op0=ALU.mult, op1=ALU.mult),
                 reads=[t0B, t1B, gpB], writes=[o_tB])
        c = q0 // 2048
        S.sp.dma(oT_src[hh, c, :, q0 % 2048:q0 % 2048 + nq], o_t[:, 0:nq], reads=[o_tB], writes=[_nb(osrcB[hh][c])])
        if (q0 + nq) % 2048 == 0 or bi == len(QBLKS) - 1:
            S.all_gather_async(oT_src[hh, c], oT_all[hh, c], GROUPS, osrcB[hh][c], XB["o"][hh][c])

    NS = len(steps)
    late = {}
    for i in range(NS + LA + FDLY + 1):
        if i < NS:
            blk, kt, m = steps[i]
            if kt == 0 and m == 0 and (blk["idx"] - 2) in late:
                fin_late(late.pop(blk["idx"] - 2)[1])
            emit_score(i)
        j = i - LA
        if 0 <= j < NS:
            emit_pv(j)
            blk, kt, m = steps[j]
            if kt == blk["kt_last"] and m == 1:
                fin_early(blk)
                late[blk["idx"]] = (i + FDLY, blk)
        for k_ in [k_ for k_, (due, _) in late.items() if due <= i]:
            fin_late(late.pop(k_)[1])
    assert not late and not pts


def attn_out(C, sc, oT_all, w_o, hT, hTB, hn, hnB, XB):
    nc, S = C.nc, C.S
    wo = W1024(C, w_o)
    cid = C.cid
    hn4 = hn[:].rearrange("p (hp hh) t -> p hp hh t", hh=2)
    for hh in range(2):
        o5 = oT_all[hh].rearrange("c (hp p) t -> p hp c t", p=128)
        o5s = oT_all[hh, 1:5].rearrange("c (hp p) t -> p hp c t", p=128)
        S.sp.dma(hn4[:, :, hh, 0:2048].unsqueeze(2), o5[:, :, bass.ds(cid, 1), :], reads=XB["o"][hh],
                 writes=[hnB[hp_ * 2 + hh][ti] for hp_ in range(4) for ti in range(4)])
        S.sp.dma(hn4[:, :, hh, 2048:WIN].unsqueeze(2), o5s[:, :, bass.ds(cid, 1), 0:WIN - 2048], reads=XB["o"][hh],
                 writes=[hnB[hp_ * 2 + hh][4] for hp_ in range(4)])
    for ti, (t0, n) in enumerate(TILES):
        for oc in range(8):
            ps, psB = C.ps.next()
            for kc in range(8):
                l_, lB = wo.lhsT(kc, oc)
                S.pe.op(lambda e: e.matmul(ps[:, 0:n], lhsT=l_, rhs=hn[:, kc, t0:t0 + n],
                                           start=(kc == 0), stop=(kc == 7)), reads=[lB, hnB[kc][ti]], writes=[psB])
            S.dve.op(lambda e: e.tensor_tensor(out=hT[:, oc, t0:t0 + n], in0=ps[:, 0:n], in1=hT[:, oc, t0:t0 + n], op=ALU.add),
                     reads=[psB, hTB[oc][ti]], writes=[hTB[oc][ti]])


def final_norm(C, hT, hTB, out_o):
    S = C.S
    for ti, (t0, n) in enumerate(TILES):
        bl = [hTB[kc][ti] for kc in range(8)]
        rmsnorm_tile(C, lambda kc: hT[:, kc, t0:t0 + n], bl, "fin_g", n, lambda kc: hT[:, kc, t0:t0 + n], bl)
        for kc in range(8):
            S.sp.dma(fm(out_o)[:, kc, t0:t0 + n], hT[:, kc, t0:t0 + n], reads=[bl[kc]])


def build_fused():
    nc = bass.Bass("TRN2", target_bir_lowering=False)
    I = lambda name, shape, dt=F32: nc.dram_tensor(name, list(shape), dt, kind="ExternalInput").ap()
    T_ = lambda name, shape, dt=F32: nc.dram_tensor(name, list(shape), dt).ap()
    xw = I("xw", [D, 3 + WIN])
    vecs = I("vecs", [128, NV])
    ident = I("ident", [128, 128], BF16)
    masks = I("masks", [128, 4 * 512], BF16)
    w_in = I("w_in", [D, 2 * D])
    w_a = I("w_a", [D, 128])
    w_i = I("w_i", [D, 128])
    w_out = I("w_out", [D, D])
    w_up0 = I("w_up0", [D, 6 * D])
    w_dn0 = I("w_dn0", [3 * D, D])
    wA_d = I("wA_hp", [D, 512])
    wR_d = I("wR_hp", [D, 512])
    wV_d = I("wV_hp", [D, 256])
    cosF = I("cosF", [128, TSEQ])
    sinF = I("sinF", [128, TSEQ])
    w_o = I("w_o", [D, D])
    w_up1 = I("w_up1", [D, 6 * D])
    w_dn1 = I("w_dn1", [3 * D, D])
    out_o = nc.dram_tensor("out_o", [D, WIN], F32, kind="ExternalOutput").ap()
    gate_s = T_("gate_s", [D, WIN], BF16)
    hloc_s = T_("hloc_s", [D, WIN])
    pcum_s = T_("pcum_s", [D, WIN])
    ends_src = T_("ends_src", [128, 16])
    ends_all = T_("ends_all", [4 * 128, 16])
    hn_src = T_("hn_src", [4, D, 512], BF16)
    hn_all = T_("hn_all", [4, 4 * D, 512], BF16)
    hn_src4 = T_("hn_src4", [D, 16], BF16)
    hn_all4 = T_("hn_all4", [4 * D, 16], BF16)
    qk_sel = T_("qk_sel", [2 * D, WIN], BF16)
    v_sel = T_("v_sel", [2, 4 * WIN, 128], BF16)
    oT_src = T_("oT_src", [2, 5, 128, 2048], BF16)
    oT_all = T_("oT_all", [2, 5, 4 * 128, 2048], BF16)
    XB = {"hn": [Buf() for _ in range(5)], "o": [[Buf() for _ in range(5)] for _ in range(2)], "selqk": [[], []], "selv": []}
    with nc.allow_low_precision("bf16 matmul operands, fp32 accumulation"), Scope(nc) as top:
        C = Ctx(nc, top, vecs, ident)
        S = C.S
        with Scope(nc) as sc:
            C.phase(sc, n_w=5)
            lru_front(C, sc, xw, w_in, w_a, w_i, gate_s, hloc_s, pcum_s, ends_src)
            S.barrier()
        S.all_gather([(ends_src, ends_all)], GROUPS)
        hT = top.sb("hT", [128, 8, WIN], F32)
        hTB = [[Buf() for _ in TILES] for _ in range(8)]
        with Scope(nc) as sc:
            C.phase(sc, n_w=4)
            hn = sc.sb("hnw", [128, 8, WIN], BF16)
            hnB = [[Buf() for _ in TILES] for _ in range(8)]
            with Scope(nc) as s2:
                lru_back(C, s2, xw, gate_s, hloc_s, pcum_s, ends_all, w_out, hT, hTB)
                S.barrier()
            with Scope(nc) as s2:
                ffn_block(C, s2, hT, hTB, hn, hnB, w_up0, w_dn0, 0)
                S.barrier()
            with Scope(nc) as s2:
                qkv_block(C, s2, hT, hTB, hn, hnB, wA_d, wR_d, wV_d, cosF, sinF, hn_src, hn_src4, hn_all, hn_all4, qk_sel, v_sel, XB)
                S.barrier()
        with Scope(nc) as sc:
            C.phase(sc, n_w=0, n_ps=0)
            attention(C, sc, None, None, qk_sel, v_sel, masks, oT_src, oT_all, XB)
            S.barrier()
        with Scope(nc) as sc:
            C.phase(sc, n_w=4)
            hn = sc.sb("hnw", [128, 8, WIN], BF16)
            hnB = [[Buf() for _ in TILES] for _ in range(8)]
            attn_out(C, sc, oT_all, w_o, hT, hTB, hn, hnB, XB)
            ffn_block(C, sc, hT, hTB, hn, hnB, w_up1, w_dn1, 1)
            final_norm(C, hT, hTB, out_o)
            S.finish()
    return nc


_CACHE = {}


def _get(name, fn):
    if name not in _CACHE:
        _CACHE[name] = fn()
    return _CACHE[name]


def _rope_tables():
    inv = (1.0 / (10000.0 ** (np.arange(0, 64, 2, dtype=np.float32) / np.float32(64)))).astype(np.float32)
    ang = (np.arange(TSEQ, dtype=np.float32)[:, None] * inv[None, :]).astype(np.float32)
    ang = np.concatenate([ang, ang], -1)
    cos = np.cos(ang).astype(np.float32)
    sin = np.sin(ang).astype(np.float32)
    sgn = np.concatenate([-np.ones(32, np.float32), np.ones(32, np.float32)])
    sin = sin * sgn[None, :]
    cosT = np.concatenate([cos.T, cos.T], 0)
    sinT = np.concatenate([sin.T, sin.T], 0)
    return np.ascontiguousarray(cosT), np.ascontiguousarray(sinT)


def _masks():
    p = np.arange(128)[:, None]
    f = np.arange(512)[None, :]
    m = np.stack([((r * 128 + p) <= f) for r in range(4)], 1).astype(np.float32)
    return np.ascontiguousarray(m.reshape(128, 2048)).astype(ml_dtypes.bfloat16)


def kernel(**inp):
    inp = {k: np.asarray(v) for k, v in inp.items()}
    B = 2
    x = inp["x"]
    seq = [np.concatenate([inp["meta_tokens"], x[b]], 0) for b in range(B)]
    cores = [(b, c) for b in range(B) for c in range(4)]
    ids = list(range(NCORE))
    cosT, sinT = _rope_tables()
    wqkv = inp["attn_w_qkv"][0]
    perm = np.arange(2048).reshape(-1, 2, 32)[:, ::-1, :].reshape(-1)
    w_qk = np.ascontiguousarray(wqkv[:, 0:2048])
    shared = {
        "ident": np.eye(128, dtype=np.float32).astype(ml_dtypes.bfloat16),
        "masks": _masks(),
        "w_in": np.ascontiguousarray(inp["lru_w_in"][0]),
        "w_a": np.ascontiguousarray(inp["lru_w_a"][0].reshape(D, 128)),
        "w_i": np.ascontiguousarray(inp["lru_w_i"][0].reshape(D, 128)),
        "w_out": np.ascontiguousarray(inp["lru_w_out"][0]),
        "w_up0": np.ascontiguousarray(inp["ffn_w_up"][0]),
        "w_dn0": np.ascontiguousarray(inp["ffn_w_down"][0]),
        "cosF": cosT,
        "sinF": sinT,
        "w_o": np.ascontiguousarray(inp["attn_w_o"][0]),
        "w_up1": np.ascontiguousarray(inp["ffn_w_up"][1]),
        "w_dn1": np.ascontiguousarray(inp["ffn_w_down"][1]),
    }
    in_maps = []
    for (b, c) in cores:
        s0 = 2048 * c
        w = np.zeros((3 + WIN, D), np.float32)
        w[3:] = seq[b][s0:s0 + WIN]
        if c > 0:
            w[0:3] = seq[b][s0 - 3:s0]
        m = dict(shared)
        m["xw"] = np.ascontiguousarray(w.T)
        m["vecs"] = pack_vecs(inp, c)
        w_qkr = w_qk[:, perm]
        cols = np.concatenate([np.arange(wh * D + (2 * c + hh_) * 128, wh * D + (2 * c + hh_ + 1) * 128) for wh in range(2) for hh_ in range(2)])
        m["wA_hp"] = np.ascontiguousarray(w_qk[:, cols])
        m["wR_hp"] = np.ascontiguousarray(w_qkr[:, cols])
        m["wV_hp"] = np.ascontiguousarray(wqkv[:, 2048 + c * 256:2048 + (c + 1) * 256])
        in_maps.append(m)
    nc = _get("fused", build_fused)
    res = run_bass_kernel_spmd(nc, in_maps, core_ids=ids).results
    out = np.zeros((B, 8192, D), np.float32)
    for i, (b, c) in enumerate(cores):
        out[b, 2048 * c:2048 * (c + 1)] = res[i]["out_o"][:, HALO:].T
    return out
```

```python
import math
import numpy as np
import ml_dtypes
import concourse.bass as bass
import concourse.mybir as mybir
from concourse.bass_utils import run_bass_kernel_spmd
from contextlib import ExitStack

F32 = mybir.dt.float32
BF16 = mybir.dt.bfloat16
AF = mybir.ActivationFunctionType
ALU = mybir.AluOpType

D = 1024
WIN = 2064
HALO = 16
OWN = 2048
TSEQ = 8208
TP = 8704
NCORE = 8
LAMBDA_INIT = 0.8 - 0.6 * math.exp(-0.3 * 1)
TILES = [(0, 512), (512, 512), (1024, 512), (1536, 512), (2048, 16)]
SEM_ROT = 24000
WSLOT = 6144
CC_QOS = "P2"


class Buf:
    __slots__ = ("name", "w", "r")

    def __init__(self, name=""):
        self.name = name
        self.w = None
        self.r = {}


class Eng:
    def __init__(self, S, name, eng, same_sync=True):
        self.S = S
        self.name = name
        self.eng = eng
        self.same_sync = same_sync
        self.known = {}
        self.sem = None
        self.cnt = 0
        self._newsem()

    def _newsem(self):
        self.sem = self.S.nc.alloc_semaphore(f"s{self.name}{self.S.nsem}")
        self.S.nsem += 1
        self.cnt = 0

    def rotate(self):
        if self.cnt > 0:
            self.eng.wait_ge(self.sem, self.cnt)
            self.known[id(self.sem)] = self.cnt
        self.S.old.append((self.sem, self.cnt))
        self._newsem()

    def _collect(self, reads, writes):
        need = {}

        def add(st):
            if st is None:
                return
            sem, val, en = st
            if en == self.name and sem is self.sem:
                if not self.same_sync or val < self.cnt:
                    return
            if self.known.get(id(sem), 0) >= val:
                return
            cur = need.get(id(sem))
            if cur is None or cur[1] < val:
                need[id(sem)] = st

        for b in reads:
            add(b.w)
        for b in writes:
            add(b.w)
            for st in b.r.values():
                add(st)
        return need

    def _emit_waits(self, need):
        for sem, val, en in need.values():
            self.eng.wait_ge(sem, val)
            self.known[id(sem)] = val

    def op(self, fn, reads=(), writes=()):
        if self.cnt >= SEM_ROT:
            self.rotate()
        need = self._collect(reads, writes)
        self._emit_waits(need)
        inst = fn(self.eng)
        self.cnt += 1
        inst.then_inc(self.sem, 1)
        st = (self.sem, self.cnt, self.name)
        for b in reads:
            b.r[id(self.sem)] = st
        for b in writes:
            b.w = st
            b.r = {}
        return inst

    def wait_stamp(self, st):
        need = {}
        sem, val, en = st
        if self.known.get(id(sem), 0) < val:
            self.eng.wait_ge(sem, val)
            self.known[id(sem)] = val


class DmaQ:
    def __init__(self, S, name, eng, nsems=8):
        self.S = S
        self.name = name
        self.eng = eng
        self.known = {}
        self.sems = [S.nc.alloc_semaphore(f"d{name}{i}") for i in range(nsems)]
        self.vals = [0] * nsems
        self.i = 0
        self.same_sync = False
        self.sem = None
        self.cnt = 0

    def dma(self, out, in_, reads=(), writes=(), **kw):
        k = self.i
        self.i = (self.i + 1) % len(self.sems)
        sem = self.sems[k]
        need = Eng._collect(self, reads, writes)
        if self.vals[k] > 0 and self.known.get(id(sem), 0) < self.vals[k]:
            need[id(sem)] = (sem, self.vals[k], self.name)
        Eng._emit_waits(self, need)
        inst = self.eng.dma_start(out=out, in_=in_, **kw)
        self.vals[k] += 16
        inst.then_inc(sem, 16)
        st = (sem, self.vals[k], self.name)
        for b in reads:
            b.r[id(sem)] = st
        for b in writes:
            b.w = st
            b.r = {}
        return st


class Sched:
    def __init__(self, nc):
        self.nc = nc
        self.nsem = 0
        self.old = []
        self.pe = Eng(self, "pe", nc.tensor, same_sync=False)
        self.dve = Eng(self, "dve", nc.vector)
        self.act = Eng(self, "act", nc.scalar)
        self.pool = Eng(self, "pool", nc.gpsimd)
        self.sp = DmaQ(self, "sp", nc.sync, 12)
        self.gq = DmaQ(self, "gq", nc.gpsimd, 6)
        self.engs = [self.pe, self.dve, self.act, self.pool]

    def barrier(self):
        stamps = [(e.sem, e.cnt, e.name) for e in self.engs if e.cnt > 0]
        for q in (self.sp, self.gq):
            for k, sem in enumerate(q.sems):
                if q.vals[k] > 0:
                    stamps.append((sem, q.vals[k], q.name))
        if getattr(self, "ccnt", 0) > 0:
            stamps.append((self.csem, self.ccnt, "cc"))
        for e in self.engs + [self.sp, self.gq]:
            for sem, val, en in stamps:
                if e.known.get(id(sem), 0) < val:
                    e.eng.wait_ge(sem, val)
                    e.known[id(sem)] = val

    def finish(self):
        self.barrier()

    def all_gather_async(self, src, dst, groups, src_bufs, dst_buf):
        if not hasattr(self, "csem"):
            self.csem = self.nc.alloc_semaphore("ccsem")
            self.ccnt = 0
        need = Eng._collect(self.pool, src_bufs, [dst_buf])
        Eng._emit_waits(self.pool, need)
        self.nc.gpsimd.collective_compute("AllGather", ALU.bypass, replica_groups=groups, ins=[src.opt()], outs=[dst.opt()], dma_qos=CC_QOS).then_inc(self.csem)
        self.ccnt += 1
        st = (self.csem, self.ccnt, "cc")
        for b in src_bufs:
            b.r[id(self.csem)] = st
        dst_buf.w = st
        dst_buf.r = {}

    def all_gather(self, pairs, groups):
        self.barrier()
        if not hasattr(self, "csem"):
            self.csem = self.nc.alloc_semaphore("ccsem")
            self.ccnt = 0
        for src, dst in pairs:
            self.nc.gpsimd.collective_compute("AllGather", ALU.bypass, replica_groups=groups, ins=[src.opt()], outs=[dst.opt()]).then_inc(self.csem)
            self.ccnt += 1
        for e in self.engs + [self.sp, self.gq]:
            e.eng.wait_ge(self.csem, self.ccnt)


class Scope:
    _n = 0

    def __init__(self, nc):
        self.nc = nc
        self.stack = ExitStack()

    def __enter__(self):
        self.stack.__enter__()
        return self

    def __exit__(self, *a):
        return self.stack.__exit__(*a)

    def sb(self, name, shape, dtype):
        Scope._n += 1
        return self.stack.enter_context(self.nc.sbuf_tensor(f"{name}_{Scope._n}", list(shape), dtype))

    def psum(self, name, shape, dtype):
        Scope._n += 1
        return self.stack.enter_context(self.nc.psum_tensor(f"{name}_{Scope._n}", list(shape), dtype))


class Ring:
    def __init__(self, sc, name, shape, dtype, n, psum=False):
        alloc = sc.psum if psum else sc.sb
        self.t = [alloc(f"{name}{i}", list(shape), dtype) for i in range(n)]
        self.b = [Buf(f"{name}{i}") for i in range(n)]
        self.i = 0

    def next(self):
        k = self.i
        self.i = (self.i + 1) % len(self.t)
        return self.t[k], self.b[k]


_VEC_LAYOUT = [
    ("mix_g0", 8), ("mix_g1", 8), ("ffn_g0", 8), ("ffn_g1", 8), ("fin_g", 8),
    ("b_in", 16), ("cw", 32), ("cb", 8), ("b_a", 8), ("b_i", 8), ("L", 8), ("b_out", 8),
    ("fcw0", 144), ("fcb0", 48), ("fcw1", 144), ("fcb1", 48),
    ("lam", 256), ("subg", 128), ("eps", 1), ("cmask", 1), ("onehot", 4), ("one", 1), ("subgp", 1),
]
VO = {}
_o = 0
for _n, _c in _VEC_LAYOUT:
    VO[_n] = _o
    _o += _c
NV = _o


def _pk(v):
    return np.ascontiguousarray(v.reshape(-1, 128).T)


def pack_vecs(inp, c):
    V = np.zeros((128, NV), np.float32)

    def put(name, arr):
        V[:, VO[name]:VO[name] + arr.shape[1]] = arr

    put("mix_g0", _pk(inp["mix_norm_g"][0]))
    put("mix_g1", _pk(inp["mix_norm_g"][1]))
    put("ffn_g0", _pk(inp["ffn_norm_g"][0]))
    put("ffn_g1", _pk(inp["ffn_norm_g"][1]))
    put("fin_g", _pk(inp["final_norm_g"]))
    put("b_in", _pk(inp["lru_b_in"][0]))
    put("cw", np.concatenate([_pk(inp["lru_conv_w"][0][k]) for k in range(4)], 1))
    put("cb", _pk(inp["lru_conv_b"][0]))
    put("b_a", _pk(inp["lru_b_a"][0]))
    put("b_i", _pk(inp["lru_b_i"][0]))
    put("L", _pk(inp["lru_L"][0]))
    put("b_out", _pk(inp["lru_b_out"][0]))
    for l in range(2):
        put(f"fcw{l}", np.concatenate([_pk(inp["ffn_conv_w"][l][k]) for k in range(3)], 1))
        put(f"fcb{l}", _pk(inp["ffn_conv_b"][l]))
    lam = np.concatenate([inp["attn_lambda_q1"][0], inp["attn_lambda_k1"][0], inp["attn_lambda_q2"][0], inp["attn_lambda_k2"][0]])
    put("lam", np.broadcast_to(lam[None, :], (128, 256)))
    put("subg", np.broadcast_to(inp["attn_subln_g"][0][None, :], (128, 128)))
    V[:, VO["eps"]] = 1e-6
    V[:, VO["cmask"]] = 0.0 if c == 0 else 1.0
    V[:, VO["onehot"] + c] = 1.0
    V[:, VO["one"]] = 1.0
    V[:, VO["subgp"]] = inp["attn_subln_g"][0]
    return V


class Ctx:
    def __init__(self, nc, sc, vecs_dram, ident_dram, n_f32=8, n_bf=8):
        self.nc = nc
        self.S = Sched(nc)
        S = self.S
        self.ps = None
        self.wr = None
        self.fr = Ring(sc, "fr", [128, 516], F32, n_f32)
        self.br = Ring(sc, "br", [128, 512], BF16, n_bf)
        self.vecs = sc.sb("vecs_sb", [128, NV], F32)
        self.vB = Buf("vecs")
        S.sp.dma(self.vecs[:], vecs_dram, writes=[self.vB])
        self.ident = sc.sb("ident_sb", [128, 128], BF16)
        self.idB = Buf("ident")
        S.sp.dma(self.ident[:], ident_dram, writes=[self.idB])
        self.ones = sc.sb("ones", [128, 128], BF16)
        self.onesB = Buf("ones")
        S.pool.op(lambda e: e.memset(self.ones[:], 1.0), writes=[self.onesB])
        self.cid = nc.sync.partition_id() % 4

    def phase(self, sc, n_w=4, n_ps=8):
        self.ps = Ring(sc, "ps", [128, 512], F32, n_ps, psum=True) if n_ps else None
        self.wr = Ring(sc, "wr", [128, WSLOT], BF16, n_w) if n_w else None

    def v(self, name, i=0):
        o = VO[name] + i
        return self.vecs[:, o:o + 1]


def _nb(lst):
    b = Buf()
    lst.append(b)
    return b


def fm(dram):
    return dram.rearrange("(kc p) t -> p kc t", p=128)


def rmsnorm_tile(C, src, srcB, gname, n, dst, dstB, nfeat=1024.0):
    S = C.S
    ps, psB = C.ps.next()
    for kc in range(8):
        sq, sqB = C.br.next()
        S.act.op(lambda e: e.activation(out=sq[:, 0:n], in_=src(kc), func=AF.Square), reads=[srcB[kc]], writes=[sqB])
        S.pe.op(lambda e: e.matmul(ps[:, 0:n], lhsT=C.ones[:], rhs=sq[:, 0:n], start=(kc == 0), stop=(kc == 7)),
                reads=[sqB, C.onesB], writes=[psB])
    rs, rsB = C.fr.next()
    S.act.op(lambda e: e.activation(out=rs[:, 0:n], in_=ps[:, 0:n], func=AF.Sqrt, bias=C.v("eps"), scale=1.0 / nfeat),
             reads=[psB, C.vB], writes=[rsB])
    S.dve.op(lambda e: e.reciprocal(out=rs[:, 0:n], in_=rs[:, 0:n]), reads=[rsB], writes=[rsB])
    for kc in range(8):
        S.dve.op(lambda e: e.scalar_tensor_tensor(out=dst(kc), in0=src(kc), scalar=C.v(gname, kc), in1=rs[:, 0:n],
                                                  op0=ALU.mult, op1=ALU.mult),
                 reads=[srcB[kc], rsB, C.vB], writes=[dstB[kc]])


def load_w(C, dram_rows_ap, ncols, nk=8):
    assert nk * ncols <= WSLOT
    slot, sB = C.wr.next()
    view = slot[:, 0:nk * ncols].rearrange("p (kc n) -> p kc n", kc=nk)
    C.S.gq.dma(view, dram_rows_ap.rearrange("(kc p) n -> p kc n", p=128), writes=[sB])
    return slot, sB


class W1024:
    def __init__(self, C, dram):
        self.h = [load_w(C, dram[:, i * 512:(i + 1) * 512], 512) for i in range(2)]

    def lhsT(self, kc, oc):
        slot, sB = self.h[oc // 4]
        c0 = kc * 512 + (oc % 4) * 128
        return slot[:, c0:c0 + 128], sB


GROUPS = [[0, 1, 2, 3], [4, 5, 6, 7]]
TPA = 8320
TO = 10240
NKT = 65
QBLKS = [(512 * i, 512) for i in range(16)] + [(8192, 128)]


def lru_front(C, sc, xw, w_in, w_a, w_i, gate_o, hloc_o, pcum_o, ends_src):
    nc, S = C.nc, C.S
    wg = W1024(C, w_in[:, 0:D])
    wrc = W1024(C, w_in[:, D:2 * D])
    wai, waiB = C.wr.next()
    S.gq.dma(wai[:, 0:1024].rearrange("p (h j) -> p h j", h=8), w_a.rearrange("(h p) j -> p h j", p=128), writes=[waiB])
    S.gq.dma(wai[:, 1024:2048].rearrange("p (h j) -> p h j", h=8), w_i.rearrange("(h p) j -> p h j", p=128), writes=[waiB])
    c8 = sc.sb("c8", [128, 8], F32)
    c16 = sc.sb("c16", [128, 8], F32)
    c8B = Buf("c8")
    Lap = C.vecs[:, VO["L"]:VO["L"] + 8]
    S.act.op(lambda e: e.activation(out=c8[:], in_=Lap, func=AF.Sigmoid), reads=[C.vB], writes=[c8B])
    S.act.op(lambda e: e.activation(out=c8[:], in_=c8[:], func=AF.Ln), reads=[c8B], writes=[c8B])
    S.dve.op(lambda e: e.tensor_scalar(out=c16[:], in0=c8[:], scalar1=16.0, scalar2=None, op0=ALU.mult), reads=[c8B], writes=[c8B])
    S.dve.op(lambda e: e.tensor_scalar(out=c8[:], in0=c8[:], scalar1=8.0, scalar2=None, op0=ALU.mult), reads=[c8B], writes=[c8B])
    h8 = sc.sb("h8", [128, 8], F32)
    hb = sc.sb("hb", [128, 16], F32)
    qtr = sc.sb("qtr", [128, 1], F32)
    S.dve.op(lambda e: e.tensor_scalar(out=h8[:], in0=c8[:], scalar1=0.5, scalar2=None, op0=ALU.mult), reads=[c8B], writes=[c8B])
    S.dve.op(lambda e: e.tensor_scalar(out=hb[:, 0:8], in0=C.vecs[:, VO["b_a"]:VO["b_a"] + 8], scalar1=0.5, scalar2=None, op0=ALU.mult), reads=[C.vB, c8B], writes=[c8B])
    S.dve.op(lambda e: e.tensor_scalar(out=hb[:, 8:16], in0=C.vecs[:, VO["b_i"]:VO["b_i"] + 8], scalar1=0.5, scalar2=None, op0=ALU.mult), reads=[C.vB, c8B], writes=[c8B])
    S.dve.op(lambda e: e.memset(qtr[:], 0.25), writes=[c8B])

    xt = sc.sb("xt", [128, 8, 512], F32)
    xtB = [Buf() for _ in range(8)]
    hn = [sc.sb(f"hn{i}", [128, 8, 512], BF16) for i in range(2)]
    hnB = [[Buf() for _ in range(8)] for _ in range(2)]
    rec = [sc.sb(f"rec{i}", [128, 8, 516], F32) for i in range(2)]
    recB = [[Buf() for _ in range(8)] for _ in range(2)]
    gt = sc.sb("gt", [128, 8, 512], BF16)
    gtB = [Buf() for _ in range(8)]
    hl = sc.sb("hl", [128, 8, 512], F32)
    hlB = [Buf() for _ in range(8)]
    pc = sc.sb("pc", [128, 8, 512], F32)
    pcB = [Buf() for _ in range(8)]
    zeros = sc.sb("zeros", [128, 512], F32)
    zB = Buf("zeros")
    S.pool.op(lambda e: e.memset(zeros[:], 0.0), writes=[zB])
    st = sc.sb("st", [128, 16], F32)
    stB = [Buf() for _ in range(8)]
    xwf = fm(xw)
    cvr = Ring(sc, "cvr", [128, 516], F32, 4)

    def norm_in(par, c0, n):
        for kc in range(8):
            S.sp.dma(xt[:, kc, 0:n], xwf[:, kc, c0:c0 + n], writes=[xtB[kc]])
        rmsnorm_tile(C, lambda kc: xt[:, kc, 0:n], xtB, "mix_g0", n,
                     lambda kc: hn[par][:, kc, 0:n], hnB[par])

    def mm8(W, par, oc, n):
        ps, psB = C.ps.next()
        for kc in range(8):
            l_, lB = W.lhsT(kc, oc)
            S.pe.op(lambda e: e.matmul(ps[:, 0:n], lhsT=l_, rhs=hn[par][:, kc, 0:n], start=(kc == 0), stop=(kc == 7)),
                    reads=[lB, hnB[par][kc]], writes=[psB])
        return ps, psB

    norm_in(1, 0, 3)
    for oc in range(8):
        ps, psB = mm8(wrc, 1, oc, 3)
        S.dve.op(lambda e: e.tensor_scalar(out=rec[0][:, oc, 0:3], in0=ps[:, 0:3], scalar1=C.v("b_in", 8 + oc), scalar2=C.v("cmask"),
                                           op0=ALU.add, op1=ALU.mult), reads=[psB, C.vB], writes=[recB[0][oc]])

    for ti, (t0, n) in enumerate(TILES):
        par = ti % 2
        norm_in(par, 3 + t0, n)
        for oc in range(8):
            ps, psB = mm8(wg, par, oc, n)
            S.act.op(lambda e: e.activation(out=gt[:, oc, 0:n], in_=ps[:, 0:n], func=AF.Gelu_apprx_tanh, bias=C.v("b_in", oc)),
                     reads=[psB, C.vB], writes=[gtB[oc]])
            S.gq.dma(fm(gate_o)[:, oc, t0:t0 + n], gt[:, oc, 0:n], reads=[gtB[oc]])
        stg = {}

        def stage1(oc):
            ps, psB = mm8(wrc, par, oc, n)
            S.dve.op(lambda e: e.tensor_scalar(out=rec[par][:, oc, 3:3 + n], in0=ps[:, 0:n], scalar1=C.v("b_in", 8 + oc), scalar2=None, op0=ALU.add),
                     reads=[psB, C.vB], writes=[recB[par][oc]])
            if ti + 1 < len(TILES):
                S.pool.op(lambda e: e.tensor_copy(out=rec[1 - par][:, oc, 0:3], in_=rec[par][:, oc, n:n + 3]),
                          reads=[recB[par][oc]], writes=[recB[1 - par][oc]])
            cv, cvB = cvr.next()
            S.dve.op(lambda e: e.tensor_scalar(out=cv[:, 0:n], in0=rec[par][:, oc, 3:3 + n], scalar1=C.v("cw", 3 * 8 + oc), scalar2=C.v("cb", oc),
                                               op0=ALU.mult, op1=ALU.add), reads=[recB[par][oc], C.vB], writes=[cvB])
            for k in range(3):
                S.dve.op(lambda e: e.scalar_tensor_tensor(out=cv[:, 0:n], in0=rec[par][:, oc, k:k + n], scalar=C.v("cw", k * 8 + oc), in1=cv[:, 0:n],
                                                          op0=ALU.mult, op1=ALU.add), reads=[recB[par][oc], cvB, C.vB], writes=[cvB])
            cvb, cvbB = C.br.next()
            S.dve.op(lambda e: e.tensor_copy(out=cvb[:, 0:n], in_=cv[:, 0:n]), reads=[cvB], writes=[cvbB])
            psa, psaB = C.ps.next()
            S.pe.op(lambda e: e.matmul(psa[:, 0:n], lhsT=wai[:, oc * 128:(oc + 1) * 128], rhs=cvb[:, 0:n], start=True, stop=True),
                    reads=[waiB, cvbB], writes=[psaB])
            psi, psiB = C.ps.next()
            S.pe.op(lambda e: e.matmul(psi[:, 0:n], lhsT=wai[:, 1024 + oc * 128:1024 + (oc + 1) * 128], rhs=cvb[:, 0:n], start=True, stop=True),
                    reads=[waiB, cvbB], writes=[psiB])
            stg[oc] = (cv, cvB, psa, psaB, psi, psiB)

        def stage2(oc):
            cv, cvB, psa, psaB, psi, psiB = stg.pop(oc)
            r_, rB = C.fr.next()
            S.act.op(lambda e: e.activation(out=r_[:, 0:n], in_=psa[:, 0:n], func=AF.Tanh, bias=hb[:, oc:oc + 1], scale=0.5), reads=[psaB, c8B], writes=[rB])
            gi, giB = C.fr.next()
            S.act.op(lambda e: e.activation(out=gi[:, 0:n], in_=psi[:, 0:n], func=AF.Tanh, bias=hb[:, 8 + oc:9 + oc], scale=0.5), reads=[psiB, c8B], writes=[giB])
            a_, aB = C.fr.next()
            S.act.op(lambda e: e.activation(out=a_[:, 0:n], in_=r_[:, 0:n], func=AF.Exp, scale=h8[:, oc:oc + 1], bias=h8[:, oc:oc + 1]), reads=[rB, c8B], writes=[aB])
            S.act.op(lambda e: e.activation(out=r_[:, 0:n], in_=a_[:, 0:n], func=AF.Square), reads=[aB, rB], writes=[rB])
            S.act.op(lambda e: e.activation(out=r_[:, 0:n], in_=r_[:, 0:n], func=AF.Sqrt, bias=qtr[:, 0:1], scale=-0.25), reads=[rB, c8B], writes=[rB])
            S.dve.op(lambda e: e.scalar_tensor_tensor(out=gi[:, 0:n], in0=gi[:, 0:n], scalar=1.0, in1=cv[:, 0:n], op0=ALU.add, op1=ALU.mult),
                     reads=[giB, cvB], writes=[giB])
            S.dve.op(lambda e: e.tensor_tensor(out=gi[:, 0:n], in0=gi[:, 0:n], in1=r_[:, 0:n], op=ALU.mult), reads=[giB, rB], writes=[giB])
            if ti == 0:
                ih, ip = 0.0, 1.0
                rd = []
            else:
                ip = st[:, oc:oc + 1]
                ih = st[:, 8 + oc:9 + oc]
                rd = [stB[oc]]
            S.dve.op(lambda e: e.tensor_tensor_scan(out=hl[:, oc, 0:n], data0=a_[:, 0:n], data1=gi[:, 0:n], initial=ih, op0=ALU.mult, op1=ALU.add),
                     reads=[aB, giB] + rd, writes=[hlB[oc]])
            S.dve.op(lambda e: e.tensor_tensor_scan(out=pc[:, oc, 0:n], data0=a_[:, 0:n], data1=zeros[:, 0:n], initial=ip, op0=ALU.mult, op1=ALU.add),
                     reads=[aB, zB] + rd, writes=[pcB[oc]])
            if ti + 1 < len(TILES):
                S.pool.op(lambda e: e.tensor_copy(out=st[:, 8 + oc:9 + oc], in_=hl[:, oc, n - 1:n]), reads=[hlB[oc]], writes=[stB[oc]])
                S.pool.op(lambda e: e.tensor_copy(out=st[:, oc:oc + 1], in_=pc[:, oc, n - 1:n]), reads=[pcB[oc]], writes=[stB[oc]])
            S.gq.dma(fm(hloc_o)[:, oc, t0:t0 + n], hl[:, oc, 0:n], reads=[hlB[oc]])
            S.gq.dma(fm(pcum_o)[:, oc, t0:t0 + n], pc[:, oc, 0:n], reads=[pcB[oc]])

        stage1(0)
        stage1(1)
        for oc in range(8):
            if oc + 2 < 8:
                stage1(oc + 2)
            stage2(oc)
    S.sp.dma(ends_src, st[:], reads=stB)


NG = 8


def fm_w(w):
    return w.rearrange("(kc p) n -> p kc n", p=128)


def ffn_block(C, sc, hT, hTB, hn, hnB, w_up, w_dn, layer):
    nc, S = C.nc, C.S
    gname = f"ffn_g{layer}"
    for ti, (t0, n) in enumerate(TILES):
        rmsnorm_tile(C, lambda kc: hT[:, kc, t0:t0 + n], [hTB[kc][ti] for kc in range(8)], gname, n,
                     lambda kc: hn[:, kc, t0:t0 + n], [hnB[kc][ti] for kc in range(8)])
    tails = [sc.sb(f"tl{layer}_{i}", [128, 6, 2], F32) for i in range(2)]
    tlB = [[Buf() for _ in range(6)] for _ in range(2)]
    act = [sc.sb(f"act{layer}_{i}", [128, 3, 512], BF16) for i in range(2)]
    actB = [[Buf() for _ in range(3)] for _ in range(2)]
    fcw, fcb = f"fcw{layer}", f"fcb{layer}"
    step = 0
    pending = None

    def emit_down(wd, wdB, ap_, ti, t0, n):
        for oc in range(8):
            ps, psB = C.ps.next()
            for j in range(3):
                S.pe.op(lambda e: e.matmul(ps[:, 0:n], lhsT=wd[:, j * D + oc * 128: j * D + (oc + 1) * 128], rhs=act[ap_][:, j, 0:n],
                                           start=(j == 0), stop=(j == 2)), reads=[wdB, actB[ap_][j]], writes=[psB])
            S.dve.op(lambda e: e.tensor_tensor(out=hT[:, oc, t0:t0 + n], in0=ps[:, 0:n], in1=hT[:, oc, t0:t0 + n], op=ALU.add),
                     reads=[psB, hTB[oc][ti]], writes=[hTB[oc][ti]])

    def load_group(g):
        slot, sB = C.wr.next()
        view = slot[:, 0:8 * 768].rearrange("p (kc n) -> p kc n", kc=8)
        S.gq.dma(view[:, :, 0:384], fm_w(w_up)[:, :, g * 384:(g + 1) * 384], writes=[sB])
        S.gq.dma(view[:, :, 384:768], fm_w(w_up)[:, :, 3072 + g * 384:3072 + (g + 1) * 384], writes=[sB])
        wd, wdB = load_w(C, w_dn[g * 384:(g + 1) * 384, :], D, nk=3)
        return slot, sB, wd, wdB

    nxt = load_group(0)
    for g in range(NG):
        slot, sB, wd, wdB = nxt
        for ti, (t0, n) in enumerate(TILES):
            if ti == 1 and g + 1 < NG:
                nxt = load_group(g + 1)
            ap_ = step % 2
            step += 1
            tp = ti % 2
            for j in range(3):
                outs = []
                for half in range(2):
                    fc = half * 24 + g * 3 + j
                    ci = half * 3 + j
                    ps, psB = C.ps.next()
                    for kc in range(8):
                        c0 = kc * 768 + half * 384 + j * 128
                        S.pe.op(lambda e: e.matmul(ps[:, 0:n], lhsT=slot[:, c0:c0 + 128], rhs=hn[:, kc, t0:t0 + n], start=(kc == 0), stop=(kc == 7)),
                                reads=[sB, hnB[kc][ti]], writes=[psB])
                    raw, rawB = C.fr.next()
                    S.act.op(lambda e: e.activation(out=raw[:, 2:2 + n], in_=ps[:, 0:n], func=AF.Identity), reads=[psB], writes=[rawB])
                    if ti == 0:
                        S.pool.op(lambda e: e.memset(raw[:, 0:2], 0.0), writes=[rawB])
                    else:
                        S.pool.op(lambda e: e.tensor_copy(out=raw[:, 0:2], in_=tails[1 - tp][:, ci, :]), reads=[tlB[1 - tp][ci]], writes=[rawB])
                    if ti + 1 < len(TILES):
                        S.pool.op(lambda e: e.tensor_copy(out=tails[tp][:, ci, :], in_=raw[:, n:n + 2]), reads=[rawB], writes=[tlB[tp][ci]])
                    cv, cvB = C.fr.next()
                    S.act.op(lambda e: e.activation(out=cv[:, 0:n], in_=ps[:, 0:n], func=AF.Identity, bias=C.v(fcb, fc), scale=C.v(fcw, 2 * 48 + fc)),
                             reads=[psB, C.vB], writes=[cvB])
                    for k in range(2):
                        S.dve.op(lambda e: e.scalar_tensor_tensor(out=cv[:, 0:n], in0=raw[:, k:k + n], scalar=C.v(fcw, k * 48 + fc), in1=cv[:, 0:n],
                                                                  op0=ALU.mult, op1=ALU.add), reads=[rawB, cvB, C.vB], writes=[cvB])
                    outs.append((cv, cvB))
                (gv, gvB), (vv, vvB) = outs
                S.act.op(lambda e: e.activation(out=gv[:, 0:n], in_=gv[:, 0:n], func=AF.Gelu_apprx_tanh), reads=[gvB], writes=[gvB])
                S.dve.op(lambda e: e.tensor_tensor(out=act[ap_][:, j, 0:n], in0=gv[:, 0:n], in1=vv[:, 0:n], op=ALU.mult),
                         reads=[gvB, vvB], writes=[actB[ap_][j]])
            if pending is not None:
                emit_down(*pending)
            pending = (wd, wdB, ap_, ti, t0, n)
    emit_down(*pending)


def lru_back(C, sc, xw, gate_i, hloc_i, pcum_i, ends_all, w_out, hT, hTB):
    nc, S = C.nc, C.S
    wo = W1024(C, w_out)
    en = sc.sb("ends_sb", [128, 4, 2, 8], F32)
    enB = Buf("ends")
    S.sp.dma(en[:].rearrange("p j a k -> p j (a k)"), ends_all.rearrange("(j p) f -> p j f", p=128), writes=[enB])
    cr = sc.sb("carry", [128, 8], F32)
    cj = sc.sb("cj", [128, 8], F32)
    crB = Buf("carry")
    S.dve.op(lambda e: e.memset(cr[:], 0.0), writes=[crB])
    S.dve.op(lambda e: e.memset(cj[:], 0.0), writes=[crB])
    for j in range(3):
        S.dve.op(lambda e: e.tensor_tensor(out=cj[:], in0=cj[:], in1=en[:, j, 0, :], op=ALU.mult), reads=[crB, enB], writes=[crB])
        S.dve.op(lambda e: e.tensor_tensor(out=cj[:], in0=cj[:], in1=en[:, j, 1, :], op=ALU.add), reads=[crB, enB], writes=[crB])
        S.dve.op(lambda e: e.scalar_tensor_tensor(out=cr[:], in0=cj[:], scalar=C.v("onehot", j + 1), in1=cr[:], op0=ALU.mult, op1=ALU.add),
                 reads=[crB, C.vB], writes=[crB])
    yb = sc.sb("yb", [128, 8, 512], BF16)
    ybB = [Buf() for _ in range(8)]
    xwf = fm(xw)
    for ti, (t0, n) in enumerate(TILES):
        for kc in range(8):
            hl, hlB = C.fr.next()
            S.sp.dma(hl[:, 0:n], fm(hloc_i)[:, kc, t0:t0 + n], writes=[hlB])
            pcm, pcmB = C.fr.next()
            S.sp.dma(pcm[:, 0:n], fm(pcum_i)[:, kc, t0:t0 + n], writes=[pcmB])
            gt, gtB = C.br.next()
            S.sp.dma(gt[:, 0:n], fm(gate_i)[:, kc, t0:t0 + n], writes=[gtB])
            S.dve.op(lambda e: e.scalar_tensor_tensor(out=hl[:, 0:n], in0=pcm[:, 0:n], scalar=cr[:, kc:kc + 1], in1=hl[:, 0:n], op0=ALU.mult, op1=ALU.add),
                     reads=[pcmB, hlB, crB], writes=[hlB])
            S.dve.op(lambda e: e.tensor_tensor(out=yb[:, kc, 0:n], in0=hl[:, 0:n], in1=gt[:, 0:n], op=ALU.mult),
                     reads=[hlB, gtB], writes=[ybB[kc]])
        for oc in range(8):
            xs, xsB = C.fr.next()
            S.sp.dma(xs[:, 0:n], xwf[:, oc, 3 + t0:3 + t0 + n], writes=[xsB])
            ps, psB = C.ps.next()
            for kc in range(8):
                l_, lB = wo.lhsT(kc, oc)
                S.pe.op(lambda e: e.matmul(ps[:, 0:n], lhsT=l_, rhs=yb[:, kc, 0:n], start=(kc == 0), stop=(kc == 7)),
                        reads=[lB, ybB[kc]], writes=[psB])
            S.dve.op(lambda e: e.scalar_tensor_tensor(out=hT[:, oc, t0:t0 + n], in0=ps[:, 0:n], scalar=C.v("b_out", oc), in1=xs[:, 0:n],
                                                      op0=ALU.add, op1=ALU.add), reads=[psB, xsB, C.vB], writes=[hTB[oc][ti]])


def qkv_block(C, sc, hT, hTB, hn, hnB, wA_d, wR_d, wV_d, cosF, sinF, hn_src, hn_src4, hn_all, hn_all4, qk_sel, v_sel, XB):
    nc, S = C.nc, C.S
    wA, wAB = load_w(C, wA_d, 512)
    wR, wRB = load_w(C, wR_d, 512)
    wV, wVB = load_w(C, wV_d, 256)
    for ti, (t0, n) in enumerate(TILES):
        bl = [hnB[kc][ti] for kc in range(8)]
        rmsnorm_tile(C, lambda kc: hT[:, kc, t0:t0 + n], [hTB[kc][ti] for kc in range(8)], "mix_g1", n,
                     lambda kc: hn[:, kc, t0:t0 + n], bl)
        src = hn_src[ti] if n == 512 else hn_src4
        dst = hn_all[ti] if n == 512 else hn_all4
        stB = []
        S.sp.dma(src.rearrange("(kc p) t -> p kc t", p=128), hn[:, :, t0:t0 + n], reads=bl, writes=[_nb(stB)])
        S.all_gather_async(src, dst, GROUPS, stB, XB["hn"][ti])
    NXH = 4
    xh = [hn[:, :, i * 512:(i + 1) * 512] for i in range(NXH)]
    xhB = [Buf() for _ in range(NXH)]
    xh_guard = [[hnB[kc][i] for kc in range(8)] for i in range(NXH)]
    csr = [sc.sb(f"csr{i}", [128, 512], F32) for i in range(2)]
    snr = [sc.sb(f"snr{i}", [128, 512], F32) for i in range(2)]
    csrB = [Buf() for _ in range(2)]
    snrB = [Buf() for _ in range(2)]
    NVB = 6
    vb = [sc.sb(f"vb{i}", [128, 256], BF16) for i in range(NVB)]
    vbB = [Buf() for _ in range(NVB)]
    vsel_r = v_sel.rearrange("h r v -> r h v")
    vi = 0
    plan = []
    for ti, (t0, n) in enumerate(TILES):
        for j in range(4):
            lo = HALO if (j > 0 and ti == 0) else 0
            plan.append((ti, t0, n, j, lo))

    def issue_loads(k):
        ti, t0, n, j, lo = plan[k]
        all_t = hn_all[ti] if n == 512 else hn_all4
        m_ = n - lo
        s0 = 2048 * j + t0 + lo
        S.sp.dma(xh[k % NXH][:, :, 0:m_], all_t[j * 1024:(j + 1) * 1024, lo:n].rearrange("(kc p) t -> p kc t", p=128), reads=[XB["hn"][ti]],
                 writes=[xhB[k % NXH]] + xh_guard[k % NXH])
        S.sp.dma(csr[k % 2][:, 0:m_], cosF[:, s0:s0 + m_], writes=[csrB[k % 2]])
        S.sp.dma(snr[k % 2][:, 0:m_], sinF[:, s0:s0 + m_], writes=[snrB[k % 2]])

    issue_loads(0)
    for step, (ti, t0, n, j, lo) in enumerate(plan):
        if True:
            m_ = n - lo
            c0 = t0 + lo
            x_, xB = xh[step % NXH], xhB[step % NXH]
            cs, csB, sn, snB = csr[step % 2], csrB[step % 2], snr[step % 2], snrB[step % 2]
            if step + 1 < len(plan):
                issue_loads(step + 1)
            for c4 in range(4):
                psA, psAB = C.ps.next()
                psR, psRB = C.ps.next()
                for (w_, wB_, p_, pB_) in ((wA, wAB, psA, psAB), (wR, wRB, psR, psRB)):
                    for kc in range(8):
                        S.pe.op(lambda e: e.matmul(p_[:, 0:m_], lhsT=w_[:, kc * 512 + c4 * 128: kc * 512 + (c4 + 1) * 128], rhs=x_[:, kc, 0:m_],
                                                   start=(kc == 0), stop=(kc == 7)), reads=[wB_, xB], writes=[pB_])
                t1, t1B = C.fr.next()
                S.dve.op(lambda e: e.tensor_tensor(out=t1[:, 0:m_], in0=psA[:, 0:m_], in1=cs[:, 0:m_], op=ALU.mult), reads=[psAB, csB], writes=[t1B])
                t2, t2B = C.fr.next()
                S.dve.op(lambda e: e.tensor_tensor(out=t2[:, 0:m_], in0=psR[:, 0:m_], in1=sn[:, 0:m_], op=ALU.mult), reads=[psRB, snB], writes=[t2B])
                ob, obB = C.br.next()
                S.dve.op(lambda e: e.tensor_tensor(out=ob[:, 0:m_], in0=t1[:, 0:m_], in1=t2[:, 0:m_], op=ALU.add), reads=[t1B, t2B], writes=[obB])
                r0 = c4 * 512 + j * 128
                S.sp.dma(qk_sel[r0:r0 + 128, c0:c0 + m_], ob[:, 0:m_], reads=[obB], writes=[_nb(XB["selqk"][c4 // 2])])
            for u0 in range(0, m_, 128):
                um = min(128, m_ - u0)
                ps, psB = C.ps.next()
                for kc in range(8):
                    S.pe.op(lambda e: e.matmul(ps[0:um, 0:256], lhsT=x_[:, kc, u0:u0 + um], rhs=wV[:, kc * 256:(kc + 1) * 256],
                                               start=(kc == 0), stop=(kc == 7)), reads=[wVB, xB], writes=[psB])
                v_, vB_ = vb[vi % NVB], vbB[vi % NVB]
                vi += 1
                S.act.op(lambda e: e.activation(out=v_[0:um, :], in_=ps[0:um, 0:256], func=AF.Identity), reads=[psB], writes=[vB_])
                rr = j * WIN + c0 + u0
                S.sp.dma(vsel_r[rr:rr + um, :, :], v_[0:um, :].rearrange("r (h v) -> r h v", h=2), reads=[vB_], writes=[_nb(XB["selv"])])


def attention(C, sc, qk_all, v_all, qk_sel, v_sel, masks, oT_src, oT_all, XB):
    nc, S = C.nc, C.S
    hp = C.cid
    q = sc.sb("q", [128, 2 * TPA], BF16)
    kz = [sc.sb(f"kz{m}", [128, TPA], BF16) for m in range(2)]
    vs = sc.sb("vs", [128, NKT, 2, 128], BF16)
    mk = sc.sb("mk", [128, 128], BF16)
    mkB = Buf("mk")
    qBs = [[Buf() for _ in range(5)] for _ in range(2)]
    kzB = [[Buf() for _ in range(5)] for _ in range(2)]
    vsB = [Buf() for _ in range(NKT)]
    S.pool.op(lambda e: e.memset(vs[:, 64, :, :].rearrange("p b c -> p (b c)"), 0.0), writes=[vsB[64]])
    for hh in range(2):
        S.dve.op(lambda e: e.memset(q[:, hh * TPA + TSEQ:(hh + 1) * TPA], 0.0), writes=[qBs[hh][4]])
    for m in range(2):
        S.pool.op(lambda e: e.memset(kz[m][:], 0.0), writes=kzB[m])
    selB_q, selB_k, selB_v = XB["selqk"][0], XB["selqk"][1], XB["selv"]
    def owner(s_):
        return 4 if s_ >= TSEQ else (0 if s_ < WIN else 1 + (s_ - WIN) // 2048)

    def owners(s_lo, s_hi):
        return sorted({owner(s_lo), owner(s_hi - 1)} | ({owner(x) for x in (WIN, WIN + 2048, WIN + 4096, TSEQ) if s_lo < x < s_hi}))

    def load_q(hh, j):
        lo = 0 if j == 0 else HALO
        r0 = hh * 512 + j * 128
        S.sp.dma(q[:, hh * TPA + 2048 * j + lo: hh * TPA + 2048 * j + WIN], qk_sel[r0:r0 + 128, lo:WIN], reads=selB_q, writes=[qBs[hh][j]])

    def load_kj(hh, j):
        lo = 0 if j == 0 else HALO
        r0 = 1024 + hh * 512 + j * 128
        for m in range(2):
            S.sp.dma(kz[m][m * 64:(m + 1) * 64, 2048 * j + lo:2048 * j + WIN], qk_sel[r0 + m * 64:r0 + (m + 1) * 64, lo:WIN], reads=selB_k, writes=[kzB[m][j]])

    def load_k(hh):
        for j in range(4):
            load_kj(hh, j)

    vsel_r = v_sel.rearrange("h r v -> r h v")

    def load_v(t):
        for p0, p1 in ((0, 16), (16, 128)) if t in (16, 32, 48, 64) else ((0, 128),):
            s_ = t * 128 + p0
            if s_ >= TSEQ:
                continue
            j = 0 if s_ < WIN else 1 + (s_ - WIN) // 2048
            r0 = s_ + 16 * j
            S.sp.dma(vs[p0:p1, t, :, :], vsel_r[r0:r0 + (p1 - p0), :, :], reads=selB_v, writes=[vsB[t]])

    for j in range(4):
        load_q(0, j)
        load_kj(0, j)
        for t in range(16 * j + (1 if j else 0), 16 * (j + 1) + 1):
            load_v(t)
    for j in range(4):
        load_q(1, j)
    S.sp.dma(mk[:], masks[:, 0:128], writes=[mkB])
    lt = sc.sb("lt", [128, 128], F32)
    ls = sc.sb("ls", [128, 4], F32)
    lB = Buf("lam")
    lo_ = VO["lam"]
    S.dve.op(lambda e: e.tensor_tensor(out=lt[:, 0:64], in0=C.vecs[:, lo_:lo_ + 64], in1=C.vecs[:, lo_ + 64:lo_ + 128], op=ALU.mult), reads=[C.vB], writes=[lB])
    S.dve.op(lambda e: e.tensor_tensor(out=lt[:, 64:128], in0=C.vecs[:, lo_ + 128:lo_ + 192], in1=C.vecs[:, lo_ + 192:lo_ + 256], op=ALU.mult), reads=[C.vB, lB], writes=[lB])
    S.dve.op(lambda e: e.tensor_reduce(out=ls[:, 0:1], in_=lt[:, 0:64], axis=mybir.AxisListType.X, op=ALU.add), reads=[lB], writes=[lB])
    S.dve.op(lambda e: e.tensor_reduce(out=ls[:, 1:2], in_=lt[:, 64:128], axis=mybir.AxisListType.X, op=ALU.add), reads=[lB], writes=[lB])
    S.act.op(lambda e: e.activation(out=ls[:, 0:2], in_=ls[:, 0:2], func=AF.Exp), reads=[lB], writes=[lB])
    S.dve.op(lambda e: e.tensor_tensor(out=ls[:, 2:3], in0=ls[:, 1:2], in1=ls[:, 0:1], op=ALU.subtract), reads=[lB], writes=[lB])
    S.dve.op(lambda e: e.tensor_scalar(out=ls[:, 3:4], in0=ls[:, 2:3], scalar1=float(-LAMBDA_INIT), scalar2=None, op0=ALU.add), reads=[lB], writes=[lB])
    nlam = ls[:, 3:4]
    gp = sc.sb("gp", [128, 1], F32)
    gpB = Buf("gp")
    S.dve.op(lambda e: e.tensor_scalar(out=gp[:], in0=C.v("subgp"), scalar1=float(1.0 - LAMBDA_INIT), scalar2=None, op0=ALU.mult),
             reads=[C.vB], writes=[gpB])

    accT = [[sc.psum(f"accT{i}_{m}", [128, 512], F32) for m in range(2)] for i in range(2)]
    accB = [[Buf() for m in range(2)] for i in range(2)]
    sps = Ring(sc, "sps", [128, 512], F32, 3, psum=True)
    rsacc = sc.psum("rsacc", [128, 512], F32)
    rsaccB = Buf("rsacc")
    rs1 = [sc.sb(f"rs1_{i}", [128, 512], F32) for i in range(2)]
    rs1B = [Buf() for _ in range(2)]
    otb = [sc.sb(f"aot{i}", [128, 512], BF16) for i in range(2)]
    otB = [Buf() for _ in range(2)]
    osrcB = [[[] for _ in range(5)] for _ in range(2)]
    LA = 2
    FDLY = 10
    rs1d = [rs1[0], rs1[1]]
    rs1dB = [rs1B[0], rs1B[1]]
    blocks = []
    for hh in range(2):
        for bi, (q0, nq) in enumerate(QBLKS):
            nqs = nq // 128
            qt0 = q0 // 128
            kt_last = min(NKT - 1, qt0 + nqs - 1)
            blocks.append(dict(idx=len(blocks), hh=hh, bi=bi, q0=q0, nq=nq, qt0=qt0, kt_last=kt_last, par=len(blocks) % 2, rs1_started=False))
    steps = [(blk, kt, m) for blk in blocks for kt in range(blk["kt_last"] + 1) for m in range(2)]
    pts = {}
    loaded_k = {0}

    def emit_score(i):
        blk, kt, m = steps[i]
        hh, q0, nq, qt0, par = blk["hh"], blk["q0"], blk["nq"], blk["qt0"], blk["par"]
        if hh not in loaded_k:
            load_k(hh)
            loaded_k.add(hh)
        dq = kt - qt0
        c0 = max(dq, 0) * 128
        sp_, spB = sps.next()
        S.pe.op(lambda e: e.matmul(sp_[:, c0:nq], lhsT=kz[m][:, kt * 128:(kt + 1) * 128],
                                   rhs=q[:, hh * TPA + q0 + c0: hh * TPA + q0 + nq], start=True, stop=True),
                reads=[kzB[m][c_] for c_ in owners(kt * 128, kt * 128 + 128)] + [qBs[hh][c_] for c_ in owners(q0 + c0, q0 + nq)], writes=[spB])
        pt, ptB = C.br.next()
        S.act.op(lambda e: e.activation(out=pt[:, c0:nq], in_=sp_[:, c0:nq], func=AF.Exp, scale=0.125), reads=[spB], writes=[ptB])
        if dq >= 0:
            S.dve.op(lambda e: e.tensor_tensor(out=pt[:, c0:c0 + 128], in0=pt[:, c0:c0 + 128], in1=mk[:], op=ALU.mult),
                     reads=[ptB, mkB], writes=[ptB])
        if m == 1:
            r1, r1B = rs1d[par], rs1dB[par]
            if not blk["rs1_started"]:
                S.dve.op(lambda e: e.tensor_copy(out=r1[:, 0:nq], in_=pt[:, 0:nq]), reads=[ptB], writes=[r1B])
                blk["rs1_started"] = True
            else:
                S.dve.op(lambda e: e.tensor_tensor(out=r1[:, c0:nq], in0=r1[:, c0:nq], in1=pt[:, c0:nq], op=ALU.add),
                         reads=[ptB, r1B], writes=[r1B])
        pts[i] = (pt, ptB, c0)

    def emit_pv(i):
        blk, kt, m = steps[i]
        hh, nq, par, kt_last = blk["hh"], blk["nq"], blk["par"], blk["kt_last"]
        pt, ptB, c0 = pts.pop(i)
        S.pe.op(lambda e: e.matmul(accT[par][m][:, c0:nq], lhsT=vs[:, kt, hh, :], rhs=pt[:, c0:nq],
                                   start=(kt == 0), stop=(kt == kt_last)),
                reads=[ptB, vsB[kt]], writes=[accB[par][m]])
        if m == 0:
            S.pe.op(lambda e: e.matmul(rsacc[:, c0:nq], lhsT=C.ones[:], rhs=pt[:, c0:nq], start=(kt == 0), stop=(kt == kt_last)),
                    reads=[ptB, C.onesB], writes=[rsaccB])

    def fin_early(blk):
        nq = blk["nq"]
        rl, rlB = C.fr.next()
        S.act.op(lambda e: e.activation(out=rl[:, 0:nq], in_=rsacc[:, 0:nq], func=AF.Identity), reads=[rsaccB], writes=[rlB])
        blk["rl0"] = (rl, rlB)

    def fin_late(blk):
        hh, bi, q0, nq, par = blk["hh"], blk["bi"], blk["q0"], blk["nq"], blk["par"]
        r1, r1B = rs1d[par], rs1dB[par]
        tn = []
        for m in range(2):
            if m == 0:
                rl, rlB = blk["rl0"]
                S.dve.op(lambda e: e.reciprocal(out=rl[:, 0:nq], in_=rl[:, 0:nq]), reads=[rlB], writes=[rlB])
            else:
                rl, rlB = C.fr.next()
                hi, hiB = C.br.next()
                S.dve.op(lambda e: e.tensor_copy(out=hi[:, 0:nq], in_=r1[:, 0:nq]), reads=[r1B], writes=[hiB])
                lo2, lo2B = C.br.next()
                S.dve.op(lambda e: e.tensor_tensor(out=lo2[:, 0:nq], in0=r1[:, 0:nq], in1=hi[:, 0:nq], op=ALU.subtract),
                         reads=[r1B, hiB], writes=[lo2B])
                fin, finB = sps.next()
                S.pe.op(lambda e: e.matmul(fin[:, 0:nq], lhsT=C.ones[:], rhs=hi[:, 0:nq], start=True, stop=False), reads=[hiB, C.onesB], writes=[finB])
                S.pe.op(lambda e: e.matmul(fin[:, 0:nq], lhsT=C.ones[:], rhs=lo2[:, 0:nq], start=False, stop=True), reads=[lo2B, C.onesB], writes=[finB])
                S.dve.op(lambda e: e.reciprocal(out=rl[:, 0:nq], in_=fin[:, 0:nq]), reads=[finB], writes=[rlB])
            t_, tB = C.fr.next()
            S.dve.op(lambda e: e.tensor_tensor(out=t_[:, 0:nq], in0=accT[par][m][:, 0:nq], in1=rl[:, 0:nq], op=ALU.mult),
                     reads=[accB[par][m], rlB], writes=[tB])
            tn.append((t_, tB))
        (t0_, t0B), (t1_, t1B) = tn
        S.dve.op(lambda e: e.scalar_tensor_tensor(out=t0_[:, 0:nq], in0=t1_[:, 0:nq], scalar=nlam, in1=t0_[:, 0:nq], op0=ALU.mult, op1=ALU.add),
                 reads=[t0B, t1B, lB], writes=[t0B])
        sq, sqB = C.br.next()
        S.act.op(lambda e: e.activation(out=sq[:, 0:nq], in_=t0_[:, 0:nq], func=AF.Square), reads=[t0B], writes=[sqB])
        fin, finB = sps.next()
        S.pe.op(lambda e: e.matmul(fin[:, 0:nq], lhsT=C.ones[:], rhs=sq[:, 0:nq], start=True, stop=True), reads=[sqB, C.onesB], writes=[finB])
        S.act.op(lambda e: e.activation(out=t1_[:, 0:nq], in_=fin[:, 0:nq], func=AF.Sqrt, bias=C.v("eps"), scale=1.0 / 128),
                 reads=[finB, C.vB], writes=[t1B])
        S.dve.op(lambda e: e.reciprocal(out=t1_[:, 0:nq], in_=t1_[:, 0:nq]), reads=[t1B], writes=[t1B])
        o_t, o_tB = otb[par], otB[par]
        S.dve.op(lambda e: e.scalar_tensor_tensor(out=o_t[:, 0:nq], in0=t0_[:, 0:nq], scalar=gp[:, 0:1], in1=t1_[:, 0:nq], op0=ALU.mult, op1=ALU.mult),
                 reads=[t0B, t1B, gpB], writes=[o_tB])
        c = q0 // 2048
        S.sp.dma(oT_src[hh, c, :, q0 % 2048:q0 % 2048 + nq], o_t[:, 0:nq], reads=[o_tB], writes=[_nb(osrcB[hh][c])])
        if (q0 + nq) % 2048 == 0 or bi == len(QBLKS) - 1:
            S.all_gather_async(oT_src[hh, c], oT_all[hh, c], GROUPS, osrcB[hh][c], XB["o"][hh][c])

    NS = len(steps)
    late = {}
    for i in range(NS + LA + FDLY + 1):
        if i < NS:
            blk, kt, m = steps[i]
            if kt == 0 and m == 0 and (blk["idx"] - 2) in late:
                fin_late(late.pop(blk["idx"] - 2)[1])
            emit_score(i)
        j = i - LA
        if 0 <= j < NS:
            emit_pv(j)
            blk, kt, m = steps[j]
            if kt == blk["kt_last"] and m == 1:
                fin_early(blk)
                late[blk["idx"]] = (i + FDLY, blk)
        for k_ in [k_ for k_, (due, _) in late.items() if due <= i]:
            fin_late(late.pop(k_)[1])
    assert not late and not pts


def attn_out(C, sc, oT_all, w_o, hT, hTB, hn, hnB, XB):
    nc, S = C.nc, C.S
    wo = W1024(C, w_o)
    cid = C.cid
    hn4 = hn[:].rearrange("p (hp hh) t -> p hp hh t", hh=2)
    for hh in range(2):
        o5 = oT_all[hh].rearrange("c (hp p) t -> p hp c t", p=128)
        o5s = oT_all[hh, 1:5].rearrange("c (hp p) t -> p hp c t", p=128)
        S.sp.dma(hn4[:, :, hh, 0:2048].unsqueeze(2), o5[:, :, bass.ds(cid, 1), :], reads=XB["o"][hh],
                 writes=[hnB[hp_ * 2 + hh][ti] for hp_ in range(4) for ti in range(4)])
        S.sp.dma(hn4[:, :, hh, 2048:WIN].unsqueeze(2), o5s[:, :, bass.ds(cid, 1), 0:WIN - 2048], reads=XB["o"][hh],
                 writes=[hnB[hp_ * 2 + hh][4] for hp_ in range(4)])
    for ti, (t0, n) in enumerate(TILES):
        for oc in range(8):
            ps, psB = C.ps.next()
            for kc in range(8):
                l_, lB = wo.lhsT(kc, oc)
                S.pe.op(lambda e: e.matmul(ps[:, 0:n], lhsT=l_, rhs=hn[:, kc, t0:t0 + n],
                                           start=(kc == 0), stop=(kc == 7)), reads=[lB, hnB[kc][ti]], writes=[psB])
            S.dve.op(lambda e: e.tensor_tensor(out=hT[:, oc, t0:t0 + n], in0=ps[:, 0:n], in1=hT[:, oc, t0:t0 + n], op=ALU.add),
                     reads=[psB, hTB[oc][ti]], writes=[hTB[oc][ti]])


def final_norm(C, hT, hTB, out_o):
    S = C.S
    for ti, (t0, n) in enumerate(TILES):
        bl = [hTB[kc][ti] for kc in range(8)]
        rmsnorm_tile(C, lambda kc: hT[:, kc, t0:t0 + n], bl, "fin_g", n, lambda kc: hT[:, kc, t0:t0 + n], bl)
        for kc in range(8):
            S.sp.dma(fm(out_o)[:, kc, t0:t0 + n], hT[:, kc, t0:t0 + n], reads=[bl[kc]])


def build_fused():
    nc = bass.Bass("TRN2", target_bir_lowering=False)
    I = lambda name, shape, dt=F32: nc.dram_tensor(name, list(shape), dt, kind="ExternalInput").ap()
    T_ = lambda name, shape, dt=F32: nc.dram_tensor(name, list(shape), dt).ap()
    xw = I("xw", [D, 3 + WIN])
    vecs = I("vecs", [128, NV])
    ident = I("ident", [128, 128], BF16)
    masks = I("masks", [128, 4 * 512], BF16)
    w_in = I("w_in", [D, 2 * D])
    w_a = I("w_a", [D, 128])
    w_i = I("w_i", [D, 128])
    w_out = I("w_out", [D, D])
    w_up0 = I("w_up0", [D, 6 * D])
    w_dn0 = I("w_dn0", [3 * D, D])
    wA_d = I("wA_hp", [D, 512])
    wR_d = I("wR_hp", [D, 512])
    wV_d = I("wV_hp", [D, 256])
    cosF = I("cosF", [128, TSEQ])
    sinF = I("sinF", [128, TSEQ])
    w_o = I("w_o", [D, D])
    w_up1 = I("w_up1", [D, 6 * D])
    w_dn1 = I("w_dn1", [3 * D, D])
    out_o = nc.dram_tensor("out_o", [D, WIN], F32, kind="ExternalOutput").ap()
    gate_s = T_("gate_s", [D, WIN], BF16)
    hloc_s = T_("hloc_s", [D, WIN])
    pcum_s = T_("pcum_s", [D, WIN])
    ends_src = T_("ends_src", [128, 16])
    ends_all = T_("ends_all", [4 * 128, 16])
    hn_src = T_("hn_src", [4, D, 512], BF16)
    hn_all = T_("hn_all", [4, 4 * D, 512], BF16)
    hn_src4 = T_("hn_src4", [D, 16], BF16)
    hn_all4 = T_("hn_all4", [4 * D, 16], BF16)
    qk_sel = T_("qk_sel", [2 * D, WIN], BF16)
    v_sel = T_("v_sel", [2, 4 * WIN, 128], BF16)
    oT_src = T_("oT_src", [2, 5, 128, 2048], BF16)
    oT_all = T_("oT_all", [2, 5, 4 * 128, 2048], BF16)
    XB = {"hn": [Buf() for _ in range(5)], "o": [[Buf() for _ in range(5)] for _ in range(2)], "selqk": [[], []], "selv": []}
    with nc.allow_low_precision("bf16 matmul operands, fp32 accumulation"), Scope(nc) as top:
        C = Ctx(nc, top, vecs, ident)
        S = C.S
        with Scope(nc) as sc:
            C.phase(sc, n_w=5)
            lru_front(C, sc, xw, w_in, w_a, w_i, gate_s, hloc_s, pcum_s, ends_src)
            S.barrier()
        S.all_gather([(ends_src, ends_all)], GROUPS)
        hT = top.sb("hT", [128, 8, WIN], F32)
        hTB = [[Buf() for _ in TILES] for _ in range(8)]
        with Scope(nc) as sc:
            C.phase(sc, n_w=4)
            hn = sc.sb("hnw", [128, 8, WIN], BF16)
            hnB = [[Buf() for _ in TILES] for _ in range(8)]
            with Scope(nc) as s2:
                lru_back(C, s2, xw, gate_s, hloc_s, pcum_s, ends_all, w_out, hT, hTB)
                S.barrier()
            with Scope(nc) as s2:
                ffn_block(C, s2, hT, hTB, hn, hnB, w_up0, w_dn0, 0)
                S.barrier()
            with Scope(nc) as s2:
                qkv_block(C, s2, hT, hTB, hn, hnB, wA_d, wR_d, wV_d, cosF, sinF, hn_src, hn_src4, hn_all, hn_all4, qk_sel, v_sel, XB)
                S.barrier()
        with Scope(nc) as sc:
            C.phase(sc, n_w=0, n_ps=0)
            attention(C, sc, None, None, qk_sel, v_sel, masks, oT_src, oT_all, XB)
            S.barrier()
        with Scope(nc) as sc:
            C.phase(sc, n_w=4)
            hn = sc.sb("hnw", [128, 8, WIN], BF16)
            hnB = [[Buf() for _ in TILES] for _ in range(8)]
            attn_out(C, sc, oT_all, w_o, hT, hTB, hn, hnB, XB)
            ffn_block(C, sc, hT, hTB, hn, hnB, w_up1, w_dn1, 1)
            final_norm(C, hT, hTB, out_o)
            S.finish()
    return nc


_CACHE = {}


def _get(name, fn):
    if name not in _CACHE:
        _CACHE[name] = fn()
    return _CACHE[name]


def _rope_tables():
    inv = (1.0 / (10000.0 ** (np.arange(0, 64, 2, dtype=np.float32) / np.float32(64)))).astype(np.float32)
    ang = (np.arange(TSEQ, dtype=np.float32)[:, None] * inv[None, :]).astype(np.float32)
    ang = np.concatenate([ang, ang], -1)
    cos = np.cos(ang).astype(np.float32)
    sin = np.sin(ang).astype(np.float32)
    sgn = np.concatenate([-np.ones(32, np.float32), np.ones(32, np.float32)])
    sin = sin * sgn[None, :]
    cosT = np.concatenate([cos.T, cos.T], 0)
    sinT = np.concatenate([sin.T, sin.T], 0)
    return np.ascontiguousarray(cosT), np.ascontiguousarray(sinT)


def _masks():
    p = np.arange(128)[:, None]
    f = np.arange(512)[None, :]
    m = np.stack([((r * 128 + p) <= f) for r in range(4)], 1).astype(np.float32)
    return np.ascontiguousarray(m.reshape(128, 2048)).astype(ml_dtypes.bfloat16)


def kernel(**inp):
    inp = {k: np.asarray(v) for k, v in inp.items()}
    B = 2
    x = inp["x"]
    seq = [np.concatenate([inp["meta_tokens"], x[b]], 0) for b in range(B)]
    cores = [(b, c) for b in range(B) for c in range(4)]
    ids = list(range(NCORE))
    cosT, sinT = _rope_tables()
    wqkv = inp["attn_w_qkv"][0]
    perm = np.arange(2048).reshape(-1, 2, 32)[:, ::-1, :].reshape(-1)
    w_qk = np.ascontiguousarray(wqkv[:, 0:2048])
    shared = {
        "ident": np.eye(128, dtype=np.float32).astype(ml_dtypes.bfloat16),
        "masks": _masks(),
        "w_in": np.ascontiguousarray(inp["lru_w_in"][0]),
        "w_a": np.ascontiguousarray(inp["lru_w_a"][0].reshape(D, 128)),
        "w_i": np.ascontiguousarray(inp["lru_w_i"][0].reshape(D, 128)),
        "w_out": np.ascontiguousarray(inp["lru_w_out"][0]),
        "w_up0": np.ascontiguousarray(inp["ffn_w_up"][0]),
        "w_dn0": np.ascontiguousarray(inp["ffn_w_down"][0]),
        "cosF": cosT,
        "sinF": sinT,
        "w_o": np.ascontiguousarray(inp["attn_w_o"][0]),
        "w_up1": np.ascontiguousarray(inp["ffn_w_up"][1]),
        "w_dn1": np.ascontiguousarray(inp["ffn_w_down"][1]),
    }
    in_maps = []
    for (b, c) in cores:
        s0 = 2048 * c
        w = np.zeros((3 + WIN, D), np.float32)
        w[3:] = seq[b][s0:s0 + WIN]
        if c > 0:
            w[0:3] = seq[b][s0 - 3:s0]
        m = dict(shared)
        m["xw"] = np.ascontiguousarray(w.T)
        m["vecs"] = pack_vecs(inp, c)
        w_qkr = w_qk[:, perm]
        cols = np.concatenate([np.arange(wh * D + (2 * c + hh_) * 128, wh * D + (2 * c + hh_ + 1) * 128) for wh in range(2) for hh_ in range(2)])
        m["wA_hp"] = np.ascontiguousarray(w_qk[:, cols])
        m["wR_hp"] = np.ascontiguousarray(w_qkr[:, cols])
        m["wV_hp"] = np.ascontiguousarray(wqkv[:, 2048 + c * 256:2048 + (c + 1) * 256])
        in_maps.append(m)
    nc = _get("fused", build_fused)
    res = run_bass_kernel_spmd(nc, in_maps, core_ids=ids).results
    out = np.zeros((B, 8192, D), np.float32)
    for i, (b, c) in enumerate(cores):
        out[b, 2048 * c:2048 * (c + 1)] = res[i]["out_o"][:, HALO:].T
    return out
```

```python
import math
import numpy as np
import ml_dtypes
import concourse.bass as bass
import concourse.mybir as mybir
from concourse.bass_utils import run_bass_kernel_spmd
from contextlib import ExitStack

F32 = mybir.dt.float32
BF16 = mybir.dt.bfloat16
AF = mybir.ActivationFunctionType
ALU = mybir.AluOpType

D = 1024
WIN = 2064
HALO = 16
OWN = 2048
TSEQ = 8208
TP = 8704
NCORE = 8
LAMBDA_INIT = 0.8 - 0.6 * math.exp(-0.3 * 1)
TILES = [(0, 512), (512, 512), (1024, 512), (1536, 512), (2048, 16)]
SEM_ROT = 24000
WSLOT = 6144
CC_QOS = "P2"


class Buf:
    __slots__ = ("name", "w", "r")

    def __init__(self, name=""):
        self.name = name
        self.w = None
        self.r = {}


class Eng:
    def __init__(self, S, name, eng, same_sync=True):
        self.S = S
        self.name = name
        self.eng = eng
        self.same_sync = same_sync
        self.known = {}
        self.sem = None
        self.cnt = 0
        self._newsem()

    def _newsem(self):
        self.sem = self.S.nc.alloc_semaphore(f"s{self.name}{self.S.nsem}")
        self.S.nsem += 1
        self.cnt = 0

    def rotate(self):
        if self.cnt > 0:
            self.eng.wait_ge(self.sem, self.cnt)
            self.known[id(self.sem)] = self.cnt
        self.S.old.append((self.sem, self.cnt))
        self._newsem()

    def _collect(self, reads, writes):
        need = {}

        def add(st):
            if st is None:
                return
            sem, val, en = st
            if en == self.name and sem is self.sem:
                if not self.same_sync or val < self.cnt:
                    return
            if self.known.get(id(sem), 0) >= val:
                return
            cur = need.get(id(sem))
            if cur is None or cur[1] < val:
                need[id(sem)] = st

        for b in reads:
            add(b.w)
        for b in writes:
            add(b.w)
            for st in b.r.values():
                add(st)
        return need

    def _emit_waits(self, need):
        for sem, val, en in need.values():
            self.eng.wait_ge(sem, val)
            self.known[id(sem)] = val

    def op(self, fn, reads=(), writes=()):
        if self.cnt >= SEM_ROT:
            self.rotate()
        need = self._collect(reads, writes)
        self._emit_waits(need)
        inst = fn(self.eng)
        self.cnt += 1
        inst.then_inc(self.sem, 1)
        st = (self.sem, self.cnt, self.name)
        for b in reads:
            b.r[id(self.sem)] = st
        for b in writes:
            b.w = st
            b.r = {}
        return inst

    def wait_stamp(self, st):
        need = {}
        sem, val, en = st
        if self.known.get(id(sem), 0) < val:
            self.eng.wait_ge(sem, val)
            self.known[id(sem)] = val


class DmaQ:
    def __init__(self, S, name, eng, nsems=8):
        self.S = S
        self.name = name
        self.eng = eng
        self.known = {}
        self.sems = [S.nc.alloc_semaphore(f"d{name}{i}") for i in range(nsems)]
        self.vals = [0] * nsems
        self.i = 0
        self.same_sync = False
        self.sem = None
        self.cnt = 0

    def dma(self, out, in_, reads=(), writes=(), **kw):
        k = self.i
        self.i = (self.i + 1) % len(self.sems)
        sem = self.sems[k]
        need = Eng._collect(self, reads, writes)
        if self.vals[k] > 0 and self.known.get(id(sem), 0) < self.vals[k]:
            need[id(sem)] = (sem, self.vals[k], self.name)
        Eng._emit_waits(self, need)
        inst = self.eng.dma_start(out=out, in_=in_, **kw)
        self.vals[k] += 16
        inst.then_inc(sem, 16)
        st = (sem, self.vals[k], self.name)
        for b in reads:
            b.r[id(sem)] = st
        for b in writes:
            b.w = st
            b.r = {}
        return st


class Sched:
    def __init__(self, nc):
        self.nc = nc
        self.nsem = 0
        self.old = []
        self.pe = Eng(self, "pe", nc.tensor, same_sync=False)
        self.dve = Eng(self, "dve", nc.vector)
        self.act = Eng(self, "act", nc.scalar)
        self.pool = Eng(self, "pool", nc.gpsimd)
        self.sp = DmaQ(self, "sp", nc.sync, 12)
        self.gq = DmaQ(self, "gq", nc.gpsimd, 6)
        self.engs = [self.pe, self.dve, self.act, self.pool]

    def barrier(self):
        stamps = [(e.sem, e.cnt, e.name) for e in self.engs if e.cnt > 0]
        for q in (self.sp, self.gq):
            for k, sem in enumerate(q.sems):
                if q.vals[k] > 0:
                    stamps.append((sem, q.vals[k], q.name))
        if getattr(self, "ccnt", 0) > 0:
            stamps.append((self.csem, self.ccnt, "cc"))
        for e in self.engs + [self.sp, self.gq]:
            for sem, val, en in stamps:
                if e.known.get(id(sem), 0) < val:
                    e.eng.wait_ge(sem, val)
                    e.known[id(sem)] = val

    def finish(self):
        self.barrier()

    def all_gather_async(self, src, dst, groups, src_bufs, dst_buf):
        if not hasattr(self, "csem"):
            self.csem = self.nc.alloc_semaphore("ccsem")
            self.ccnt = 0
        need = Eng._collect(self.pool, src_bufs, [dst_buf])
        Eng._emit_waits(self.pool, need)
        self.nc.gpsimd.collective_compute("AllGather", ALU.bypass, replica_groups=groups, ins=[src.opt()], outs=[dst.opt()], dma_qos=CC_QOS).then_inc(self.csem)
        self.ccnt += 1
        st = (self.csem, self.ccnt, "cc")
        for b in src_bufs:
            b.r[id(self.csem)] = st
        dst_buf.w = st
        dst_buf.r = {}

    def all_gather(self, pairs, groups):
        self.barrier()
        if not hasattr(self, "csem"):
            self.csem = self.nc.alloc_semaphore("ccsem")
            self.ccnt = 0
        for src, dst in pairs:
            self.nc.gpsimd.collective_compute("AllGather", ALU.bypass, replica_groups=groups, ins=[src.opt()], outs=[dst.opt()]).then_inc(self.csem)
            self.ccnt += 1
        for e in self.engs + [self.sp, self.gq]:
            e.eng.wait_ge(self.csem, self.ccnt)


class Scope:
    _n = 0

    def __init__(self, nc):
        self.nc = nc
        self.stack = ExitStack()

    def __enter__(self):
        self.stack.__enter__()
        return self

    def __exit__(self, *a):
        return self.stack.__exit__(*a)

    def sb(self, name, shape, dtype):
        Scope._n += 1
        return self.stack.enter_context(self.nc.sbuf_tensor(f"{name}_{Scope._n}", list(shape), dtype))

    def psum(self, name, shape, dtype):
        Scope._n += 1
        return self.stack.enter_context(self.nc.psum_tensor(f"{name}_{Scope._n}", list(shape), dtype))


class Ring:
    def __init__(self, sc, name, shape, dtype, n, psum=False):
        alloc = sc.psum if psum else sc.sb
        self.t = [alloc(f"{name}{i}", list(shape), dtype) for i in range(n)]
        self.b = [Buf(f"{name}{i}") for i in range(n)]
        self.i = 0

    def next(self):
        k = self.i
        self.i = (self.i + 1) % len(self.t)
        return self.t[k], self.b[k]


_VEC_LAYOUT = [
    ("mix_g0", 8), ("mix_g1", 8), ("ffn_g0", 8), ("ffn_g1", 8), ("fin_g", 8),
    ("b_in", 16), ("cw", 32), ("cb", 8), ("b_a", 8), ("b_i", 8), ("L", 8), ("b_out", 8),
    ("fcw0", 144), ("fcb0", 48), ("fcw1", 144), ("fcb1", 48),
    ("lam", 256), ("subg", 128), ("eps", 1), ("cmask", 1), ("onehot", 4), ("one", 1), ("subgp", 1),
]
VO = {}
_o = 0
for _n, _c in _VEC_LAYOUT:
    VO[_n] = _o
    _o += _c
NV = _o


def _pk(v):
    return np.ascontiguousarray(v.reshape(-1, 128).T)


def pack_vecs(inp, c):
    V = np.zeros((128, NV), np.float32)

    def put(name, arr):
        V[:, VO[name]:VO[name] + arr.shape[1]] = arr

    put("mix_g0", _pk(inp["mix_norm_g"][0]))
    put("mix_g1", _pk(inp["mix_norm_g"][1]))
    put("ffn_g0", _pk(inp["ffn_norm_g"][0]))
    put("ffn_g1", _pk(inp["ffn_norm_g"][1]))
    put("fin_g", _pk(inp["final_norm_g"]))
    put("b_in", _pk(inp["lru_b_in"][0]))
    put("cw", np.concatenate([_pk(inp["lru_conv_w"][0][k]) for k in range(4)], 1))
    put("cb", _pk(inp["lru_conv_b"][0]))
    put("b_a", _pk(inp["lru_b_a"][0]))
    put("b_i", _pk(inp["lru_b_i"][0]))
    put("L", _pk(inp["lru_L"][0]))
    put("b_out", _pk(inp["lru_b_out"][0]))
    for l in range(2):
        put(f"fcw{l}", np.concatenate([_pk(inp["ffn_conv_w"][l][k]) for k in range(3)], 1))
        put(f"fcb{l}", _pk(inp["ffn_conv_b"][l]))
    lam = np.concatenate([inp["attn_lambda_q1"][0], inp["attn_lambda_k1"][0], inp["attn_lambda_q2"][0], inp["attn_lambda_k2"][0]])
    put("lam", np.broadcast_to(lam[None, :], (128, 256)))
    put("subg", np.broadcast_to(inp["attn_subln_g"][0][None, :], (128, 128)))
    V[:, VO["eps"]] = 1e-6
    V[:, VO["cmask"]] = 0.0 if c == 0 else 1.0
    V[:, VO["onehot"] + c] = 1.0
    V[:, VO["one"]] = 1.0
    V[:, VO["subgp"]] = inp["attn_subln_g"][0]
    return V


class Ctx:
    def __init__(self, nc, sc, vecs_dram, ident_dram, n_f32=8, n_bf=8):
        self.nc = nc
        self.S = Sched(nc)
        S = self.S
        self.ps = None
        self.wr = None
        self.fr = Ring(sc, "fr", [128, 516], F32, n_f32)
        self.br = Ring(sc, "br", [128, 512], BF16, n_bf)
        self.vecs = sc.sb("vecs_sb", [128, NV], F32)
        self.vB = Buf("vecs")
        S.sp.dma(self.vecs[:], vecs_dram, writes=[self.vB])
        self.ident = sc.sb("ident_sb", [128, 128], BF16)
        self.idB = Buf("ident")
        S.sp.dma(self.ident[:], ident_dram, writes=[self.idB])
        self.ones = sc.sb("ones", [128, 128], BF16)
        self.onesB = Buf("ones")
        S.pool.op(lambda e: e.memset(self.ones[:], 1.0), writes=[self.onesB])
        self.cid = nc.sync.partition_id() % 4

    def phase(self, sc, n_w=4, n_ps=8):
        self.ps = Ring(sc, "ps", [128, 512], F32, n_ps, psum=True) if n_ps else None
        self.wr = Ring(sc, "wr", [128, WSLOT], BF16, n_w) if n_w else None

    def v(self, name, i=0):
        o = VO[name] + i
        return self.vecs[:, o:o + 1]


def _nb(lst):
    b = Buf()
    lst.append(b)
    return b


def fm(dram):
    return dram.rearrange("(kc p) t -> p kc t", p=128)


def rmsnorm_tile(C, src, srcB, gname, n, dst, dstB, nfeat=1024.0):
    S = C.S
    ps, psB = C.ps.next()
    for kc in range(8):
        sq, sqB = C.br.next()
        S.act.op(lambda e: e.activation(out=sq[:, 0:n], in_=src(kc), func=AF.Square), reads=[srcB[kc]], writes=[sqB])
        S.pe.op(lambda e: e.matmul(ps[:, 0:n], lhsT=C.ones[:], rhs=sq[:, 0:n], start=(kc == 0), stop=(kc == 7)),
                reads=[sqB, C.onesB], writes=[psB])
    rs, rsB = C.fr.next()
    S.act.op(lambda e: e.activation(out=rs[:, 0:n], in_=ps[:, 0:n], func=AF.Sqrt, bias=C.v("eps"), scale=1.0 / nfeat),
             reads=[psB, C.vB], writes=[rsB])
    S.dve.op(lambda e: e.reciprocal(out=rs[:, 0:n], in_=rs[:, 0:n]), reads=[rsB], writes=[rsB])
    for kc in range(8):
        S.dve.op(lambda e: e.scalar_tensor_tensor(out=dst(kc), in0=src(kc), scalar=C.v(gname, kc), in1=rs[:, 0:n],
                                                  op0=ALU.mult, op1=ALU.mult),
                 reads=[srcB[kc], rsB, C.vB], writes=[dstB[kc]])


def load_w(C, dram_rows_ap, ncols, nk=8):
    assert nk * ncols <= WSLOT
    slot, sB = C.wr.next()
    view = slot[:, 0:nk * ncols].rearrange("p (kc n) -> p kc n", kc=nk)
    C.S.gq.dma(view, dram_rows_ap.rearrange("(kc p) n -> p kc n", p=128), writes=[sB])
    return slot, sB


class W1024:
    def __init__(self, C, dram):
        self.h = [load_w(C, dram[:, i * 512:(i + 1) * 512], 512) for i in range(2)]

    def lhsT(self, kc, oc):
        slot, sB = self.h[oc // 4]
        c0 = kc * 512 + (oc % 4) * 128
        return slot[:, c0:c0 + 128], sB


GROUPS = [[0, 1, 2, 3], [4, 5, 6, 7]]
TPA = 8320
TO = 10240
NKT = 65
QBLKS = [(512 * i, 512) for i in range(16)] + [(8192, 128)]


def lru_front(C, sc, xw, w_in, w_a, w_i, gate_o, hloc_o, pcum_o, ends_src):
    nc, S = C.nc, C.S
    wg = W1024(C, w_in[:, 0:D])
    wrc = W1024(C, w_in[:, D:2 * D])
    wai, waiB = C.wr.next()
    S.gq.dma(wai[:, 0:1024].rearrange("p (h j) -> p h j", h=8), w_a.rearrange("(h p) j -> p h j", p=128), writes=[waiB])
    S.gq.dma(wai[:, 1024:2048].rearrange("p (h j) -> p h j", h=8), w_i.rearrange("(h p) j -> p h j", p=128), writes=[waiB])
    c8 = sc.sb("c8", [128, 8], F32)
    c16 = sc.sb("c16", [128, 8], F32)
    c8B = Buf("c8")
    Lap = C.vecs[:, VO["L"]:VO["L"] + 8]
    S.act.op(lambda e: e.activation(out=c8[:], in_=Lap, func=AF.Sigmoid), reads=[C.vB], writes=[c8B])
    S.act.op(lambda e: e.activation(out=c8[:], in_=c8[:], func=AF.Ln), reads=[c8B], writes=[c8B])
    S.dve.op(lambda e: e.tensor_scalar(out=c16[:], in0=c8[:], scalar1=16.0, scalar2=None, op0=ALU.mult), reads=[c8B], writes=[c8B])
    S.dve.op(lambda e: e.tensor_scalar(out=c8[:], in0=c8[:], scalar1=8.0, scalar2=None, op0=ALU.mult), reads=[c8B], writes=[c8B])
    h8 = sc.sb("h8", [128, 8], F32)
    hb = sc.sb("hb", [128, 16], F32)
    qtr = sc.sb("qtr", [128, 1], F32)
    S.dve.op(lambda e: e.tensor_scalar(out=h8[:], in0=c8[:], scalar1=0.5, scalar2=None, op0=ALU.mult), reads=[c8B], writes=[c8B])
    S.dve.op(lambda e: e.tensor_scalar(out=hb[:, 0:8], in0=C.vecs[:, VO["b_a"]:VO["b_a"] + 8], scalar1=0.5, scalar2=None, op0=ALU.mult), reads=[C.vB, c8B], writes=[c8B])
    S.dve.op(lambda e: e.tensor_scalar(out=hb[:, 8:16], in0=C.vecs[:, VO["b_i"]:VO["b_i"] + 8], scalar1=0.5, scalar2=None, op0=ALU.mult), reads=[C.vB, c8B], writes=[c8B])
    S.dve.op(lambda e: e.memset(qtr[:], 0.25), writes=[c8B])

    xt = sc.sb("xt", [128, 8, 512], F32)
    xtB = [Buf() for _ in range(8)]
    hn = [sc.sb(f"hn{i}", [128, 8, 512], BF16) for i in range(2)]
    hnB = [[Buf() for _ in range(8)] for _ in range(2)]
    rec = [sc.sb(f"rec{i}", [128, 8, 516], F32) for i in range(2)]
    recB = [[Buf() for _ in range(8)] for _ in range(2)]
    gt = sc.sb("gt", [128, 8, 512], BF16)
    gtB = [Buf() for _ in range(8)]
    hl = sc.sb("hl", [128, 8, 512], F32)
    hlB = [Buf() for _ in range(8)]
    pc = sc.sb("pc", [128, 8, 512], F32)
    pcB = [Buf() for _ in range(8)]
    zeros = sc.sb("zeros", [128, 512], F32)
    zB = Buf("zeros")
    S.pool.op(lambda e: e.memset(zeros[:], 0.0), writes=[zB])
    st = sc.sb("st", [128, 16], F32)
    stB = [Buf() for _ in range(8)]
    xwf = fm(xw)
    cvr = Ring(sc, "cvr", [128, 516], F32, 4)

    def norm_in(par, c0, n):
        for kc in range(8):
            S.sp.dma(xt[:, kc, 0:n], xwf[:, kc, c0:c0 + n], writes=[xtB[kc]])
        rmsnorm_tile(C, lambda kc: xt[:, kc, 0:n], xtB, "mix_g0", n,
                     lambda kc: hn[par][:, kc, 0:n], hnB[par])

    def mm8(W, par, oc, n):
        ps, psB = C.ps.next()
        for kc in range(8):
            l_, lB = W.lhsT(kc, oc)
            S.pe.op(lambda e: e.matmul(ps[:, 0:n], lhsT=l_, rhs=hn[par][:, kc, 0:n], start=(kc == 0), stop=(kc == 7)),
                    reads=[lB, hnB[par][kc]], writes=[psB])
        return ps, psB

    norm_in(1, 0, 3)
    for oc in range(8):
        ps, psB = mm8(wrc, 1, oc, 3)
        S.dve.op(lambda e: e.tensor_scalar(out=rec[0][:, oc, 0:3], in0=ps[:, 0:3], scalar1=C.v("b_in", 8 + oc), scalar2=C.v("cmask"),
                                           op0=ALU.add, op1=ALU.mult), reads=[psB, C.vB], writes=[recB[0][oc]])

    for ti, (t0, n) in enumerate(TILES):
        par = ti % 2
        norm_in(par, 3 + t0, n)
        for oc in range(8):
            ps, psB = mm8(wg, par, oc, n)
            S.act.op(lambda e: e.activation(out=gt[:, oc, 0:n], in_=ps[:, 0:n], func=AF.Gelu_apprx_tanh, bias=C.v("b_in", oc)),
                     reads=[psB, C.vB], writes=[gtB[oc]])
            S.gq.dma(fm(gate_o)[:, oc, t0:t0 + n], gt[:, oc, 0:n], reads=[gtB[oc]])
        stg = {}

        def stage1(oc):
            ps, psB = mm8(wrc, par, oc, n)
            S.dve.op(lambda e: e.tensor_scalar(out=rec[par][:, oc, 3:3 + n], in0=ps[:, 0:n], scalar1=C.v("b_in", 8 + oc), scalar2=None, op0=ALU.add),
                     reads=[psB, C.vB], writes=[recB[par][oc]])
            if ti + 1 < len(TILES):
                S.pool.op(lambda e: e.tensor_copy(out=rec[1 - par][:, oc, 0:3], in_=rec[par][:, oc, n:n + 3]),
                          reads=[recB[par][oc]], writes=[recB[1 - par][oc]])
            cv, cvB = cvr.next()
            S.dve.op(lambda e: e.tensor_scalar(out=cv[:, 0:n], in0=rec[par][:, oc, 3:3 + n], scalar1=C.v("cw", 3 * 8 + oc), scalar2=C.v("cb", oc),
                                               op0=ALU.mult, op1=ALU.add), reads=[recB[par][oc], C.vB], writes=[cvB])
            for k in range(3):
                S.dve.op(lambda e: e.scalar_tensor_tensor(out=cv[:, 0:n], in0=rec[par][:, oc, k:k + n], scalar=C.v("cw", k * 8 + oc), in1=cv[:, 0:n],
                                                          op0=ALU.mult, op1=ALU.add), reads=[recB[par][oc], cvB, C.vB], writes=[cvB])
            cvb, cvbB = C.br.next()
            S.dve.op(lambda e: e.tensor_copy(out=cvb[:, 0:n], in_=cv[:, 0:n]), reads=[cvB], writes=[cvbB])
            psa, psaB = C.ps.next()
            S.pe.op(lambda e: e.matmul(psa[:, 0:n], lhsT=wai[:, oc * 128:(oc + 1) * 128], rhs=cvb[:, 0:n], start=True, stop=True),
                    reads=[waiB, cvbB], writes=[psaB])
            psi, psiB = C.ps.next()
            S.pe.op(lambda e: e.matmul(psi[:, 0:n], lhsT=wai[:, 1024 + oc * 128:1024 + (oc + 1) * 128], rhs=cvb[:, 0:n], start=True, stop=True),
                    reads=[waiB, cvbB], writes=[psiB])
            stg[oc] = (cv, cvB, psa, psaB, psi, psiB)

        def stage2(oc):
            cv, cvB, psa, psaB, psi, psiB = stg.pop(oc)
            r_, rB = C.fr.next()
            S.act.op(lambda e: e.activation(out=r_[:, 0:n], in_=psa[:, 0:n], func=AF.Tanh, bias=hb[:, oc:oc + 1], scale=0.5), reads=[psaB, c8B], writes=[rB])
            gi, giB = C.fr.next()
            S.act.op(lambda e: e.activation(out=gi[:, 0:n], in_=psi[:, 0:n], func=AF.Tanh, bias=hb[:, 8 + oc:9 + oc], scale=0.5), reads=[psiB, c8B], writes=[giB])
            a_, aB = C.fr.next()
            S.act.op(lambda e: e.activation(out=a_[:, 0:n], in_=r_[:, 0:n], func=AF.Exp, scale=h8[:, oc:oc + 1], bias=h8[:, oc:oc + 1]), reads=[rB, c8B], writes=[aB])
            S.act.op(lambda e: e.activation(out=r_[:, 0:n], in_=a_[:, 0:n], func=AF.Square), reads=[aB, rB], writes=[rB])
            S.act.op(lambda e: e.activation(out=r_[:, 0:n], in_=r_[:, 0:n], func=AF.Sqrt, bias=qtr[:, 0:1], scale=-0.25), reads=[rB, c8B], writes=[rB])
            S.dve.op(lambda e: e.scalar_tensor_tensor(out=gi[:, 0:n], in0=gi[:, 0:n], scalar=1.0, in1=cv[:, 0:n], op0=ALU.add, op1=ALU.mult),
                     reads=[giB, cvB], writes=[giB])
            S.dve.op(lambda e: e.tensor_tensor(out=gi[:, 0:n], in0=gi[:, 0:n], in1=r_[:, 0:n], op=ALU.mult), reads=[giB, rB], writes=[giB])
            if ti == 0:
                ih, ip = 0.0, 1.0
                rd = []
            else:
                ip = st[:, oc:oc + 1]
                ih = st[:, 8 + oc:9 + oc]
                rd = [stB[oc]]
            S.dve.op(lambda e: e.tensor_tensor_scan(out=hl[:, oc, 0:n], data0=a_[:, 0:n], data1=gi[:, 0:n], initial=ih, op0=ALU.mult, op1=ALU.add),
                     reads=[aB, giB] + rd, writes=[hlB[oc]])
            S.dve.op(lambda e: e.tensor_tensor_scan(out=pc[:, oc, 0:n], data0=a_[:, 0:n], data1=zeros[:, 0:n], initial=ip, op0=ALU.mult, op1=ALU.add),
                     reads=[aB, zB] + rd, writes=[pcB[oc]])
            if ti + 1 < len(TILES):
                S.pool.op(lambda e: e.tensor_copy(out=st[:, 8 + oc:9 + oc], in_=hl[:, oc, n - 1:n]), reads=[hlB[oc]], writes=[stB[oc]])
                S.pool.op(lambda e: e.tensor_copy(out=st[:, oc:oc + 1], in_=pc[:, oc, n - 1:n]), reads=[pcB[oc]], writes=[stB[oc]])
            S.gq.dma(fm(hloc_o)[:, oc, t0:t0 + n], hl[:, oc, 0:n], reads=[hlB[oc]])
            S.gq.dma(fm(pcum_o)[:, oc, t0:t0 + n], pc[:, oc, 0:n], reads=[pcB[oc]])

        stage1(0)
        stage1(1)
        for oc in range(8):
            if oc + 2 < 8:
                stage1(oc + 2)
            stage2(oc)
    S.sp.dma(ends_src, st[:], reads=stB)


NG = 8


def fm_w(w):
    return w.rearrange("(kc p) n -> p kc n", p=128)


def ffn_block(C, sc, hT, hTB, hn, hnB, w_up, w_dn, layer):
    nc, S = C.nc, C.S
    gname = f"ffn_g{layer}"
    for ti, (t0, n) in enumerate(TILES):
        rmsnorm_tile(C, lambda kc: hT[:, kc, t0:t0 + n], [hTB[kc][ti] for kc in range(8)], gname, n,
                     lambda kc: hn[:, kc, t0:t0 + n], [hnB[kc][ti] for kc in range(8)])
    tails = [sc.sb(f"tl{layer}_{i}", [128, 6, 2], F32) for i in range(2)]
    tlB = [[Buf() for _ in range(6)] for _ in range(2)]
    act = [sc.sb(f"act{layer}_{i}", [128, 3, 512], BF16) for i in range(2)]
    actB = [[Buf() for _ in range(3)] for _ in range(2)]
    fcw, fcb = f"fcw{layer}", f"fcb{layer}"
    step = 0
    pending = None

    def emit_down(wd, wdB, ap_, ti, t0, n):
        for oc in range(8):
            ps, psB = C.ps.next()
            for j in range(3):
                S.pe.op(lambda e: e.matmul(ps[:, 0:n], lhsT=wd[:, j * D + oc * 128: j * D + (oc + 1) * 128], rhs=act[ap_][:, j, 0:n],
                                           start=(j == 0), stop=(j == 2)), reads=[wdB, actB[ap_][j]], writes=[psB])
            S.dve.op(lambda e: e.tensor_tensor(out=hT[:, oc, t0:t0 + n], in0=ps[:, 0:n], in1=hT[:, oc, t0:t0 + n], op=ALU.add),
                     reads=[psB, hTB[oc][ti]], writes=[hTB[oc][ti]])

    def load_group(g):
        slot, sB = C.wr.next()
        view = slot[:, 0:8 * 768].rearrange("p (kc n) -> p kc n", kc=8)
        S.gq.dma(view[:, :, 0:384], fm_w(w_up)[:, :, g * 384:(g + 1) * 384], writes=[sB])
        S.gq.dma(view[:, :, 384:768], fm_w(w_up)[:, :, 3072 + g * 384:3072 + (g + 1) * 384], writes=[sB])
        wd, wdB = load_w(C, w_dn[g * 384:(g + 1) * 384, :], D, nk=3)
        return slot, sB, wd, wdB

    nxt = load_group(0)
    for g in range(NG):
        slot, sB, wd, wdB = nxt
        for ti, (t0, n) in enumerate(TILES):
            if ti == 1 and g + 1 < NG:
                nxt = load_group(g + 1)
            ap_ = step % 2
            step += 1
            tp = ti % 2
            for j in range(3):
                outs = []
                for half in range(2):
                    fc = half * 24 + g * 3 + j
                    ci = half * 3 + j
                    ps, psB = C.ps.next()
                    for kc in range(8):
                        c0 = kc * 768 + half * 384 + j * 128
                        S.pe.op(lambda e: e.matmul(ps[:, 0:n], lhsT=slot[:, c0:c0 + 128], rhs=hn[:, kc, t0:t0 + n], start=(kc == 0), stop=(kc == 7)),
                                reads=[sB, hnB[kc][ti]], writes=[psB])
                    raw, rawB = C.fr.next()
                    S.act.op(lambda e: e.activation(out=raw[:, 2:2 + n], in_=ps[:, 0:n], func=AF.Identity), reads=[psB], writes=[rawB])
                    if ti == 0:
                        S.act.op(lambda e: e.activation(out=raw[:, 0:2], in_=raw[:, 2:4], func=AF.Identity, scale=0.0), reads=[rawB], writes=[rawB])
                    else:
                        S.act.op(lambda e: e.activation(out=raw[:, 0:2], in_=tails[1 - tp][:, ci, :], func=AF.Identity), reads=[tlB[1 - tp][ci]], writes=[rawB])
                    if ti + 1 < len(TILES):
                        S.act.op(lambda e: e.activation(out=tails[tp][:, ci, :], in_=raw[:, n:n + 2], func=AF.Identity), reads=[rawB], writes=[tlB[tp][ci]])
                    cv, cvB = C.fr.next()
                    S.act.op(lambda e: e.activation(out=cv[:, 0:n], in_=ps[:, 0:n], func=AF.Identity, bias=C.v(fcb, fc), scale=C.v(fcw, 2 * 48 + fc)),
                             reads=[psB, C.vB], writes=[cvB])
                    for k in range(2):
                        S.dve.op(lambda e: e.scalar_tensor_tensor(out=cv[:, 0:n], in0=raw[:, k:k + n], scalar=C.v(fcw, k * 48 + fc), in1=cv[:, 0:n],
                                                                  op0=ALU.mult, op1=ALU.add), reads=[rawB, cvB, C.vB], writes=[cvB])
                    outs.append((cv, cvB))
                (gv, gvB), (vv, vvB) = outs
                S.act.op(lambda e: e.activation(out=gv[:, 0:n], in_=gv[:, 0:n], func=AF.Gelu_apprx_tanh), reads=[gvB], writes=[gvB])
                S.dve.op(lambda e: e.tensor_tensor(out=act[ap_][:, j, 0:n], in0=gv[:, 0:n], in1=vv[:, 0:n], op=ALU.mult),
                         reads=[gvB, vvB], writes=[actB[ap_][j]])
            if pending is not None:
                emit_down(*pending)
            pending = (wd, wdB, ap_, ti, t0, n)
    emit_down(*pending)


def lru_back(C, sc, xw, gate_i, hloc_i, pcum_i, ends_all, w_out, hT, hTB):
    nc, S = C.nc, C.S
    wo = W1024(C, w_out)
    en = sc.sb("ends_sb", [128, 4, 2, 8], F32)
    enB = Buf("ends")
    S.sp.dma(en[:].rearrange("p j a k -> p j (a k)"), ends_all.rearrange("(j p) f -> p j f", p=128), writes=[enB])
    cr = sc.sb("carry", [128, 8], F32)
    cj = sc.sb("cj", [128, 8], F32)
    crB = Buf("carry")
    S.dve.op(lambda e: e.memset(cr[:], 0.0), writes=[crB])
    S.dve.op(lambda e: e.memset(cj[:], 0.0), writes=[crB])
    for j in range(3):
        S.dve.op(lambda e: e.tensor_tensor(out=cj[:], in0=cj[:], in1=en[:, j, 0, :], op=ALU.mult), reads=[crB, enB], writes=[crB])
        S.dve.op(lambda e: e.tensor_tensor(out=cj[:], in0=cj[:], in1=en[:, j, 1, :], op=ALU.add), reads=[crB, enB], writes=[crB])
        S.dve.op(lambda e: e.scalar_tensor_tensor(out=cr[:], in0=cj[:], scalar=C.v("onehot", j + 1), in1=cr[:], op0=ALU.mult, op1=ALU.add),
                 reads=[crB, C.vB], writes=[crB])
    yb = sc.sb("yb", [128, 8, 512], BF16)
    ybB = [Buf() for _ in range(8)]
    xwf = fm(xw)
    for ti, (t0, n) in enumerate(TILES):
        for kc in range(8):
            hl, hlB = C.fr.next()
            S.sp.dma(hl[:, 0:n], fm(hloc_i)[:, kc, t0:t0 + n], writes=[hlB])
            pcm, pcmB = C.fr.next()
            S.sp.dma(pcm[:, 0:n], fm(pcum_i)[:, kc, t0:t0 + n], writes=[pcmB])
            gt, gtB = C.br.next()
            S.sp.dma(gt[:, 0:n], fm(gate_i)[:, kc, t0:t0 + n], writes=[gtB])
            S.dve.op(lambda e: e.scalar_tensor_tensor(out=hl[:, 0:n], in0=pcm[:, 0:n], scalar=cr[:, kc:kc + 1], in1=hl[:, 0:n], op0=ALU.mult, op1=ALU.add),
                     reads=[pcmB, hlB, crB], writes=[hlB])
            S.pool.op(lambda e: e.tensor_tensor(out=yb[:, kc, 0:n], in0=hl[:, 0:n], in1=gt[:, 0:n], op=ALU.mult),
                      reads=[hlB, gtB], writes=[ybB[kc]])
        for oc in range(8):
            xs, xsB = C.fr.next()
            S.sp.dma(xs[:, 0:n], xwf[:, oc, 3 + t0:3 + t0 + n], writes=[xsB])
            ps, psB = C.ps.next()
            for kc in range(8):
                l_, lB = wo.lhsT(kc, oc)
                S.pe.op(lambda e: e.matmul(ps[:, 0:n], lhsT=l_, rhs=yb[:, kc, 0:n], start=(kc == 0), stop=(kc == 7)),
                        reads=[lB, ybB[kc]], writes=[psB])
            S.dve.op(lambda e: e.scalar_tensor_tensor(out=hT[:, oc, t0:t0 + n], in0=ps[:, 0:n], scalar=C.v("b_out", oc), in1=xs[:, 0:n],
                                                      op0=ALU.add, op1=ALU.add), reads=[psB, xsB, C.vB], writes=[hTB[oc][ti]])


def qkv_block(C, sc, hT, hTB, hn, hnB, wA_d, wR_d, wV_d, cosF, sinF, hn_src, hn_src4, hn_all, hn_all4, qk_sel, v_sel, XB):
    nc, S = C.nc, C.S
    wA, wAB = load_w(C, wA_d, 512)
    wR, wRB = load_w(C, wR_d, 512)
    wV, wVB = load_w(C, wV_d, 256)
    for ti, (t0, n) in enumerate(TILES):
        bl = [hnB[kc][ti] for kc in range(8)]
        rmsnorm_tile(C, lambda kc: hT[:, kc, t0:t0 + n], [hTB[kc][ti] for kc in range(8)], "mix_g1", n,
                     lambda kc: hn[:, kc, t0:t0 + n], bl)
        src = hn_src[ti] if n == 512 else hn_src4
        dst = hn_all[ti] if n == 512 else hn_all4
        stB = []
        S.sp.dma(src.rearrange("(kc p) t -> p kc t", p=128), hn[:, :, t0:t0 + n], reads=bl, writes=[_nb(stB)])
        S.all_gather_async(src, dst, GROUPS, stB, XB["hn"][ti])
    NXH = 4
    xh = [hn[:, :, i * 512:(i + 1) * 512] for i in range(NXH)]
    xhB = [Buf() for _ in range(NXH)]
    xh_guard = [[hnB[kc][i] for kc in range(8)] for i in range(NXH)]
    csr = [sc.sb(f"csr{i}", [128, 512], F32) for i in range(2)]
    snr = [sc.sb(f"snr{i}", [128, 512], F32) for i in range(2)]
    csrB = [Buf() for _ in range(2)]
    snrB = [Buf() for _ in range(2)]
    NVB = 6
    vb = [sc.sb(f"vb{i}", [128, 256], BF16) for i in range(NVB)]
    vbB = [Buf() for _ in range(NVB)]
    vsel_r = v_sel.rearrange("h r v -> r h v")
    vi = 0
    plan = []
    for ti, (t0, n) in enumerate(TILES):
        for j in range(4):
            lo = HALO if (j > 0 and ti == 0) else 0
            plan.append((ti, t0, n, j, lo))

    def issue_loads(k):
        ti, t0, n, j, lo = plan[k]
        all_t = hn_all[ti] if n == 512 else hn_all4
        m_ = n - lo
        s0 = 2048 * j + t0 + lo
        S.sp.dma(xh[k % NXH][:, :, 0:m_], all_t[j * 1024:(j + 1) * 1024, lo:n].rearrange("(kc p) t -> p kc t", p=128), reads=[XB["hn"][ti]],
                 writes=[xhB[k % NXH]] + xh_guard[k % NXH])
        S.sp.dma(csr[k % 2][:, 0:m_], cosF[:, s0:s0 + m_], writes=[csrB[k % 2]])
        S.sp.dma(snr[k % 2][:, 0:m_], sinF[:, s0:s0 + m_], writes=[snrB[k % 2]])

    issue_loads(0)
    for step, (ti, t0, n, j, lo) in enumerate(plan):
        if True:
            m_ = n - lo
            c0 = t0 + lo
            x_, xB = xh[step % NXH], xhB[step % NXH]
            cs, csB, sn, snB = csr[step % 2], csrB[step % 2], snr[step % 2], snrB[step % 2]
            if step + 1 < len(plan):
                issue_loads(step + 1)
            for c4 in range(4):
                psA, psAB = C.ps.next()
                psR, psRB = C.ps.next()
                for (w_, wB_, p_, pB_) in ((wA, wAB, psA, psAB), (wR, wRB, psR, psRB)):
                    for kc in range(8):
                        S.pe.op(lambda e: e.matmul(p_[:, 0:m_], lhsT=w_[:, kc * 512 + c4 * 128: kc * 512 + (c4 + 1) * 128], rhs=x_[:, kc, 0:m_],
                                                   start=(kc == 0), stop=(kc == 7)), reads=[wB_, xB], writes=[pB_])
                t1, t1B = C.fr.next()
                S.dve.op(lambda e: e.tensor_tensor(out=t1[:, 0:m_], in0=psA[:, 0:m_], in1=cs[:, 0:m_], op=ALU.mult), reads=[psAB, csB], writes=[t1B])
                t2, t2B = C.fr.next()
                S.dve.op(lambda e: e.tensor_tensor(out=t2[:, 0:m_], in0=psR[:, 0:m_], in1=sn[:, 0:m_], op=ALU.mult), reads=[psRB, snB], writes=[t2B])
                ob, obB = C.br.next()
                S.dve.op(lambda e: e.tensor_tensor(out=ob[:, 0:m_], in0=t1[:, 0:m_], in1=t2[:, 0:m_], op=ALU.add), reads=[t1B, t2B], writes=[obB])
                r0 = c4 * 512 + j * 128
                S.sp.dma(qk_sel[r0:r0 + 128, c0:c0 + m_], ob[:, 0:m_], reads=[obB], writes=[_nb(XB["selqk"][c4 // 2])])
            for u0 in range(0, m_, 128):
                um = min(128, m_ - u0)
                ps, psB = C.ps.next()
                for kc in range(8):
                    S.pe.op(lambda e: e.matmul(ps[0:um, 0:256], lhsT=x_[:, kc, u0:u0 + um], rhs=wV[:, kc * 256:(kc + 1) * 256],
                                               start=(kc == 0), stop=(kc == 7)), reads=[wVB, xB], writes=[psB])
                v_, vB_ = vb[vi % NVB], vbB[vi % NVB]
                vi += 1
                S.act.op(lambda e: e.activation(out=v_[0:um, :], in_=ps[0:um, 0:256], func=AF.Identity), reads=[psB], writes=[vB_])
                rr = j * WIN + c0 + u0
                S.sp.dma(vsel_r[rr:rr + um, :, :], v_[0:um, :].rearrange("r (h v) -> r h v", h=2), reads=[vB_], writes=[_nb(XB["selv"])])


def attention(C, sc, qk_all, v_all, qk_sel, v_sel, masks, oT_src, oT_all, XB):
    nc, S = C.nc, C.S
    hp = C.cid
    q = sc.sb("q", [128, 2 * TPA], BF16)
    kz = [sc.sb(f"kz{m}", [128, TPA], BF16) for m in range(2)]
    vs = sc.sb("vs", [128, NKT, 2, 128], BF16)
    mk = sc.sb("mk", [128, 128], BF16)
    mkB = Buf("mk")
    qBs = [[Buf() for _ in range(5)] for _ in range(2)]
    kzB = [[Buf() for _ in range(5)] for _ in range(2)]
    vsB = [Buf() for _ in range(NKT)]
    S.pool.op(lambda e: e.memset(vs[:, 64, :, :].rearrange("p b c -> p (b c)"), 0.0), writes=[vsB[64]])
    for hh in range(2):
        S.dve.op(lambda e: e.memset(q[:, hh * TPA + TSEQ:(hh + 1) * TPA], 0.0), writes=[qBs[hh][4]])
    for m in range(2):
        S.pool.op(lambda e: e.memset(kz[m][:], 0.0), writes=kzB[m])
    selB_q, selB_k, selB_v = XB["selqk"][0], XB["selqk"][1], XB["selv"]
    def owner(s_):
        return 4 if s_ >= TSEQ else (0 if s_ < WIN else 1 + (s_ - WIN) // 2048)

    def owners(s_lo, s_hi):
        return sorted({owner(s_lo), owner(s_hi - 1)} | ({owner(x) for x in (WIN, WIN + 2048, WIN + 4096, TSEQ) if s_lo < x < s_hi}))

    def load_q(hh, j):
        lo = 0 if j == 0 else HALO
        r0 = hh * 512 + j * 128
        S.sp.dma(q[:, hh * TPA + 2048 * j + lo: hh * TPA + 2048 * j + WIN], qk_sel[r0:r0 + 128, lo:WIN], reads=selB_q, writes=[qBs[hh][j]])

    def load_kj(hh, j):
        lo = 0 if j == 0 else HALO
        r0 = 1024 + hh * 512 + j * 128
        for m in range(2):
            S.sp.dma(kz[m][m * 64:(m + 1) * 64, 2048 * j + lo:2048 * j + WIN], qk_sel[r0 + m * 64:r0 + (m + 1) * 64, lo:WIN], reads=selB_k, writes=[kzB[m][j]])

    def load_k(hh):
        for j in range(4):
            load_kj(hh, j)

    vsel_r = v_sel.rearrange("h r v -> r h v")

    def load_v(t):
        for p0, p1 in ((0, 16), (16, 128)) if t in (16, 32, 48, 64) else ((0, 128),):
            s_ = t * 128 + p0
            if s_ >= TSEQ:
                continue
            j = 0 if s_ < WIN else 1 + (s_ - WIN) // 2048
            r0 = s_ + 16 * j
            S.sp.dma(vs[p0:p1, t, :, :], vsel_r[r0:r0 + (p1 - p0), :, :], reads=selB_v, writes=[vsB[t]])

    for j in range(4):
        load_q(0, j)
        load_kj(0, j)
        for t in range(16 * j + (1 if j else 0), 16 * (j + 1) + 1):
            load_v(t)
    for j in range(4):
        load_q(1, j)
    S.sp.dma(mk[:], masks[:, 0:128], writes=[mkB])
    lt = sc.sb("lt", [128, 128], F32)
    ls = sc.sb("ls", [128, 4], F32)
    lB = Buf("lam")
    lo_ = VO["lam"]
    S.dve.op(lambda e: e.tensor_tensor(out=lt[:, 0:64], in0=C.vecs[:, lo_:lo_ + 64], in1=C.vecs[:, lo_ + 64:lo_ + 128], op=ALU.mult), reads=[C.vB], writes=[lB])
    S.dve.op(lambda e: e.tensor_tensor(out=lt[:, 64:128], in0=C.vecs[:, lo_ + 128:lo_ + 192], in1=C.vecs[:, lo_ + 192:lo_ + 256], op=ALU.mult), reads=[C.vB, lB], writes=[lB])
    S.dve.op(lambda e: e.tensor_reduce(out=ls[:, 0:1], in_=lt[:, 0:64], axis=mybir.AxisListType.X, op=ALU.add), reads=[lB], writes=[lB])
    S.dve.op(lambda e: e.tensor_reduce(out=ls[:, 1:2], in_=lt[:, 64:128], axis=mybir.AxisListType.X, op=ALU.add), reads=[lB], writes=[lB])
    S.act.op(lambda e: e.activation(out=ls[:, 0:2], in_=ls[:, 0:2], func=AF.Exp), reads=[lB], writes=[lB])
    S.dve.op(lambda e: e.tensor_tensor(out=ls[:, 2:3], in0=ls[:, 1:2], in1=ls[:, 0:1], op=ALU.subtract), reads=[lB], writes=[lB])
    S.dve.op(lambda e: e.tensor_scalar(out=ls[:, 3:4], in0=ls[:, 2:3], scalar1=float(-LAMBDA_INIT), scalar2=None, op0=ALU.add), reads=[lB], writes=[lB])
    nlam = ls[:, 3:4]
    gp = sc.sb("gp", [128, 1], F32)
    gpB = Buf("gp")
    S.dve.op(lambda e: e.tensor_scalar(out=gp[:], in0=C.v("subgp"), scalar1=float(1.0 - LAMBDA_INIT), scalar2=None, op0=ALU.mult),
             reads=[C.vB], writes=[gpB])

    accT = [[sc.psum(f"accT{i}_{m}", [128, 512], F32) for m in range(2)] for i in range(2)]
    accB = [[Buf() for m in range(2)] for i in range(2)]
    sps = Ring(sc, "sps", [128, 512], F32, 3, psum=True)
    rsacc = sc.psum("rsacc", [128, 512], F32)
    rsaccB = Buf("rsacc")
    rs1 = [sc.sb(f"rs1_{i}", [128, 512], F32) for i in range(2)]
    rs1B = [Buf() for _ in range(2)]
    otb = [sc.sb(f"aot{i}", [128, 512], BF16) for i in range(2)]
    otB = [Buf() for _ in range(2)]
    osrcB = [[[] for _ in range(5)] for _ in range(2)]
    LA = 2
    FDLY = 10
    rs1d = [rs1[0], rs1[1]]
    rs1dB = [rs1B[0], rs1B[1]]
    blocks = []
    for hh in range(2):
        for bi, (q0, nq) in enumerate(QBLKS):
            nqs = nq // 128
            qt0 = q0 // 128
            kt_last = min(NKT - 1, qt0 + nqs - 1)
            blocks.append(dict(idx=len(blocks), hh=hh, bi=bi, q0=q0, nq=nq, qt0=qt0, kt_last=kt_last, par=len(blocks) % 2, rs1_started=False))
    steps = [(blk, kt, m) for blk in blocks for kt in range(blk["kt_last"] + 1) for m in range(2)]
    pts = {}
    loaded_k = {0}

    def emit_score(i):
        blk, kt, m = steps[i]
        hh, q0, nq, qt0, par = blk["hh"], blk["q0"], blk["nq"], blk["qt0"], blk["par"]
        if hh not in loaded_k:
            load_k(hh)
            loaded_k.add(hh)
        dq = kt - qt0
        c0 = max(dq, 0) * 128
        sp_, spB = sps.next()
        S.pe.op(lambda e: e.matmul(sp_[:, c0:nq], lhsT=kz[m][:, kt * 128:(kt + 1) * 128],
                                   rhs=q[:, hh * TPA + q0 + c0: hh * TPA + q0 + nq], start=True, stop=True),
                reads=[kzB[m][c_] for c_ in owners(kt * 128, kt * 128 + 128)] + [qBs[hh][c_] for c_ in owners(q0 + c0, q0 + nq)], writes=[spB])
        pt, ptB = C.br.next()
        S.act.op(lambda e: e.activation(out=pt[:, c0:nq], in_=sp_[:, c0:nq], func=AF.Exp, scale=0.125), reads=[spB], writes=[ptB])
        if dq >= 0:
            S.dve.op(lambda e: e.tensor_tensor(out=pt[:, c0:c0 + 128], in0=pt[:, c0:c0 + 128], in1=mk[:], op=ALU.mult),
                     reads=[ptB, mkB], writes=[ptB])
        if m == 1:
            r1, r1B = rs1d[par], rs1dB[par]
            if not blk["rs1_started"]:
                S.dve.op(lambda e: e.tensor_copy(out=r1[:, 0:nq], in_=pt[:, 0:nq]), reads=[ptB], writes=[r1B])
                blk["rs1_started"] = True
            else:
                S.dve.op(lambda e: e.tensor_tensor(out=r1[:, c0:nq], in0=r1[:, c0:nq], in1=pt[:, c0:nq], op=ALU.add),
                         reads=[ptB, r1B], writes=[r1B])
        pts[i] = (pt, ptB, c0)

    def emit_pv(i):
        blk, kt, m = steps[i]
        hh, nq, par, kt_last = blk["hh"], blk["nq"], blk["par"], blk["kt_last"]
        pt, ptB, c0 = pts.pop(i)
        S.pe.op(lambda e: e.matmul(accT[par][m][:, c0:nq], lhsT=vs[:, kt, hh, :], rhs=pt[:, c0:nq],
                                   start=(kt == 0), stop=(kt == kt_last)),
                reads=[ptB, vsB[kt]], writes=[accB[par][m]])
        if m == 0:
            S.pe.op(lambda e: e.matmul(rsacc[:, c0:nq], lhsT=C.ones[:], rhs=pt[:, c0:nq], start=(kt == 0), stop=(kt == kt_last)),
                    reads=[ptB, C.onesB], writes=[rsaccB])

    def fin_early(blk):
        nq = blk["nq"]
        rl, rlB = C.fr.next()
        S.act.op(lambda e: e.activation(out=rl[:, 0:nq], in_=rsacc[:, 0:nq], func=AF.Identity), reads=[rsaccB], writes=[rlB])
        blk["rl0"] = (rl, rlB)

    def fin_late(blk):
        hh, bi, q0, nq, par = blk["hh"], blk["bi"], blk["q0"], blk["nq"], blk["par"]
        r1, r1B = rs1d[par], rs1dB[par]
        tn = []
        for m in range(2):
            if m == 0:
                rl, rlB = blk["rl0"]
                S.dve.op(lambda e: e.reciprocal(out=rl[:, 0:nq], in_=rl[:, 0:nq]), reads=[rlB], writes=[rlB])
            else:
                rl, rlB = C.fr.next()
                hi, hiB = C.br.next()
                S.dve.op(lambda e: e.tensor_copy(out=hi[:, 0:nq], in_=r1[:, 0:nq]), reads=[r1B], writes=[hiB])
                lo2, lo2B = C.br.next()
                S.dve.op(lambda e: e.tensor_tensor(out=lo2[:, 0:nq], in0=r1[:, 0:nq], in1=hi[:, 0:nq], op=ALU.subtract),
                         reads=[r1B, hiB], writes=[lo2B])
                fin, finB = sps.next()
                S.pe.op(lambda e: e.matmul(fin[:, 0:nq], lhsT=C.ones[:], rhs=hi[:, 0:nq], start=True, stop=False), reads=[hiB, C.onesB], writes=[finB])
                S.pe.op(lambda e: e.matmul(fin[:, 0:nq], lhsT=C.ones[:], rhs=lo2[:, 0:nq], start=False, stop=True), reads=[lo2B, C.onesB], writes=[finB])
                S.dve.op(lambda e: e.reciprocal(out=rl[:, 0:nq], in_=fin[:, 0:nq]), reads=[finB], writes=[rlB])
            t_, tB = C.fr.next()
            S.dve.op(lambda e: e.tensor_tensor(out=t_[:, 0:nq], in0=accT[par][m][:, 0:nq], in1=rl[:, 0:nq], op=ALU.mult),
                     reads=[accB[par][m], rlB], writes=[tB])
            tn.append((t_, tB))
        (t0_, t0B), (t1_, t1B) = tn
        S.dve.op(lambda e: e.scalar_tensor_tensor(out=t0_[:, 0:nq], in0=t1_[:, 0:nq], scalar=nlam, in1=t0_[:, 0:nq], op0=ALU.mult, op1=ALU.add),
                 reads=[t0B, t1B, lB], writes=[t0B])
        sq, sqB = C.br.next()
        S.act.op(lambda e: e.activation(out=sq[:, 0:nq], in_=t0_[:, 0:nq], func=AF.Square), reads=[t0B], writes=[sqB])
        fin, finB = sps.next()
        S.pe.op(lambda e: e.matmul(fin[:, 0:nq], lhsT=C.ones[:], rhs=sq[:, 0:nq], start=True, stop=True), reads=[sqB, C.onesB], writes=[finB])
        S.act.op(lambda e: e.activation(out=t1_[:, 0:nq], in_=fin[:, 0:nq], func=AF.Sqrt, bias=C.v("eps"), scale=1.0 / 128),
                 reads=[finB, C.vB], writes=[t1B])
        S.dve.op(lambda e: e.reciprocal(out=t1_[:, 0:nq], in_=t1_[:, 0:nq]), reads=[t1B], writes=[t1B])
        o_t, o_tB = otb[par], otB[par]
        S.dve.op(lambda e: e.scalar_tensor_tensor(out=o_t[:, 0:nq], in0=t0_[:, 0:nq], scalar=gp[:, 0:1], in1=t1_[:, 0:nq], op0=ALU.mult, op1=ALU.mult),
                 reads=[t0B, t1B, gpB], writes=[o_tB])
        c = q0 // 2048
        S.sp.dma(oT_src[hh, c, :, q0 % 2048:q0 % 2048 + nq], o_t[:, 0:nq], reads=[o_tB], writes=[_nb(osrcB[hh][c])])
        if (q0 + nq) % 2048 == 0 or bi == len(QBLKS) - 1:
            S.all_gather_async(oT_src[hh, c], oT_all[hh, c], GROUPS, osrcB[hh][c], XB["o"][hh][c])

    NS = len(steps)
    late = {}
    for i in range(NS + LA + FDLY + 1):
        if i < NS:
            blk, kt, m = steps[i]
            if kt == 0 and m == 0 and (blk["idx"] - 2) in late:
                fin_late(late.pop(blk["idx"] - 2)[1])
            emit_score(i)
        j = i - LA
        if 0 <= j < NS:
            emit_pv(j)
            blk, kt, m = steps[j]
            if kt == blk["kt_last"] and m == 1:
                fin_early(blk)
                late[blk["idx"]] = (i + FDLY, blk)
        for k_ in [k_ for k_, (due, _) in late.items() if due <= i]:
            fin_late(late.pop(k_)[1])
    assert not late and not pts


def attn_out(C, sc, oT_all, w_o, hT, hTB, hn, hnB, XB):
    nc, S = C.nc, C.S
    wo = W1024(C, w_o)
    cid = C.cid
    hn4 = hn[:].rearrange("p (hp hh) t -> p hp hh t", hh=2)
    for hh in range(2):
        o5 = oT_all[hh].rearrange("c (hp p) t -> p hp c t", p=128)
        o5s = oT_all[hh, 1:5].rearrange("c (hp p) t -> p hp c t", p=128)
        S.sp.dma(hn4[:, :, hh, 0:2048].unsqueeze(2), o5[:, :, bass.ds(cid, 1), :], reads=XB["o"][hh],
                 writes=[hnB[hp_ * 2 + hh][ti] for hp_ in range(4) for ti in range(4)])
        S.sp.dma(hn4[:, :, hh, 2048:WIN].unsqueeze(2), o5s[:, :, bass.ds(cid, 1), 0:WIN - 2048], reads=XB["o"][hh],
                 writes=[hnB[hp_ * 2 + hh][4] for hp_ in range(4)])
    for ti, (t0, n) in enumerate(TILES):
        for oc in range(8):
            ps, psB = C.ps.next()
            for kc in range(8):
                l_, lB = wo.lhsT(kc, oc)
                S.pe.op(lambda e: e.matmul(ps[:, 0:n], lhsT=l_, rhs=hn[:, kc, t0:t0 + n],
                                           start=(kc == 0), stop=(kc == 7)), reads=[lB, hnB[kc][ti]], writes=[psB])
            S.dve.op(lambda e: e.tensor_tensor(out=hT[:, oc, t0:t0 + n], in0=ps[:, 0:n], in1=hT[:, oc, t0:t0 + n], op=ALU.add),
                     reads=[psB, hTB[oc][ti]], writes=[hTB[oc][ti]])


def final_norm(C, hT, hTB, out_o):
    S = C.S
    for ti, (t0, n) in enumerate(TILES):
        bl = [hTB[kc][ti] for kc in range(8)]
        rmsnorm_tile(C, lambda kc: hT[:, kc, t0:t0 + n], bl, "fin_g", n, lambda kc: hT[:, kc, t0:t0 + n], bl)
        for kc in range(8):
            S.sp.dma(fm(out_o)[:, kc, t0:t0 + n], hT[:, kc, t0:t0 + n], reads=[bl[kc]])


def build_fused():
    nc = bass.Bass("TRN2", target_bir_lowering=False)
    I = lambda name, shape, dt=F32: nc.dram_tensor(name, list(shape), dt, kind="ExternalInput").ap()
    T_ = lambda name, shape, dt=F32: nc.dram_tensor(name, list(shape), dt).ap()
    xw = I("xw", [D, 3 + WIN])
    vecs = I("vecs", [128, NV])
    ident = I("ident", [128, 128], BF16)
    masks = I("masks", [128, 4 * 512], BF16)
    w_in = I("w_in", [D, 2 * D])
    w_a = I("w_a", [D, 128])
    w_i = I("w_i", [D, 128])
    w_out = I("w_out", [D, D])
    w_up0 = I("w_up0", [D, 6 * D])
    w_dn0 = I("w_dn0", [3 * D, D])
    wA_d = I("wA_hp", [D, 512])
    wR_d = I("wR_hp", [D, 512])
    wV_d = I("wV_hp", [D, 256])
    cosF = I("cosF", [128, TSEQ])
    sinF = I("sinF", [128, TSEQ])
    w_o = I("w_o", [D, D])
    w_up1 = I("w_up1", [D, 6 * D])
    w_dn1 = I("w_dn1", [3 * D, D])
    out_o = nc.dram_tensor("out_o", [D, WIN], F32, kind="ExternalOutput").ap()
    gate_s = T_("gate_s", [D, WIN], BF16)
    hloc_s = T_("hloc_s", [D, WIN])
    pcum_s = T_("pcum_s", [D, WIN])
    ends_src = T_("ends_src", [128, 16])
    ends_all = T_("ends_all", [4 * 128, 16])
    hn_src = T_("hn_src", [4, D, 512], BF16)
    hn_all = T_("hn_all", [4, 4 * D, 512], BF16)
    hn_src4 = T_("hn_src4", [D, 16], BF16)
    hn_all4 = T_("hn_all4", [4 * D, 16], BF16)
    qk_sel = T_("qk_sel", [2 * D, WIN], BF16)
    v_sel = T_("v_sel", [2, 4 * WIN, 128], BF16)
    oT_src = T_("oT_src", [2, 5, 128, 2048], BF16)
    oT_all = T_("oT_all", [2, 5, 4 * 128, 2048], BF16)
    XB = {"hn": [Buf() for _ in range(5)], "o": [[Buf() for _ in range(5)] for _ in range(2)], "selqk": [[], []], "selv": []}
    with nc.allow_low_precision("bf16 matmul operands, fp32 accumulation"), Scope(nc) as top:
        C = Ctx(nc, top, vecs, ident)
        S = C.S
        with Scope(nc) as sc:
            C.phase(sc, n_w=5)
            lru_front(C, sc, xw, w_in, w_a, w_i, gate_s, hloc_s, pcum_s, ends_src)
            S.barrier()
        S.all_gather([(ends_src, ends_all)], GROUPS)
        hT = top.sb("hT", [128, 8, WIN], F32)
        hTB = [[Buf() for _ in TILES] for _ in range(8)]
        with Scope(nc) as sc:
            C.phase(sc, n_w=4)
            hn = sc.sb("hnw", [128, 8, WIN], BF16)
            hnB = [[Buf() for _ in TILES] for _ in range(8)]
            with Scope(nc) as s2:
                lru_back(C, s2, xw, gate_s, hloc_s, pcum_s, ends_all, w_out, hT, hTB)
                S.barrier()
            with Scope(nc) as s2:
                ffn_block(C, s2, hT, hTB, hn, hnB, w_up0, w_dn0, 0)
                S.barrier()
            with Scope(nc) as s2:
                qkv_block(C, s2, hT, hTB, hn, hnB, wA_d, wR_d, wV_d, cosF, sinF, hn_src, hn_src4, hn_all, hn_all4, qk_sel, v_sel, XB)
                S.barrier()
        with Scope(nc) as sc:
            C.phase(sc, n_w=0, n_ps=0)
            attention(C, sc, None, None, qk_sel, v_sel, masks, oT_src, oT_all, XB)
            S.barrier()
        with Scope(nc) as sc:
            C.phase(sc, n_w=4)
            hn = sc.sb("hnw", [128, 8, WIN], BF16)
            hnB = [[Buf() for _ in TILES] for _ in range(8)]
            attn_out(C, sc, oT_all, w_o, hT, hTB, hn, hnB, XB)
            ffn_block(C, sc, hT, hTB, hn, hnB, w_up1, w_dn1, 1)
            final_norm(C, hT, hTB, out_o)
            S.finish()
    return nc


_CACHE = {}


def _get(name, fn):
    if name not in _CACHE:
        _CACHE[name] = fn()
    return _CACHE[name]


def _rope_tables():
    inv = (1.0 / (10000.0 ** (np.arange(0, 64, 2, dtype=np.float32) / np.float32(64)))).astype(np.float32)
    ang = (np.arange(TSEQ, dtype=np.float32)[:, None] * inv[None, :]).astype(np.float32)
    ang = np.concatenate([ang, ang], -1)
    cos = np.cos(ang).astype(np.float32)
    sin = np.sin(ang).astype(np.float32)
    sgn = np.concatenate([-np.ones(32, np.float32), np.ones(32, np.float32)])
    sin = sin * sgn[None, :]
    cosT = np.concatenate([cos.T, cos.T], 0)
    sinT = np.concatenate([sin.T, sin.T], 0)
    return np.ascontiguousarray(cosT), np.ascontiguousarray(sinT)


def _masks():
    p = np.arange(128)[:, None]
    f = np.arange(512)[None, :]
    m = np.stack([((r * 128 + p) <= f) for r in range(4)], 1).astype(np.float32)
    return np.ascontiguousarray(m.reshape(128, 2048)).astype(ml_dtypes.bfloat16)


def kernel(**inp):
    inp = {k: np.asarray(v) for k, v in inp.items()}
    B = 2
    x = inp["x"]
    seq = [np.concatenate([inp["meta_tokens"], x[b]], 0) for b in range(B)]
    cores = [(b, c) for b in range(B) for c in range(4)]
    ids = list(range(NCORE))
    cosT, sinT = _rope_tables()
    wqkv = inp["attn_w_qkv"][0]
    perm = np.arange(2048).reshape(-1, 2, 32)[:, ::-1, :].reshape(-1)
    w_qk = np.ascontiguousarray(wqkv[:, 0:2048])
    shared = {
        "ident": np.eye(128, dtype=np.float32).astype(ml_dtypes.bfloat16),
        "masks": _masks(),
        "w_in": np.ascontiguousarray(inp["lru_w_in"][0]),
        "w_a": np.ascontiguousarray(inp["lru_w_a"][0].reshape(D, 128)),
        "w_i": np.ascontiguousarray(inp["lru_w_i"][0].reshape(D, 128)),
        "w_out": np.ascontiguousarray(inp["lru_w_out"][0]),
        "w_up0": np.ascontiguousarray(inp["ffn_w_up"][0]),
        "w_dn0": np.ascontiguousarray(inp["ffn_w_down"][0]),
        "cosF": cosT,
        "sinF": sinT,
        "w_o": np.ascontiguousarray(inp["attn_w_o"][0]),
        "w_up1": np.ascontiguousarray(inp["ffn_w_up"][1]),
        "w_dn1": np.ascontiguousarray(inp["ffn_w_down"][1]),
    }
    in_maps = []
    for (b, c) in cores:
        s0 = 2048 * c
        w = np.zeros((3 + WIN, D), np.float32)
        w[3:] = seq[b][s0:s0 + WIN]
        if c > 0:
            w[0:3] = seq[b][s0 - 3:s0]
        m = dict(shared)
        m["xw"] = np.ascontiguousarray(w.T)
        m["vecs"] = pack_vecs(inp, c)
        w_qkr = w_qk[:, perm]
        cols = np.concatenate([np.arange(wh * D + (2 * c + hh_) * 128, wh * D + (2 * c + hh_ + 1) * 128) for wh in range(2) for hh_ in range(2)])
        m["wA_hp"] = np.ascontiguousarray(w_qk[:, cols])
        m["wR_hp"] = np.ascontiguousarray(w_qkr[:, cols])
        m["wV_hp"] = np.ascontiguousarray(wqkv[:, 2048 + c * 256:2048 + (c + 1) * 256])
        in_maps.append(m)
    nc = _get("fused", build_fused)
    res = run_bass_kernel_spmd(nc, in_maps, core_ids=ids).results
    out = np.zeros((B, 8192, D), np.float32)
    for i, (b, c) in enumerate(cores):
        out[b, 2048 * c:2048 * (c + 1)] = res[i]["out_o"][:, HALO:].T
    return out
```
